# Optimizing a Trainium2 kernel written in Bass

```python
import jax, jax.numpy as jnp
from jax import lax
import numpy as np

D_MODEL = 2048
BATCH = 4
SEQ = 4096
DEPTH = 2

GRID_W = 64
CTX_LEN = 256
N_MOD = 9
D_FF = 5632
W_RG = 2048
RG_BLOCKS = 16
RG_BW = W_RG // RG_BLOCKS
RG_CONV = 4
RG_PAD_LO = 2
RG_C = 8.0
W_SC = 2048
SC_CONV = 3
SC_PAD_LO = 1
LN_EPS = 1e-5
ALPHA = (2 * DEPTH) ** 0.25
BETA = (8 * DEPTH) ** -0.25
PROJ_SIZES = (W_RG, W_RG, W_SC, W_SC, W_SC, D_MODEL, D_MODEL)
PROJ_W = sum(PROJ_SIZES)
PROJ_SPLITS = tuple(int(v) for v in np.cumsum(PROJ_SIZES)[:-1])

kernel_name = "hybrid_rglru_shortconv_macaron_deepnorm_dit"


def layer_norm(x, g, b):
    xf = x.astype(jnp.float32)
    mu = jnp.mean(xf, axis=-1, keepdims=True)
    var = jnp.mean(jnp.square(xf - mu), axis=-1, keepdims=True)
    return ((xf - mu) * lax.rsqrt(var + LN_EPS) * g.astype(jnp.float32) + b.astype(jnp.float32)).astype(x.dtype)


def modulate(x, shift, scale):
    return x * (1 + scale) + shift


def swiglu(h, w_in, w_out):
    g, u = jnp.split(h @ w_in, 2, axis=-1)
    return (jax.nn.silu(g) * u) @ w_out


def ffn_sublayer(x, shift, scale, gate, w_in, w_out, ln_g, ln_b):
    y = swiglu(modulate(x, shift, scale), w_in, w_out)
    return layer_norm(ALPHA * x + 0.5 * gate * y, ln_g, ln_b)


def dwconv(u, w, pad_lo):
    k_w = w.shape[0]
    t = u.shape[-2]
    pad = [(0, 0)] * (u.ndim - 2) + [(pad_lo, k_w - 1 - pad_lo), (0, 0)]
    up = jnp.pad(u, pad)
    return sum(up[..., k:k + t, :] * w[k] for k in range(k_w))


def conv_latent(u, w, pad_lo):
    b, s, ch = u.shape
    rows = s // GRID_W
    return dwconv(u.reshape(b, rows, GRID_W, ch), w, pad_lo).reshape(b, s, ch)


def _lin_combine(e1, e2):
    a1, v1 = e1
    a2, v2 = e2
    return a1 * a2, a2 * v1 + v2


def rglru_scan(u, w_gate, b_gate, lam, h0, reverse):
    b, t, w = u.shape
    ub = u.reshape(b, t, RG_BLOCKS, RG_BW)
    logits = jnp.einsum("bthi,ghij->gbthj", ub, w_gate).reshape(2, b, t, w) + b_gate[:, None, None, :]
    gates = jax.nn.sigmoid(logits)
    r, i = gates[0], gates[1]
    log_a = RG_C * r * jax.nn.log_sigmoid(lam)
    a = jnp.exp(log_a)
    v = jnp.sqrt(-jnp.expm1(2 * log_a)) * (i * u)
    entry = -1 if reverse else 0
    final = 0 if reverse else -1
    v = v.at[:, entry].add(a[:, entry] * h0)
    _, h = lax.associative_scan(_lin_combine, (a, v), reverse=reverse, axis=1)
    return h, h[:, final]


def rg_branch(u_x, conv, rg_conv_w, rg_conv_b, rg_gate_w, rg_gate_b, rg_lam, h0_f, h0_b):
    xc = conv(u_x, rg_conv_w, RG_PAD_LO) + rg_conv_b
    h_f, hT_f = rglru_scan(xc, rg_gate_w[0], rg_gate_b[0], rg_lam[0], h0_f, reverse=False)
    h_b, hT_b = rglru_scan(xc, rg_gate_w[1], rg_gate_b[1], rg_lam[1], h0_b, reverse=True)
    return h_f + h_b, hT_f, hT_b


def mixer_out(proj, h_rg, conv, sc_conv_w, w_rg_out, w_sc_out, b_merge, w_o):
    _, rg_gate, sc_b, sc_c, sc_x, g_rg, g_sc = jnp.split(proj, PROJ_SPLITS, axis=-1)
    y_rg = (h_rg * jax.nn.gelu(rg_gate)) @ w_rg_out
    y_sc = (sc_b * conv(sc_c * sc_x, sc_conv_w, SC_PAD_LO)) @ w_sc_out
    merged = jax.nn.sigmoid(g_rg + b_merge[0]) * y_rg + jax.nn.sigmoid(g_sc + b_merge[1]) * y_sc
    return merged @ w_o


def setup_inputs(seed: int = 0) -> dict:
    key = jax.random.key(seed)
    ks = jax.random.split(key, 24)

    def nrm(k, shape, scale):
        return jax.random.normal(k, shape, jnp.float32) * scale

    u = jax.random.uniform(ks[17], (DEPTH, 2, W_RG), jnp.float32, 0.81, 0.998)
    a0 = u ** (1.0 / RG_C)
    rg_lam = jnp.log(a0) - jnp.log1p(-a0)
    return {
        "x": nrm(ks[0], (BATCH, SEQ, D_MODEL), 1.0),
        "c": nrm(ks[1], (BATCH, D_MODEL), 1.0),
        "ctx": nrm(ks[2], (BATCH, CTX_LEN, D_MODEL), 1.0),
        "c_ctx": nrm(ks[3], (D_MODEL,), 1.0),
        "w_mod": nrm(ks[4], (DEPTH, D_MODEL, N_MOD * D_MODEL), 0.5 * D_MODEL ** -0.5),
        "b_mod": nrm(ks[5], (DEPTH, N_MOD * D_MODEL), 0.02),
        "ln_g": 1.0 + nrm(ks[6], (DEPTH, 3, D_MODEL), 0.02),
        "ln_b": nrm(ks[7], (DEPTH, 3, D_MODEL), 0.02),
        "ffn1_w_in": nrm(ks[8], (DEPTH, D_MODEL, 2 * D_FF), D_MODEL ** -0.5),
        "ffn1_w_out": nrm(ks[9], (DEPTH, D_FF, D_MODEL), BETA * D_FF ** -0.5),
        "ffn2_w_in": nrm(ks[10], (DEPTH, D_MODEL, 2 * D_FF), D_MODEL ** -0.5),
        "ffn2_w_out": nrm(ks[11], (DEPTH, D_FF, D_MODEL), BETA * D_FF ** -0.5),
        "w_in": nrm(ks[12], (DEPTH, D_MODEL, PROJ_W), D_MODEL ** -0.5),
        "rg_conv_w": nrm(ks[13], (DEPTH, RG_CONV, W_RG), RG_CONV ** -0.5),
        "rg_conv_b": nrm(ks[14], (DEPTH, W_RG), 0.02),
        "rg_gate_w": nrm(ks[15], (DEPTH, 2, 2, RG_BLOCKS, RG_BW, RG_BW), RG_BW ** -0.5),
        "rg_gate_b": nrm(ks[16], (DEPTH, 2, 2, W_RG), 0.02),
        "rg_lam": rg_lam,
        "sc_conv_w": nrm(ks[18], (DEPTH, SC_CONV, W_SC), SC_CONV ** -0.5),
        "w_rg_out": nrm(ks[19], (DEPTH, W_RG, D_MODEL), W_RG ** -0.5),
        "w_sc_out": nrm(ks[20], (DEPTH, W_SC, D_MODEL), W_SC ** -0.5),
        "b_merge": nrm(ks[21], (DEPTH, 2, D_MODEL), 0.02),
        "w_o": nrm(ks[22], (DEPTH, D_MODEL, D_MODEL), BETA * D_MODEL ** -0.5),
    }


def reference(x, c, ctx, c_ctx, w_mod, b_mod, ln_g, ln_b, ffn1_w_in, ffn1_w_out, ffn2_w_in, ffn2_w_out,
              w_in, rg_conv_w, rg_conv_b, rg_gate_w, rg_gate_b, rg_lam, sc_conv_w, w_rg_out, w_sc_out,
              b_merge, w_o):
    xc = ctx
    batch = x.shape[0]
    zeros_state = jnp.zeros((batch, W_RG), x.dtype)
    for l in range(DEPTH):
        last = l == DEPTH - 1
        mod_lat = jnp.split((jax.nn.silu(c) @ w_mod[l] + b_mod[l])[:, None, :], N_MOD, axis=-1)
        mod_ctx = jnp.split(jax.nn.silu(c_ctx) @ w_mod[l] + b_mod[l], N_MOD, axis=-1)

        x = ffn_sublayer(x, mod_lat[0], mod_lat[1], mod_lat[2], ffn1_w_in[l], ffn1_w_out[l], ln_g[l, 0], ln_b[l, 0])
        xc = ffn_sublayer(xc, mod_ctx[0], mod_ctx[1], mod_ctx[2], ffn1_w_in[l], ffn1_w_out[l], ln_g[l, 0], ln_b[l, 0])

        hc = modulate(xc, mod_ctx[3], mod_ctx[4])
        if last:
            pc_rg = hc @ w_in[l][:, :W_RG]
        else:
            pc = hc @ w_in[l]
            pc_rg = pc[..., :W_RG]
        h_rg_c, s_f, s_b = rg_branch(pc_rg, dwconv, rg_conv_w[l], rg_conv_b[l], rg_gate_w[l], rg_gate_b[l],
                                     rg_lam[l], zeros_state, zeros_state)
        if not last:
            yc = mixer_out(pc, h_rg_c, dwconv, sc_conv_w[l], w_rg_out[l], w_sc_out[l], b_merge[l], w_o[l])
            xc = layer_norm(ALPHA * xc + mod_ctx[5] * yc, ln_g[l, 1], ln_b[l, 1])

        hl = modulate(x, mod_lat[3], mod_lat[4])
        pl = hl @ w_in[l]
        h_rg_l, _, _ = rg_branch(pl[..., :W_RG], conv_latent, rg_conv_w[l], rg_conv_b[l], rg_gate_w[l],
                                 rg_gate_b[l], rg_lam[l], s_f, s_b)
        yl = mixer_out(pl, h_rg_l, conv_latent, sc_conv_w[l], w_rg_out[l], w_sc_out[l], b_merge[l], w_o[l])
        x = layer_norm(ALPHA * x + mod_lat[5] * yl, ln_g[l, 1], ln_b[l, 1])

        x = ffn_sublayer(x, mod_lat[6], mod_lat[7], mod_lat[8], ffn2_w_in[l], ffn2_w_out[l], ln_g[l, 2], ln_b[l, 2])
        if not last:
            xc = ffn_sublayer(xc, mod_ctx[6], mod_ctx[7], mod_ctx[8], ffn2_w_in[l], ffn2_w_out[l],
                              ln_g[l, 2], ln_b[l, 2])
    return x
```

```python
import contextlib
import numpy as np
import concourse.bass as bass
import concourse.mybir as mybir
from concourse.bass_utils import run_bass_kernel_spmd

F32 = mybir.dt.float32
BF16 = mybir.dt.bfloat16
AF = mybir.ActivationFunctionType
ALU = mybir.AluOpType

D = 2048
NCH = 16
DFF = 5632
NFF = 44
TCTX = 256
TLAT = 2048
T = TCTX + TLAT
L = 2
GRID_W = 64
TW = 512
TILES = [(0, 512), (512, 512), (1024, 512), (1536, 512), (2048, 256)]
ALPHA = (2 * L) ** 0.25
EPSP = 1e-5 / ALPHA ** 2
NSLOT = 4
SLOT_ELEMS = 5632
EPOCH = 30000

_off = {}
_cur = 0
for _name, _n in [("bmod", L * 144), ("lng", L * 3 * 16), ("lnb", L * 3 * 16), ("bmerge", L * 2 * 16),
                  ("rgcw", L * 16 * 5), ("rgcb", L * 16), ("sccw", L * 16 * 3), ("gb", L * 2 * 2 * 16),
                  ("lam", L * 2 * 16), ("sel", 2), ("cc", 32)]:
    _off[_name] = _cur
    _cur += _n
NCONST = _cur


class Op:
    __slots__ = ("eng", "fn", "deps", "dma", "dma_val", "needed", "ms")

    def __init__(self, eng, fn, deps, dma):
        self.eng, self.fn, self.deps, self.dma = eng, fn, deps, dma
        self.dma_val = 0
        self.needed = False
        self.ms = 0


class Sched:
    ENG = ("pe", "act", "dve", "pool", "sp")

    def __init__(self):
        self.ops = []
        self.lw = {}
        self.rd = {}
        self.last_on = {e: None for e in self.ENG}
        self.fence_deps = {e: [] for e in self.ENG}
        self.dma_cnt = {}

    def add(self, eng, fn, r=(), w=(), dma=None):
        idx = len(self.ops)
        deps = set(self.fence_deps[eng])
        self.fence_deps[eng] = []
        for k in r:
            x = self.lw.get(k)
            if x is not None:
                deps.add(x)
        for k in w:
            x = self.lw.get(k)
            if x is not None:
                deps.add(x)
            for x in self.rd.get(k, {}).values():
                deps.add(x)
        for k in r:
            self.rd.setdefault(k, {})[eng if dma is None else ("dma", idx)] = idx
        for k in w:
            self.lw[k] = idx
            self.rd[k] = {}
        deps.discard(idx)
        op = Op(eng, fn, sorted(deps), dma)
        if dma is not None:
            self.dma_cnt[dma] = self.dma_cnt.get(dma, 0) + 16
            op.dma_val = self.dma_cnt[dma]
        for d in deps:
            self.ops[d].needed = True
        self.ops.append(op)
        if dma is None:
            self.last_on[eng] = idx
        return idx

    def fence(self):
        f = [v for v in self.last_on.values() if v is not None]
        for e in self.ENG:
            self.fence_deps[e] = list(f)

    def emit(self, nc, stack):
        cnt = {e: 0 for e in self.ENG}
        for op in self.ops:
            if op.dma is None and op.needed:
                cnt[op.eng] += 1
                op.ms = cnt[op.eng]
        esem = {e: [stack.enter_context(nc.semaphore(f"s_{e}_{i}")) for i in range(cnt[e] // EPOCH + 1)]
                for e in self.ENG}
        dsem = {}
        for i, ch in enumerate(sorted(self.dma_cnt, key=str)):
            assert self.dma_cnt[ch] < 60000, (ch, self.dma_cnt[ch])
            dsem[ch] = stack.enter_context(nc.semaphore(f"d_{i}"))
        ops = self.ops
        by_eng = {e: [op for op in ops if op.eng == e] for e in self.ENG}

        def run(e, eo):
            waited = {}
            for op in by_eng[e]:
                for d in op.deps:
                    p = ops[d]
                    if p.dma is not None:
                        sid, sem, val = ("d", p.dma), dsem[p.dma], p.dma_val
                    else:
                        if p.eng == "pe" and e == "pe":
                            continue
                        ep = (p.ms - 1) // EPOCH
                        sid, sem, val = (p.eng, ep), esem[p.eng][ep], (p.ms - 1) % EPOCH + 1
                    if waited.get(sid, 0) >= val:
                        continue
                    eo.wait_ge(sem, val)
                    waited[sid] = val
                if op.fn is None:
                    continue
                m_, a_, k_ = op.fn
                ins = getattr(eo, m_)(*a_, **k_)
                if op.dma is not None:
                    ins.then_inc(dsem[op.dma], 16)
                elif op.needed:
                    ins.then_inc(esem[e][(op.ms - 1) // EPOCH], 1)

        block = stack.enter_context(nc.Block())

        @block.tensor
        def _(eo):
            run("pe", eo)

        @block.scalar
        def _(eo):
            run("act", eo)

        @block.vector
        def _(eo):
            run("dve", eo)

        @block.gpsimd
        def _(eo):
            run("pool", eo)

        @block.sync
        def _(eo):
            run("sp", eo)


def I(method, *args, **kw):
    return (method, args, kw)


def segs(t0, w):
    out = []
    if t0 < TCTX:
        out.append((0, min(w, TCTX - t0), 1))
    if t0 + w > TCTX:
        out.append((max(0, TCTX - t0), w, 0))
    return out


class Builder:
    def __init__(self, n_cores, stop_after=None, dbg=None):
        self.n_cores = n_cores
        self.stop_after = stop_after
        self.dbg = dbg

    def build(self):
        nc = bass.Bass("TRN2", target_bir_lowering=False)
        self.nc = nc
        S = Sched()
        self.S = S
        with contextlib.ExitStack() as st:
            self.alloc(nc, st)
            self.program()
            S.emit(nc, st)
        return nc

    def alloc(self, nc, st):
        def dram_in(name, shape):
            return nc.dram_tensor(name, shape, F32, kind="ExternalInput").ap()

        self.xT = dram_in("xT", [D, T])
        self.consts = dram_in("consts", [128, NCONST])
        self.gw = dram_in("gw", [128, L * 2 * 16 * 2, 128])
        self.wA = dram_in("wA", [L * 240, 128, 4096])
        self.wB = dram_in("wB", [L * 32, 128, 5632])
        self.outT = nc.dram_tensor("outT", [D, TLAT], F32, kind="ExternalOutput").ap()
        if self.dbg:
            self.dbg_out = nc.dram_tensor("dbg", [D, T], F32, kind="ExternalOutput").ap()
        self.xa = nc.dram_tensor("xa", [D, T], F32).ap()
        self.xb = nc.dram_tensor("xb", [D, T], F32).ap()
        self.rgx = nc.dram_tensor("rgx", [D, T], F32).ap()
        self.hrg = nc.dram_tensor("hrg", [D, T], F32).ap()
        self.st_in = nc.dram_tensor("st_in", [128, 16], F32)
        self.st_out = nc.dram_tensor("st_out", [256, 16], F32)

        def sb(name, shape, dt=F32):
            return st.enter_context(nc.sbuf_tensor(name, shape, dt))

        self.cst = sb("cst", [128, NCONST])
        self.modT = sb("modT", [128, L * 288])
        self.sc1p = sb("sc1p", [128, L * 96])
        self.gsc = sb("gsc", [128, L * 96])
        self.c8 = sb("c8", [128, L * 32])
        self.c16 = sb("c16", [128, L * 32])
        self.scb = sb("scb", [128, 32], BF16)
        self.ones = sb("ones", [128, 128])
        self.states = sb("states", [128, 16])
        self.sgt = sb("sgt", [128, 32])
        self.h0 = sb("h0", [128, 16])
        self.gwt = sb("gwt", [128, 2, 128])
        self.wslot = [sb(f"wslot{i}", [128, SLOT_ELEMS], BF16) for i in range(NSLOT)]
        self.xs = sb("xs", [128, NCH * TW])
        self.xin = sb("xin", [128, NCH * TW], BF16)
        self.hid = sb("hid", [128, 48 * TW], BF16)
        self.tmp = [sb(f"tmp{i}", [128, TW]) for i in range(8)]
        self.hrt = [sb(f"hrt{i}", [128, TW]) for i in range(2)]
        self.stg = [sb(f"stg{i}", [128, TW]) for i in range(2)]
        self.mean = sb("mean", [128, TW])
        self.rstd = sb("rstd", [128, TW])
        self.ps = [st.enter_context(nc.psum_tensor(f"ps{i}", [128, 512], F32)) for i in range(8)]
        hid32 = self.hid.bitcast(F32)
        self.sbuf_scan = [hid32[:, i * T:(i + 1) * T] for i in range(5)] + \
                         [self.xs[:, i * T:(i + 1) * T] for i in range(3)]
        self.ws_i = 0
        self.pp_i = 0
        self.tp_i = 0
        self.stg_i = 0

    def C(self, name, idx, n=1):
        o = _off[name] + idx
        return self.cst[:, o:o + n]

    def wload(self, dram_panel, nelem):
        i = self.ws_i % NSLOT
        self.ws_i += 1
        slot = self.wslot[i]
        self.S.add("pool", I("dma_start", out=slot[:, 0:nelem], in_=dram_panel, max_dma_last_dim=8192),
                   w=[("w", i)], dma=("w", i))
        return slot, ("w", i)

    def bank(self):
        i = self.pp_i % 6
        self.pp_i += 1
        return self.ps[i], ("ps", i)

    def tmpt(self):
        i = self.tp_i % 8
        self.tp_i += 1
        return self.tmp[i], ("tmp", i)

    def stage(self):
        i = self.stg_i % 2
        self.stg_i += 1
        return self.stg[i], ("stg", i)

    def mm(self, out, lhsT, rhs, start, stop, r, w):
        self.S.add("pe", I("matmul", out, lhsT, rhs, start=start, stop=stop), r=r, w=w)

    def mod(self, l, j, c, s):
        o = (l * 144 + j * 16 + c) * 2 + s
        return self.modT[:, o:o + 1]

    def dk(self, name, c, i):
        return ("dram", name, c, i)

    def program(self):
        S = self.S
        nc = self.nc
        S.add("sp", I("dma_start", out=self.cst[:], in_=self.consts), w=["cst"], dma="cst")
        S.add("dve", I("memset", self.ones[:], 1.0 / D), w=["ones"])
        self.setup_consts()
        for l in range(L):
            self.mods(l)
        if self.stop_after == "mods":
            return self.finish_dbg_small()
        cur = (self.xT, "xT")
        A = (self.xa, "xa")
        Bb = (self.xb, "xb")
        for l in range(L):
            last = l == L - 1
            self.ffn(l, 0, cur[0], cur[1], A[0], A[1], tiles=TILES)
            if self.stop_after == f"ffn1_{l}":
                return self.finish_dbg(*A)
            self.rgxproj(l, A[0], A[1])
            if self.stop_after == f"rgx_{l}":
                return self.finish_dbg(self.rgx, "rgx")
            S.fence()
            self.scan(l)
            S.fence()
            if self.stop_after == f"scan_{l}":
                return self.finish_dbg(self.hrg, "hrg")
            tiles = TILES if not last else [(256, 256)] + TILES[1:]
            self.mixout(l, A[0], A[1], Bb[0], Bb[1], tiles)
            if self.stop_after == f"mix_{l}":
                return self.finish_dbg(*Bb)
            if last:
                self.ffn(l, 2, Bb[0], Bb[1], self.outT, "out", tiles=tiles, final=True)
            else:
                self.ffn(l, 2, Bb[0], Bb[1], A[0], A[1], tiles=tiles)
                if self.stop_after == f"ffn2_{l}":
                    return self.finish_dbg(*A)
                cur = A
                A, Bb = Bb, A
        S.add("sp", None, r=[self.dk("out", c, i) for c in range(NCH) for i in range(len(TILES))])

    def finish_dbg(self, src_ap, name):
        S = self.S
        S.fence()
        for c in range(NCH):
            buf = self.sbuf_scan[c % 2]
            key = ("scanbuf", c % 2)
            S.add("sp", I("dma_start", out=buf, in_=src_ap[c * 128:(c + 1) * 128, :]),
                  r=[self.dk(name, c, i) for i in range(len(TILES))], w=[key], dma=("dbgl", c % 2))
            S.add("sp", I("dma_start", out=self.dbg_out[c * 128:(c + 1) * 128, :], in_=buf),
                  r=[key], w=[("dbgout", c)], dma=("dbgs", c % 2))
        S.add("sp", None, r=[("dbgout", c) for c in range(NCH)])

    def finish_dbg_small(self):
        S = self.S
        S.add("sp", I("dma_start", out=self.dbg_out[0:128, 0:L * 288], in_=self.modT[:]),
              r=["modT0", "modT1"], w=["dbgo"], dma="dbgs")
        S.add("sp", I("dma_start", out=self.dbg_out[128:256, 0:L * 32], in_=self.c8[:]),
              r=["c8"], w=["dbgo2"], dma="dbgs2")
        S.add("sp", None, r=["dbgo", "dbgo2"])

    def setup_consts(self):
        S = self.S
        lam = self.C("lam", 0, L * 32)
        t0 = self.tmp[0]
        S.add("act", I("activation", out=t0[:, 0:L * 32], in_=lam, func=AF.Exp, scale=-1.0),
              r=["cst"], w=[("tmp", 0)])
        S.add("act", I("activation", out=t0[:, 0:L * 32], in_=t0[:, 0:L * 32], func=AF.Ln, bias=1.0),
              r=[("tmp", 0)], w=[("tmp", 0)])
        S.add("dve", I("tensor_scalar", out=self.c8[:], in0=t0[:, 0:L * 32], scalar1=-8.0, scalar2=None,
                                               op0=ALU.mult), r=[("tmp", 0)], w=["c8"])
        S.add("dve", I("tensor_scalar", out=self.c16[:], in0=t0[:, 0:L * 32], scalar1=-16.0, scalar2=None,
                                               op0=ALU.mult), r=[("tmp", 0)], w=["c16"])
        cc = self.C("cc", 0, 32)
        S.add("act", I("activation", out=self.scb[:], in_=cc, func=AF.Silu), r=["cst"], w=["scb"])

    def mods(self, l):
        S = self.S
        psb = self.ps[6 + l]
        for pn in range(72):
            slot, wk = self.wload(self.wA[l * 240 + pn], 4096)
            sv = slot[:, 0:4096].rearrange("p (k c) -> p k c", k=16)
            for half in range(2):
                m = pn * 2 + half
                for kc in range(16):
                    self.mm(psb[:, 2 * m:2 * m + 2], sv[:, kc, half * 128:(half + 1) * 128],
                            self.scb[:, 2 * kc:2 * kc + 2], kc == 0, kc == 15,
                            r=[wk, "scb"], w=[("ps", 6 + l)])
        mt = self.modT[:, l * 288:(l + 1) * 288]
        bm = self.C("bmod", l * 144, 144)
        for s in range(2):
            S.add("dve", I("tensor_tensor", out=mt[:, s::2], in0=psb[:, s:288:2], in1=bm, op=ALU.add),
                  r=[("ps", 6 + l), "cst"], w=[f"modT{l}"])
        coef = [0.5 / ALPHA, 1.0 / ALPHA, 0.5 / ALPHA]
        for i in range(3):
            src1 = self.modT[:, (l * 144 + (3 * i + 1) * 16) * 2:(l * 144 + (3 * i + 2) * 16) * 2]
            dst1 = self.sc1p[:, l * 96 + i * 32:l * 96 + (i + 1) * 32]
            S.add("dve", I("tensor_scalar", out=dst1, in0=src1, scalar1=1.0, scalar2=None,
                                                                  op0=ALU.add), r=[f"modT{l}"], w=[f"sc1p{l}"])
            src2 = self.modT[:, (l * 144 + (3 * i + 2) * 16) * 2:(l * 144 + (3 * i + 3) * 16) * 2]
            dst2 = self.gsc[:, l * 96 + i * 32:l * 96 + (i + 1) * 32]
            S.add("dve", I("tensor_scalar", out=dst2, in0=src2, scalar1=coef[i],
                                                                              scalar2=None, op0=ALU.mult),
                  r=[f"modT{l}"], w=[f"gsc{l}"])

    def SC1P(self, l, i, c, s):
        o = l * 96 + i * 32 + c * 2 + s
        return self.sc1p[:, o:o + 1]

    def GSC(self, l, i, c, s):
        o = l * 96 + i * 32 + c * 2 + s
        return self.gsc[:, o:o + 1]

    def load_tile(self, src, srcn, ti, t0, w):
        S = self.S
        i_t = self.tile_index(t0)
        srcv = src.rearrange("(c p) t -> p c t", p=128)[:, :, t0:t0 + w]
        dstv = self.xs.rearrange("p (c t) -> p c t", c=NCH)[:, :, 0:w]
        S.add("sp", I("dma_start", out=dstv, in_=srcv),
              r=[self.dk(srcn, c, i_t) for c in range(NCH)], w=[("xs", c) for c in range(NCH)], dma="xs")

    def tile_index(self, t0):
        return t0 // TW

    def modulate(self, l, i, t0, w):
        S = self.S
        for c in range(NCH):
            for (a, b, s) in segs(t0, w):
                S.add("act", I("activation",
                    out=self.xin[:, c * TW + a:c * TW + b], in_=self.xs[:, c * TW + a:c * TW + b],
                    func=AF.Identity, scale=self.SC1P(l, i, c, s), bias=self.mod(l, 3 * i, c, s)),
                    r=[("xs", c), f"modT{l}", f"sc1p{l}"], w=[("xin", c)])

    def resid_stats(self, l, i, oc, yb, ybk, t0, w, pending):
        S = self.S
        xsl = self.xs[:, oc * TW:oc * TW + w]
        for (a, b, s) in segs(t0, w):
            S.add("dve", I("scalar_tensor_tensor",
                out=self.xs[:, oc * TW + a:oc * TW + b], in0=yb[:, a:b], scalar=self.GSC(l, i, oc, s),
                in1=self.xs[:, oc * TW + a:oc * TW + b], op0=ALU.mult, op1=ALU.add),
                r=[ybk, ("xs", oc), f"gsc{l}"], w=[("xs", oc)])
        sq, sqk = self.tmpt()
        S.add("act", I("activation", out=sq[:, 0:w], in_=xsl, func=AF.Square), r=[("xs", oc)], w=[sqk])

        def stats(first, lastf):
            self.mm(self.ps[6][:, 0:w], self.ones[:], xsl, first, lastf, r=[("xs", oc), "ones"], w=[("ps", 6)])
            self.mm(self.ps[7][:, 0:w], self.ones[:], sq[:, 0:w], first, lastf, r=[sqk, "ones"], w=[("ps", 7)])
        pending.append(stats)

    def flush_stats(self, pending, n_done, total, keep=0):
        while len(pending) > keep:
            f = pending.pop(0)
            f(n_done[0] == 0, n_done[0] == total - 1)
            n_done[0] += 1

    def finalize(self, l, i, t0, w, dst, dstn, final):
        S = self.S
        i_t = self.tile_index(t0)
        mean, rstd = self.mean, self.rstd
        msq, msqk = self.tmpt()
        S.add("act", I("activation", out=mean[:, 0:w], in_=self.ps[6][:, 0:w], func=AF.Copy),
              r=[("ps", 6)], w=["mean"])
        S.add("act", I("activation", out=msq[:, 0:w], in_=self.ps[6][:, 0:w], func=AF.Square),
              r=[("ps", 6)], w=[msqk])
        S.add("dve", I("tensor_tensor", out=msq[:, 0:w], in0=self.ps[7][:, 0:w], in1=msq[:, 0:w],
                                               op=ALU.subtract), r=[("ps", 7), msqk], w=[msqk])
        S.add("dve", I("tensor_scalar", out=msq[:, 0:w], in0=msq[:, 0:w], scalar1=0.0, scalar2=EPSP,
                                               op0=ALU.max, op1=ALU.add), r=[msqk], w=[msqk])
        S.add("act", I("activation", out=msq[:, 0:w], in_=msq[:, 0:w], func=AF.Sqrt), r=[msqk], w=[msqk])
        S.add("dve", I("reciprocal", out=rstd[:, 0:w], in_=msq[:, 0:w]), r=[msqk], w=["rstd"])
        for c in range(NCH):
            t1, t1k = self.tmpt()
            xsl = self.xs[:, c * TW:c * TW + w]
            S.add("dve", I("tensor_tensor", out=t1[:, 0:w], in0=xsl, in1=mean[:, 0:w],
                                                                 op=ALU.subtract),
                  r=[("xs", c), "mean"], w=[t1k])
            S.add("dve", I("tensor_tensor", out=t1[:, 0:w], in0=t1[:, 0:w], in1=rstd[:, 0:w],
                                                          op=ALU.mult), r=[t1k, "rstd"], w=[t1k])
            sg, sgk = self.stage()
            g = self.C("lng", (l * 3 + i) * 16 + c)
            bb = self.C("lnb", (l * 3 + i) * 16 + c)
            S.add("act", I("activation", out=sg[:, 0:w], in_=t1[:, 0:w],
                                                                        func=AF.Identity, scale=g, bias=bb),
                  r=[t1k, "cst"], w=[sgk])
            if final:
                dsl = dst[c * 128:(c + 1) * 128, t0 - TCTX:t0 - TCTX + w]
            else:
                dsl = dst[c * 128:(c + 1) * 128, t0:t0 + w]
            S.add("sp", I("dma_start", out=dsl, in_=sg[:, 0:w]),
                  r=[sgk], w=[self.dk(dstn, c, i_t)], dma=sgk)

    def ffn(self, l, i, src, srcn, dst, dstn, tiles, final=False):
        S = self.S
        which = 0 if i == 0 else 1
        pa = l * 240 + 72 + which * 44
        pb = l * 32 + which * 16
        for ti, (t0, w) in enumerate(tiles):
            self.load_tile(src, srcn, ti, t0, w)
            self.modulate(l, i, t0, w)
            for j in range(NFF):
                slot, wk = self.wload(self.wA[pa + j], 4096)
                sv = slot[:, 0:4096].rearrange("p (k c) -> p k c", k=16)
                gb, gbk = self.bank()
                ub, ubk = self.bank()
                for kc in range(16):
                    self.mm(gb[:, 0:w], sv[:, kc, 0:128], self.xin[:, kc * TW:kc * TW + w], kc == 0, kc == 15,
                            r=[wk, ("xin", kc)], w=[gbk])
                for kc in range(16):
                    self.mm(ub[:, 0:w], sv[:, kc, 128:256], self.xin[:, kc * TW:kc * TW + w], kc == 0, kc == 15,
                            r=[wk, ("xin", kc)], w=[ubk])
                sg, sgk = self.tmpt()
                S.add("act", I("activation", out=sg[:, 0:w], in_=gb[:, 0:w], func=AF.Silu),
                      r=[gbk], w=[sgk])
                S.add("dve", I("tensor_tensor",
                    out=self.hid[:, j * TW:j * TW + w], in0=sg[:, 0:w], in1=ub[:, 0:w], op=ALU.mult),
                    r=[sgk, ubk], w=[("hid", j)])
            pending = []
            n_done = [0]
            for oc in range(NCH):
                slot, wk = self.wload(self.wB[pb + oc], 5632)
                sv = slot[:, 0:5632].rearrange("p (k c) -> p k c", k=NFF)
                yb, ybk = self.bank()
                for kc in range(NFF):
                    self.mm(yb[:, 0:w], sv[:, kc, :], self.hid[:, kc * TW:kc * TW + w], kc == 0, kc == NFF - 1,
                            r=[wk, ("hid", kc)], w=[ybk])
                self.flush_stats(pending, n_done, NCH, keep=0)
                self.resid_stats(l, i, oc, yb, ybk, t0, w, pending)
            self.flush_stats(pending, n_done, NCH)
            self.finalize(l, i, t0, w, dst, dstn, final)

    def rgxproj(self, l, src, srcn):
        S = self.S
        pa = l * 240 + 160
        for ti, (t0, w) in enumerate(TILES):
            i_t = self.tile_index(t0)
            self.load_tile(src, srcn, ti, t0, w)
            self.modulate(l, 1, t0, w)
            for pn in range(8):
                slot, wk = self.wload(self.wA[pa + pn], 4096)
                sv = slot[:, 0:4096].rearrange("p (k c) -> p k c", k=16)
                for half in range(2):
                    oc = pn * 2 + half
                    yb, ybk = self.bank()
                    for kc in range(16):
                        self.mm(yb[:, 0:w], sv[:, kc, half * 128:(half + 1) * 128],
                                self.xin[:, kc * TW:kc * TW + w], kc == 0, kc == 15, r=[wk, ("xin", kc)], w=[ybk])
                    sg, sgk = self.stage()
                    S.add("act", I("activation", out=sg[:, 0:w], in_=yb[:, 0:w], func=AF.Copy),
                          r=[ybk], w=[sgk])
                    dsl = self.rgx[oc * 128:(oc + 1) * 128, t0:t0 + w]
                    S.add("sp", I("dma_start", out=dsl, in_=sg[:, 0:w]),
                          r=[sgk], w=[self.dk("rgx", oc, i_t)], dma=sgk)

    def conv_views(self, ap, off, rowlen, ncols, c0):
        v = ap[:, c0:c0 + ncols]
        if rowlen != ncols:
            v = v.rearrange("p (r g) -> p r g", g=rowlen)
            lo, hi = max(0, -off), rowlen - max(0, off)
            return (lambda x: x[:, c0:c0 + ncols].rearrange("p (r g) -> p r g", g=rowlen)[:, :, lo:hi],
                    lambda x: x[:, c0:c0 + ncols].rearrange("p (r g) -> p r g", g=rowlen)[:, :, lo + off:hi + off])
        lo, hi = max(0, -off), rowlen - max(0, off)
        return (lambda x: x[:, c0 + lo:c0 + hi], lambda x: x[:, c0 + lo + off:c0 + hi + off])

    def gates(self, l, which, k, XC, XCk, Rg, Rgk, Ig, Igk, A, Ak, M, Mk):
        S = self.S
        gi = ((l * 2 + which) * 16 + k) * 2
        S.add("sp", I("dma_start", out=self.gwt[:], in_=self.gw[:, gi:gi + 2, :]), w=["gwt"], dma="gwt")
        for (t0, w) in TILES:
            rb, rbk = self.bank()
            ib, ibk = self.bank()
            self.mm(rb[:, 0:w], self.gwt[:, 0, :], XC[:, t0:t0 + w], True, True, r=["gwt", XCk], w=[rbk])
            self.mm(ib[:, 0:w], self.gwt[:, 1, :], XC[:, t0:t0 + w], True, True, r=["gwt", XCk], w=[ibk])
            br = self.C("gb", ((l * 2 + which) * 2 + 0) * 16 + k)
            bi = self.C("gb", ((l * 2 + which) * 2 + 1) * 16 + k)
            S.add("act", I("activation", out=Rg[:, t0:t0 + w], in_=rb[:, 0:w],
                                                                         func=AF.Sigmoid, bias=br),
                  r=[rbk, "cst"], w=[Rgk])
            S.add("act", I("activation", out=Ig[:, t0:t0 + w], in_=ib[:, 0:w],
                                                                         func=AF.Sigmoid, bias=bi),
                  r=[ibk, "cst"], w=[Igk])
        co = (l * 2 + which) * 16 + k
        S.add("act", I("activation", out=A, in_=Rg, func=AF.Exp, scale=self.c8[:, co:co + 1]),
              r=[Rgk, "c8"], w=[Ak])
        S.add("act", I("activation", out=M, in_=Rg, func=AF.Exp, scale=self.c16[:, co:co + 1]),
              r=[Rgk, "c16"], w=[Mk])
        S.add("dve", I("tensor_scalar", out=M, in0=M, scalar1=1.0, scalar2=None, op0=ALU.min),
              r=[Mk], w=[Mk])
        S.add("act", I("activation", out=M, in_=M, func=AF.Sqrt, scale=-1.0, bias=1.0), r=[Mk], w=[Mk])
        S.add("dve", I("tensor_tensor", out=Ig, in0=Ig, in1=XC, op=ALU.mult), r=[Igk, XCk], w=[Igk])
        S.add("dve", I("tensor_tensor", out=Ig, in0=Ig, in1=M, op=ALU.mult), r=[Igk, Mk], w=[Igk])

    def scan(self, l):
        S = self.S
        B = self.sbuf_scan
        Bk = [("scanbuf", i) for i in range(8)]
        nt = len(TILES)
        for k in range(16):
            U, Uk = B[0], Bk[0]
            XC, XCk = B[1], Bk[1]
            rows = slice(k * 128, (k + 1) * 128)
            S.add("sp", I("dma_start", out=U, in_=self.rgx[rows, :]),
                  r=[self.dk("rgx", k, i) for i in range(nt)], w=[Uk], dma=Uk)
            wb = (l * 16 + k) * 5
            bias = self.C("rgcb", l * 16 + k)
            S.add("dve", I("tensor_scalar", out=XC, in0=U, scalar1=self.C("rgcw", wb + 2),
                                                                   scalar2=bias, op0=ALU.mult, op1=ALU.add),
                  r=[Uk, "cst"], w=[XCk])
            for off in (-2, -1, 1, 2):
                for (c0, ncols, rowlen) in ((0, TCTX, TCTX), (TCTX, TLAT, GRID_W)):
                    ov, iv = self.conv_views(XC, off, rowlen, ncols, c0)
                    S.add("dve", I("scalar_tensor_tensor",
                        out=ov(XC), in0=iv(U), scalar=self.C("rgcw", wb + 2 + off), in1=ov(XC),
                        op0=ALU.mult, op1=ALU.add), r=[Uk, XCk, "cst"], w=[XCk])
            S.add("sp", I("dma_start", out=self.rgx[rows, :], in_=XC),
                  r=[XCk], w=[self.dk("rgx", k, i) for i in range(nt)], dma=("st", XCk))
            Rg, Ig, A, M, H = B[2], B[3], B[4], B[5], B[6]
            self.gates(l, 0, k, XC, XCk, Rg, Bk[2], Ig, Bk[3], A, Bk[4], M, Bk[5])
            S.add("dve", I("tensor_tensor_scan", out=H, data0=A, data1=Ig, initial=0.0, op0=ALU.mult,
                                                       op1=ALU.add), r=[Bk[4], Bk[3]], w=[Bk[6]])
            S.add("dve", I("tensor_copy", out=self.states[:, k:k + 1], in_=H[:, T - 1:T]),
                  r=[Bk[6]], w=["states"])
            S.add("sp", I("dma_start", out=self.hrg[rows, :], in_=H),
                  r=[Bk[6]], w=[self.dk("hrg", k, i) for i in range(nt)], dma=("st", Bk[6]))
        S.add("sp", I("dma_start", out=self.st_in[:, :], in_=self.states[:]), r=["states"], w=["st_in"],
              dma="st_in")
        groups = [[2 * g, 2 * g + 1] for g in range(self.n_cores // 2)]
        S.add("pool", I("collective_compute", "AllGather", ALU.bypass, replica_groups=groups,
                                                     ins=[self.st_in.ap().opt()], outs=[self.st_out.ap().opt()]),
              r=["st_in"], w=["st_out"])
        S.add("sp", I("dma_start", out=self.sgt[:].rearrange("p (r k) -> p r k", r=2),
                                          in_=self.st_out.ap().rearrange("(r p) k -> p r k", p=128)),
              r=["st_out"], w=["sgt"], dma="sgt")
        S.add("dve", I("tensor_scalar", out=self.h0[:], in0=self.sgt[:, 0:16], scalar1=self.C("sel", 0),
                                               scalar2=None, op0=ALU.mult), r=["sgt", "cst"], w=["h0"])
        S.add("dve", I("scalar_tensor_tensor", out=self.h0[:], in0=self.sgt[:, 16:32],
                                                      scalar=self.C("sel", 1), in1=self.h0[:], op0=ALU.mult,
                                                      op1=ALU.add), r=["sgt", "cst", "h0"], w=["h0"])
        for k in range(16):
            XC, XCk = B[1], Bk[1]
            rows = slice(k * 128, (k + 1) * 128)
            S.add("sp", I("dma_start", out=XC, in_=self.rgx[rows, :]),
                  r=[self.dk("rgx", k, i) for i in range(nt)], w=[XCk], dma=XCk)
            HO, HOk = B[7], Bk[7]
            S.add("sp", I("dma_start", out=HO, in_=self.hrg[rows, :]),
                  r=[self.dk("hrg", k, i) for i in range(nt)], w=[HOk], dma=HOk)
            Rg, Ig, A, M, H = B[2], B[3], B[4], B[5], B[6]
            self.gates(l, 1, k, XC, XCk, Rg, Bk[2], Ig, Bk[3], A, Bk[4], M, Bk[5])
            S.add("dve", I("tensor_tensor_scan",
                out=H[:, TCTX:T][:, ::-1], data0=A[:, TCTX:T][:, ::-1], data1=Ig[:, TCTX:T][:, ::-1],
                initial=self.h0[:, k:k + 1], op0=ALU.mult, op1=ALU.add), r=[Bk[4], Bk[3], "h0"], w=[Bk[6]])
            S.add("dve", I("tensor_tensor_scan",
                out=H[:, 0:TCTX][:, ::-1], data0=A[:, 0:TCTX][:, ::-1], data1=Ig[:, 0:TCTX][:, ::-1],
                initial=0.0, op0=ALU.mult, op1=ALU.add), r=[Bk[4], Bk[3]], w=[Bk[6]])
            S.add("dve", I("tensor_tensor", out=HO, in0=HO, in1=H, op=ALU.add), r=[HOk, Bk[6]], w=[HOk])
            S.add("sp", I("dma_start", out=self.hrg[rows, :], in_=HO),
                  r=[HOk], w=[self.dk("hrg", k, i) for i in range(nt)], dma=("st", HOk))

    def mixout(self, l, src, srcn, dst, dstn, tiles):
        S = self.S
        p1 = l * 240 + 168
        p2 = l * 240 + 200
        p3 = l * 240 + 232
        for ti, (t0, w) in enumerate(tiles):
            i_t = self.tile_index(t0)
            self.load_tile(src, srcn, ti, t0, w)
            self.modulate(l, 1, t0, w)
            for c in range(NCH):
                slA, wkA = self.wload(self.wA[p1 + 2 * c], 4096)
                slB, wkB = self.wload(self.wA[p1 + 2 * c + 1], 4096)
                svA = slA[:, 0:4096].rearrange("p (k c) -> p k c", k=16)
                svB = slB[:, 0:4096].rearrange("p (k c) -> p k c", k=16)
                banks = []
                for (sv, wk, half) in ((svA, wkA, 0), (svA, wkA, 1), (svB, wkB, 0), (svB, wkB, 1)):
                    bk, bkk = self.bank()
                    for kc in range(16):
                        self.mm(bk[:, 0:w], sv[:, kc, half * 128:(half + 1) * 128],
                                self.xin[:, kc * TW:kc * TW + w], kc == 0, kc == 15, r=[wk, ("xin", kc)], w=[bkk])
                    banks.append((bk, bkk))
                (bg, bgk), (bb, bbk), (bc, bck), (bx, bxk) = banks
                hr = self.hrt[c % 2]
                hrk = ("hrt", c % 2)
                S.add("sp", I("dma_start", out=hr[:, 0:w],
                                                             in_=self.hrg[c * 128:(c + 1) * 128, t0:t0 + w]),
                      r=[self.dk("hrg", c, i_t)], w=[hrk], dma=hrk)
                tA, tAk = self.tmpt()
                S.add("act", I("activation", out=tA[:, 0:w], in_=bg[:, 0:w], func=AF.Square),
                      r=[bgk], w=[tAk])
                S.add("dve", I("tensor_scalar", out=tA[:, 0:w], in0=tA[:, 0:w], scalar1=0.044715,
                                                             scalar2=1.0, op0=ALU.mult, op1=ALU.add),
                      r=[tAk], w=[tAk])
                S.add("dve", I("tensor_tensor", out=tA[:, 0:w], in0=tA[:, 0:w], in1=bg[:, 0:w],
                                                                   op=ALU.mult), r=[tAk, bgk], w=[tAk])
                S.add("act", I("activation", out=tA[:, 0:w], in_=tA[:, 0:w], func=AF.Sigmoid,
                                                          scale=1.5957691216057308), r=[tAk], w=[tAk])
                S.add("dve", I("tensor_tensor", out=tA[:, 0:w], in0=tA[:, 0:w], in1=bg[:, 0:w],
                                                                   op=ALU.mult), r=[tAk, bgk], w=[tAk])
                S.add("dve", I("tensor_tensor",
                    out=self.hid[:, c * TW:c * TW + w], in0=tA[:, 0:w], in1=hr[:, 0:w], op=ALU.mult),
                    r=[tAk, hrk], w=[("hid", c)])
                tB, tBk = self.tmpt()
                tC, tCk = self.tmpt()
                S.add("act", I("activation", out=tB[:, 0:w], in_=bx[:, 0:w], func=AF.Copy),
                      r=[bxk], w=[tBk])
                S.add("dve", I("tensor_tensor", out=tB[:, 0:w], in0=bc[:, 0:w], in1=tB[:, 0:w],
                                                                   op=ALU.mult), r=[tBk, bck], w=[tBk])
                wb = (l * 16 + c) * 3
                S.add("act", I("activation", out=tC[:, 0:w], in_=tB[:, 0:w],
                                                                       func=AF.Identity,
                                                                       scale=self.C("sccw", wb + 1)),
                      r=[tBk, "cst"], w=[tCk])
                for off in (-1, 1):
                    for (a, b, s) in segs(t0, w):
                        rowlen = (b - a) if s == 1 else GRID_W
                        ov, iv = self.conv_views(tC, off, rowlen, b - a, a)
                        S.add("dve", I("scalar_tensor_tensor",
                            out=ov(tC), in0=iv(tB), scalar=self.C("sccw", wb + 1 + off), in1=ov(tC),
                            op0=ALU.mult, op1=ALU.add), r=[tBk, tCk, "cst"], w=[tCk])
                S.add("dve", I("tensor_tensor",
                    out=self.hid[:, (16 + c) * TW:(16 + c) * TW + w], in0=tC[:, 0:w], in1=bb[:, 0:w], op=ALU.mult),
                    r=[tCk, bbk], w=[("hid", 16 + c)])
            for c in range(NCH):
                slA, wkA = self.wload(self.wA[p2 + 2 * c], 4096)
                slB, wkB = self.wload(self.wA[p2 + 2 * c + 1], 4096)
                svA = slA[:, 0:4096].rearrange("p (k c) -> p k c", k=16)
                svB = slB[:, 0:4096].rearrange("p (k c) -> p k c", k=16)
                yr, yrk = self.bank()
                for kc in range(16):
                    self.mm(yr[:, 0:w], svA[:, kc, 0:128], self.hid[:, kc * TW:kc * TW + w], kc == 0, kc == 15,
                            r=[wkA, ("hid", kc)], w=[yrk])
                ys, ysk = self.bank()
                for kc in range(16):
                    self.mm(ys[:, 0:w], svA[:, kc, 128:256], self.hid[:, (16 + kc) * TW:(16 + kc) * TW + w],
                            kc == 0, kc == 15, r=[wkA, ("hid", 16 + kc)], w=[ysk])
                gr, grk = self.bank()
                for kc in range(16):
                    self.mm(gr[:, 0:w], svB[:, kc, 0:128], self.xin[:, kc * TW:kc * TW + w], kc == 0, kc == 15,
                            r=[wkB, ("xin", kc)], w=[grk])
                gs, gsk = self.bank()
                for kc in range(16):
                    self.mm(gs[:, 0:w], svB[:, kc, 128:256], self.xin[:, kc * TW:kc * TW + w], kc == 0, kc == 15,
                            r=[wkB, ("xin", kc)], w=[gsk])
                tA, tAk = self.tmpt()
                tB, tBk = self.tmpt()
                b0 = self.C("bmerge", (l * 2 + 0) * 16 + c)
                b1 = self.C("bmerge", (l * 2 + 1) * 16 + c)
                S.add("act", I("activation", out=tA[:, 0:w], in_=gr[:, 0:w],
                                                                       func=AF.Sigmoid, bias=b0),
                      r=[grk, "cst"], w=[tAk])
                S.add("act", I("activation", out=tB[:, 0:w], in_=gs[:, 0:w],
                                                                       func=AF.Sigmoid, bias=b1),
                      r=[gsk, "cst"], w=[tBk])
                S.add("dve", I("tensor_tensor", out=tA[:, 0:w], in0=tA[:, 0:w], in1=yr[:, 0:w],
                                                                   op=ALU.mult), r=[tAk, yrk], w=[tAk])
                S.add("dve", I("tensor_tensor", out=tB[:, 0:w], in0=tB[:, 0:w], in1=ys[:, 0:w],
                                                                   op=ALU.mult), r=[tBk, ysk], w=[tBk])
                S.add("dve", I("tensor_tensor",
                    out=self.hid[:, (32 + c) * TW:(32 + c) * TW + w], in0=tA[:, 0:w], in1=tB[:, 0:w], op=ALU.add),
                    r=[tAk, tBk], w=[("hid", 32 + c)])
            pending = []
            n_done = [0]
            for pn in range(8):
                slot, wk = self.wload(self.wA[p3 + pn], 4096)
                sv = slot[:, 0:4096].rearrange("p (k c) -> p k c", k=16)
                for half in range(2):
                    oc = pn * 2 + half
                    yb, ybk = self.bank()
                    for kc in range(16):
                        self.mm(yb[:, 0:w], sv[:, kc, half * 128:(half + 1) * 128],
                                self.hid[:, (32 + kc) * TW:(32 + kc) * TW + w], kc == 0, kc == 15,
                                r=[wk, ("hid", 32 + kc)], w=[ybk])
                    self.flush_stats(pending, n_done, NCH, keep=0)
                    self.resid_stats(l, 1, oc, yb, ybk, t0, w, pending)
            self.flush_stats(pending, n_done, NCH)
            self.finalize(l, 1, t0, w, dst, dstn, False)


def _panelsA(Wm):
    K, N = Wm.shape
    n = N // 256
    return np.ascontiguousarray(Wm.reshape(16, 128, n, 256).transpose(2, 1, 0, 3)).reshape(n, 128, 4096)


def _panelsB(Wm):
    return np.ascontiguousarray(Wm.reshape(44, 128, 16, 128).transpose(2, 1, 0, 3)).reshape(16, 128, 5632)


def _fm(v):
    v = np.asarray(v, np.float32)
    lead = v.shape[:-1]
    return np.moveaxis(v.reshape(*lead, 16, 128), -1, 0)


def prepare(inp, n_cores=8):
    f = lambda k: np.asarray(inp[k], np.float32)
    w_in = f("w_in")
    wA = np.empty((L * 240, 128, 4096), np.float32)
    wB = np.empty((L * 32, 128, 5632), np.float32)
    for l in range(L):
        base = l * 240
        wA[base:base + 72] = _panelsA(f("w_mod")[l])
        for which, nm in enumerate(("ffn1", "ffn2")):
            wi = f(nm + "_w_in")[l]
            g = wi[:, :DFF].reshape(D, NFF, 128)
            u = wi[:, DFF:].reshape(D, NFF, 128)
            wA[base + 72 + which * 44:base + 72 + (which + 1) * 44] = _panelsA(
                np.concatenate([g, u], axis=2).reshape(D, NFF * 256))
            wB[l * 32 + which * 16:l * 32 + (which + 1) * 16] = _panelsB(f(nm + "_w_out")[l])
        W = w_in[l]
        part = lambda i: W[:, i * 2048:(i + 1) * 2048].reshape(D, 16, 128)
        rg_x, rg_gate, sc_b, sc_c, sc_x, g_rg, g_sc = [part(i) for i in range(7)]
        wA[base + 160:base + 168] = _panelsA(W[:, 0:2048])
        p1 = np.stack([np.concatenate([rg_gate, sc_b], axis=2), np.concatenate([sc_c, sc_x], axis=2)], axis=2)
        wA[base + 168:base + 200] = _panelsA(p1.reshape(D, 32 * 256))
        wro = f("w_rg_out")[l].reshape(D, 16, 128)
        wso = f("w_sc_out")[l].reshape(D, 16, 128)
        p2 = np.stack([np.concatenate([wro, wso], axis=2), np.concatenate([g_rg, g_sc], axis=2)], axis=2)
        wA[base + 200:base + 232] = _panelsA(p2.reshape(D, 32 * 256))
        wA[base + 232:base + 240] = _panelsA(f("w_o")[l])
    x, ctx, c, c_ctx = f("x"), f("ctx"), f("c"), f("c_ctx")
    gate_w = f("rg_gate_w")
    in_maps = []
    for core in range(n_cores):
        b, half = core // 2, core % 2
        xc = ctx[b]
        xl = x[b, half * TLAT:(half + 1) * TLAT]
        if half == 1:
            xc = xc[::-1]
            xl = xl[::-1]
        xT = np.ascontiguousarray(np.concatenate([xc, xl], axis=0).T)
        cs = np.zeros((128, NCONST), np.float32)

        def put(name, arr):
            arr = np.ascontiguousarray(arr, dtype=np.float32).reshape(128, -1)
            cs[:, _off[name]:_off[name] + arr.shape[1]] = arr

        put("bmod", np.moveaxis(f("b_mod").reshape(L, 144, 128), -1, 0))
        put("lng", _fm(f("ln_g")))
        put("lnb", _fm(f("ln_b")))
        put("bmerge", _fm(f("b_merge")))
        rw = f("rg_conv_w")
        z = np.zeros_like(rw[:, :1])
        taps = np.concatenate([rw, z], axis=1) if half == 0 else np.concatenate([z, rw[:, ::-1]], axis=1)
        put("rgcw", np.moveaxis(_fm(taps), 2, 3))
        put("rgcb", _fm(f("rg_conv_b")))
        sw = f("sc_conv_w")
        if half == 1:
            sw = sw[:, ::-1]
        put("sccw", np.moveaxis(_fm(sw), 2, 3))
        dirs = [0, 1] if half == 0 else [1, 0]
        put("gb", _fm(f("rg_gate_b")[:, dirs]))
        put("lam", _fm(f("rg_lam")[:, dirs]))
        put("sel", np.tile(np.array([0.0, 1.0] if half == 0 else [1.0, 0.0], np.float32), (128, 1)))
        put("cc", np.stack([_fm(c[b]), _fm(c_ctx)], axis=-1))
        gw = gate_w[:, dirs]
        gw = np.ascontiguousarray(gw.transpose(4, 0, 1, 3, 2, 5)).reshape(128, L * 2 * 16 * 2, 128)
        in_maps.append({"xT": xT, "consts": cs, "gw": gw, "wA": wA, "wB": wB})
    return in_maps


_NC_CACHE = {}


def kernel(**inputs):
    n = 8
    in_maps = prepare(inputs, n)
    if "nc" not in _NC_CACHE:
        _NC_CACHE["nc"] = Builder(n).build()
    res = run_bass_kernel_spmd(_NC_CACHE["nc"], in_maps, core_ids=list(range(n)))
    out = np.empty((4, 4096, D), np.float32)
    for core in range(n):
        b, half = core // 2, core % 2
        o = res.results[core]["outT"].T
        if half == 1:
            o = o[::-1]
        out[b, half * TLAT:(half + 1) * TLAT] = o
    return out
```

```python
import contextlib
import numpy as np
import concourse.bass as bass
import concourse.mybir as mybir
from concourse.bass_utils import run_bass_kernel_spmd

F32 = mybir.dt.float32
BF16 = mybir.dt.bfloat16
AF = mybir.ActivationFunctionType
ALU = mybir.AluOpType

D = 2048
NCH = 16
DFF = 5632
NFF = 44
TCTX = 256
TLAT = 2048
T = TCTX + TLAT
L = 2
GRID_W = 64
TW = 512
TILES = [(0, 512), (512, 512), (1024, 512), (1536, 512), (2048, 256)]
ALPHA = (2 * L) ** 0.25
EPSP = 1e-5 / ALPHA ** 2
NSLOT = 4
SLOT_ELEMS = 5632
EPOCH = 30000

_off = {}
_cur = 0
for _name, _n in [("bmod", L * 144), ("lng", L * 3 * 16), ("lnb", L * 3 * 16), ("bmerge", L * 2 * 16),
                  ("rgcw", L * 16 * 5), ("rgcb", L * 16), ("sccw", L * 16 * 3), ("gb", L * 2 * 2 * 16),
                  ("lam", L * 2 * 16), ("sel", 2), ("cc", 32)]:
    _off[_name] = _cur
    _cur += _n
NCONST = _cur


class Op:
    __slots__ = ("eng", "fn", "deps", "dma", "dma_val", "needed", "ms")

    def __init__(self, eng, fn, deps, dma):
        self.eng, self.fn, self.deps, self.dma = eng, fn, deps, dma
        self.dma_val = 0
        self.needed = False
        self.ms = 0


class Sched:
    ENG = ("pe", "act", "dve", "pool", "sp")

    def __init__(self):
        self.ops = []
        self.lw = {}
        self.rd = {}
        self.last_on = {e: None for e in self.ENG}
        self.fence_deps = {e: [] for e in self.ENG}
        self.dma_cnt = {}

    def add(self, eng, fn, r=(), w=(), dma=None):
        idx = len(self.ops)
        deps = set(self.fence_deps[eng])
        self.fence_deps[eng] = []
        for k in r:
            x = self.lw.get(k)
            if x is not None:
                deps.add(x)
        for k in w:
            x = self.lw.get(k)
            if x is not None:
                deps.add(x)
            for x in self.rd.get(k, {}).values():
                deps.add(x)
        for k in r:
            self.rd.setdefault(k, {})[eng if dma is None else ("dma", idx)] = idx
        for k in w:
            self.lw[k] = idx
            self.rd[k] = {}
        deps.discard(idx)
        op = Op(eng, fn, sorted(deps), dma)
        if dma is not None:
            self.dma_cnt[dma] = self.dma_cnt.get(dma, 0) + 16
            op.dma_val = self.dma_cnt[dma]
        for d in deps:
            self.ops[d].needed = True
        self.ops.append(op)
        if dma is None:
            self.last_on[eng] = idx
        return idx

    def fence(self):
        f = [v for v in self.last_on.values() if v is not None]
        for e in self.ENG:
            self.fence_deps[e] = list(f)

    def emit(self, nc, stack):
        cnt = {e: 0 for e in self.ENG}
        for op in self.ops:
            if op.dma is None and op.needed:
                cnt[op.eng] += 1
                op.ms = cnt[op.eng]
        esem = {e: [stack.enter_context(nc.semaphore(f"s_{e}_{i}")) for i in range(cnt[e] // EPOCH + 1)]
                for e in self.ENG}
        dsem = {}
        for i, ch in enumerate(sorted(self.dma_cnt, key=str)):
            assert self.dma_cnt[ch] < 60000, (ch, self.dma_cnt[ch])
            dsem[ch] = stack.enter_context(nc.semaphore(f"d_{i}"))
        ops = self.ops
        by_eng = {e: [op for op in ops if op.eng == e] for e in self.ENG}

        def run(e, eo):
            waited = {}
            for op in by_eng[e]:
                for d in op.deps:
                    p = ops[d]
                    if p.dma is not None:
                        sid, sem, val = ("d", p.dma), dsem[p.dma], p.dma_val
                    else:
                        if p.eng == "pe" and e == "pe":
                            continue
                        ep = (p.ms - 1) // EPOCH
                        sid, sem, val = (p.eng, ep), esem[p.eng][ep], (p.ms - 1) % EPOCH + 1
                    if waited.get(sid, 0) >= val:
                        continue
                    eo.wait_ge(sem, val)
                    waited[sid] = val
                if op.fn is None:
                    continue
                m_, a_, k_ = op.fn
                ins = getattr(eo, m_)(*a_, **k_)
                if op.dma is not None:
                    ins.then_inc(dsem[op.dma], 16)
                elif op.needed:
                    ins.then_inc(esem[e][(op.ms - 1) // EPOCH], 1)

        block = stack.enter_context(nc.Block())

        @block.tensor
        def _(eo):
            run("pe", eo)

        @block.scalar
        def _(eo):
            run("act", eo)

        @block.vector
        def _(eo):
            run("dve", eo)

        @block.gpsimd
        def _(eo):
            run("pool", eo)

        @block.sync
        def _(eo):
            run("sp", eo)


def I(method, *args, **kw):
    return (method, args, kw)


def segs(t0, w):
    out = []
    if t0 < TCTX:
        out.append((0, min(w, TCTX - t0), 1))
    if t0 + w > TCTX:
        out.append((max(0, TCTX - t0), w, 0))
    return out


class Builder:
    def __init__(self, n_cores, stop_after=None, dbg=None):
        self.n_cores = n_cores
        self.stop_after = stop_after
        self.dbg = dbg

    def build(self):
        nc = bass.Bass("TRN2", target_bir_lowering=False)
        self.nc = nc
        S = Sched()
        self.S = S
        with contextlib.ExitStack() as st:
            self.alloc(nc, st)
            self.program()
            S.emit(nc, st)
        return nc

    def alloc(self, nc, st):
        def dram_in(name, shape):
            return nc.dram_tensor(name, shape, F32, kind="ExternalInput").ap()

        self.xT = dram_in("xT", [D, T])
        self.consts = dram_in("consts", [128, NCONST])
        self.gw = dram_in("gw", [128, L * 2 * 16 * 2, 128])
        self.wA = dram_in("wA", [L * 240, 128, 4096])
        self.wB = dram_in("wB", [L * 32, 128, 5632])
        self.outT = nc.dram_tensor("outT", [D, TLAT], F32, kind="ExternalOutput").ap()
        if self.dbg:
            self.dbg_out = nc.dram_tensor("dbg", [D, T], F32, kind="ExternalOutput").ap()
        self.xa = nc.dram_tensor("xa", [D, T], F32).ap()
        self.xb = nc.dram_tensor("xb", [D, T], F32).ap()
        self.rgx = nc.dram_tensor("rgx", [D, T], F32).ap()
        self.hrg = nc.dram_tensor("hrg", [D, T], F32).ap()
        self.st_in = nc.dram_tensor("st_in", [128, 16], F32)
        self.st_out = nc.dram_tensor("st_out", [256, 16], F32)

        def sb(name, shape, dt=F32):
            return st.enter_context(nc.sbuf_tensor(name, shape, dt))

        self.cst = sb("cst", [128, NCONST])
        self.modT = sb("modT", [128, L * 288])
        self.sc1p = sb("sc1p", [128, L * 96])
        self.gsc = sb("gsc", [128, L * 96])
        self.c8 = sb("c8", [128, L * 32])
        self.c16 = sb("c16", [128, L * 32])
        self.scb = sb("scb", [128, 32], BF16)
        self.ones = sb("ones", [128, 128])
        self.states = sb("states", [128, 16])
        self.sgt = sb("sgt", [128, 32])
        self.h0 = sb("h0", [128, 16])
        self.gwt = sb("gwt", [128, 2, 128])
        self.wslot = [sb(f"wslot{i}", [128, SLOT_ELEMS], BF16) for i in range(NSLOT)]
        self.xs = sb("xs", [128, NCH * TW])
        self.xin2 = [sb("xin0", [128, NCH * TW], BF16), sb("xin1", [128, NCH * TW], BF16)]
        self.px = [sb(f"px{i}", [128, TW]) for i in range(2)]
        self.deferred = []
        self.hid = sb("hid", [128, 48 * TW], BF16)
        self.tmp = [sb(f"tmp{i}", [128, TW]) for i in range(8)]
        self.hrt = [sb(f"hrt{i}", [128, TW]) for i in range(2)]
        self.stg = [sb(f"stg{i}", [128, TW]) for i in range(2)]
        self.mean = sb("mean", [128, TW])
        self.rstd = sb("rstd", [128, TW])
        self.ps = [st.enter_context(nc.psum_tensor(f"ps{i}", [128, 512], F32)) for i in range(8)]
        hid32 = self.hid.bitcast(F32)
        self.sbuf_scan = [hid32[:, i * T:(i + 1) * T] for i in range(5)] + \
                         [self.xs[:, i * T:(i + 1) * T] for i in range(3)]
        self.ws_i = 0
        self.pp_i = 0
        self.tp_i = 0
        self.stg_i = 0

    def C(self, name, idx, n=1):
        o = _off[name] + idx
        return self.cst[:, o:o + n]

    def wload(self, dram_panel, nelem):
        i = self.ws_i % NSLOT
        self.ws_i += 1
        slot = self.wslot[i]
        self.S.add("pool", I("dma_start", out=slot[:, 0:nelem], in_=dram_panel, max_dma_last_dim=8192),
                   w=[("w", i)], dma=("w", i))
        return slot, ("w", i)

    def bank(self):
        i = self.pp_i % 6
        self.pp_i += 1
        return self.ps[i], ("ps", i)

    def tmpt(self):
        i = self.tp_i % 8
        self.tp_i += 1
        return self.tmp[i], ("tmp", i)

    def stage(self):
        i = self.stg_i % 2
        self.stg_i += 1
        return self.stg[i], ("stg", i)

    def mm(self, out, lhsT, rhs, start, stop, r, w):
        self.S.add("pe", I("matmul", out, lhsT, rhs, start=start, stop=stop), r=r, w=w)

    def mod(self, l, j, c, s):
        o = (l * 144 + j * 16 + c) * 2 + s
        return self.modT[:, o:o + 1]

    def dk(self, name, c, i):
        return ("dram", name, c, i)

    def program(self):
        S = self.S
        nc = self.nc
        S.add("sp", I("dma_start", out=self.cst[:], in_=self.consts), w=["cst"], dma="cst")
        S.add("dve", I("memset", self.ones[:], 1.0 / D), w=["ones"])
        self.setup_consts()
        for l in range(L):
            self.mods(l)
        if self.stop_after == "mods":
            return self.finish_dbg_small()
        cur = (self.xT, "xT")
        A = (self.xa, "xa")
        Bb = (self.xb, "xb")
        seq = []

        def tile_items(kind, l, i, src, dst, tiles, final=False):
            for (t0, w) in tiles:
                if kind == "ffn":
                    body = (lambda par, mid, l=l, i=i, src=src, dst=dst, t0=t0, w=w, final=final:
                            self.ffn_tile(l, i, src[0], src[1], dst[0], dst[1], t0, w, final, par, mid))
                elif kind == "rgx":
                    body = (lambda par, mid, l=l, t0=t0, w=w: self.rgx_tile(l, t0, w, par, mid))
                else:
                    body = (lambda par, mid, l=l, src=src, dst=dst, t0=t0, w=w:
                            self.mix_tile(l, src[0], src[1], dst[0], dst[1], t0, w, par, mid))
                prep = (lambda par, l=l, i=i, src=src, t0=t0, w=w: self.prep(l, i, src[0], src[1], t0, w, par))
                seq.append(("tile", prep, body))

        for l in range(L):
            last = l == L - 1
            tile_items("ffn", l, 0, cur, A, TILES)
            seq.append(("mark", f"ffn1_{l}", A))
            tile_items("rgx", l, 1, A, None, TILES)
            seq.append(("mark", f"rgx_{l}", (self.rgx, "rgx")))
            seq.append(("barrier", lambda l=l: self.scan(l)))
            seq.append(("mark", f"scan_{l}", (self.hrg, "hrg")))
            tiles = TILES if not last else [(256, 256)] + TILES[1:]
            tile_items("mix", l, 1, A, Bb, tiles)
            seq.append(("mark", f"mix_{l}", Bb))
            if last:
                tile_items("ffn", l, 2, Bb, (self.outT, "out"), tiles, final=True)
            else:
                tile_items("ffn", l, 2, Bb, A, tiles)
                seq.append(("mark", f"ffn2_{l}", A))
                cur = A
                A, Bb = Bb, A
        prepped = -1
        for idx, it in enumerate(seq):
            if it[0] == "mark":
                if self.stop_after == it[1]:
                    self.flush_deferred()
                    return self.finish_dbg(*it[2])
                continue
            if it[0] == "barrier":
                self.flush_deferred()
                S.fence()
                it[1]()
                S.fence()
                continue
            par = idx % 2
            if prepped != idx:
                it[1](par)
            nxt = None
            for k in range(idx + 1, len(seq)):
                if seq[k][0] == "mark":
                    continue
                if seq[k][0] == "tile":
                    nxt = k
                break
            if nxt is not None and self.stop_after is not None:
                for k in range(idx + 1, nxt):
                    if seq[k][0] == "mark" and seq[k][1] == self.stop_after:
                        nxt = None
                        break

            def mid(nxt=nxt):
                if nxt is not None:
                    seq[nxt][1](nxt % 2)
            it[2](par, mid)
            if nxt is not None:
                prepped = nxt
        self.flush_deferred()
        S.add("sp", None, r=[self.dk("out", c, i) for c in range(NCH) for i in range(len(TILES))])

    def finish_dbg(self, src_ap, name):
        S = self.S
        S.fence()
        for c in range(NCH):
            buf = self.sbuf_scan[c % 2]
            key = ("scanbuf", c % 2)
            S.add("sp", I("dma_start", out=buf, in_=src_ap[c * 128:(c + 1) * 128, :]),
                  r=[self.dk(name, c, i) for i in range(len(TILES))], w=[key], dma=("dbgl", c % 2))
            S.add("sp", I("dma_start", out=self.dbg_out[c * 128:(c + 1) * 128, :], in_=buf),
                  r=[key], w=[("dbgout", c)], dma=("dbgs", c % 2))
        S.add("sp", None, r=[("dbgout", c) for c in range(NCH)])

    def finish_dbg_small(self):
        S = self.S
        S.add("sp", I("dma_start", out=self.dbg_out[0:128, 0:L * 288], in_=self.modT[:]),
              r=["modT0", "modT1"], w=["dbgo"], dma="dbgs")
        S.add("sp", I("dma_start", out=self.dbg_out[128:256, 0:L * 32], in_=self.c8[:]),
              r=["c8"], w=["dbgo2"], dma="dbgs2")
        S.add("sp", None, r=["dbgo", "dbgo2"])

    def setup_consts(self):
        S = self.S
        lam = self.C("lam", 0, L * 32)
        t0 = self.tmp[0]
        S.add("act", I("activation", out=t0[:, 0:L * 32], in_=lam, func=AF.Exp, scale=-1.0),
              r=["cst"], w=[("tmp", 0)])
        S.add("act", I("activation", out=t0[:, 0:L * 32], in_=t0[:, 0:L * 32], func=AF.Ln, bias=1.0),
              r=[("tmp", 0)], w=[("tmp", 0)])
        S.add("dve", I("tensor_scalar", out=self.c8[:], in0=t0[:, 0:L * 32], scalar1=-8.0, scalar2=None,
                                               op0=ALU.mult), r=[("tmp", 0)], w=["c8"])
        S.add("dve", I("tensor_scalar", out=self.c16[:], in0=t0[:, 0:L * 32], scalar1=-16.0, scalar2=None,
                                               op0=ALU.mult), r=[("tmp", 0)], w=["c16"])
        cc = self.C("cc", 0, 32)
        S.add("act", I("activation", out=self.scb[:], in_=cc, func=AF.Silu), r=["cst"], w=["scb"])

    def mods(self, l):
        S = self.S
        psb = self.ps[6 + l]
        for pn in range(72):
            slot, wk = self.wload(self.wA[l * 240 + pn], 4096)
            sv = slot[:, 0:4096].rearrange("p (k c) -> p k c", k=16)
            for half in range(2):
                m = pn * 2 + half
                for kc in range(16):
                    self.mm(psb[:, 2 * m:2 * m + 2], sv[:, kc, half * 128:(half + 1) * 128],
                            self.scb[:, 2 * kc:2 * kc + 2], kc == 0, kc == 15,
                            r=[wk, "scb"], w=[("ps", 6 + l)])
        mt = self.modT[:, l * 288:(l + 1) * 288]
        bm = self.C("bmod", l * 144, 144)
        for s in range(2):
            S.add("dve", I("tensor_tensor", out=mt[:, s::2], in0=psb[:, s:288:2], in1=bm, op=ALU.add),
                  r=[("ps", 6 + l), "cst"], w=[f"modT{l}"])
        coef = [0.5 / ALPHA, 1.0 / ALPHA, 0.5 / ALPHA]
        for i in range(3):
            src1 = self.modT[:, (l * 144 + (3 * i + 1) * 16) * 2:(l * 144 + (3 * i + 2) * 16) * 2]
            dst1 = self.sc1p[:, l * 96 + i * 32:l * 96 + (i + 1) * 32]
            S.add("dve", I("tensor_scalar", out=dst1, in0=src1, scalar1=1.0, scalar2=None,
                                                                  op0=ALU.add), r=[f"modT{l}"], w=[f"sc1p{l}"])
            src2 = self.modT[:, (l * 144 + (3 * i + 2) * 16) * 2:(l * 144 + (3 * i + 3) * 16) * 2]
            dst2 = self.gsc[:, l * 96 + i * 32:l * 96 + (i + 1) * 32]
            S.add("dve", I("tensor_scalar", out=dst2, in0=src2, scalar1=coef[i],
                                                                              scalar2=None, op0=ALU.mult),
                  r=[f"modT{l}"], w=[f"gsc{l}"])

    def SC1P(self, l, i, c, s):
        o = l * 96 + i * 32 + c * 2 + s
        return self.sc1p[:, o:o + 1]

    def GSC(self, l, i, c, s):
        o = l * 96 + i * 32 + c * 2 + s
        return self.gsc[:, o:o + 1]

    def load_tile(self, src, srcn, ti, t0, w):
        S = self.S
        i_t = self.tile_index(t0)
        srcv = src.rearrange("(c p) t -> p c t", p=128)[:, :, t0:t0 + w]
        dstv = self.xs.rearrange("p (c t) -> p c t", c=NCH)[:, :, 0:w]
        S.add("sp", I("dma_start", out=dstv, in_=srcv),
              r=[self.dk(srcn, c, i_t) for c in range(NCH)], w=[("xs", c) for c in range(NCH)], dma="xs")

    def tile_index(self, t0):
        return t0 // TW

    def prep(self, l, i, src, srcn, t0, w, par):
        S = self.S
        i_t = self.tile_index(t0)
        xin = self.xin2[par]
        for c in range(NCH):
            px = self.px[c % 2]
            pxk = ("px", c % 2)
            S.add("sp", I("dma_start", out=px[:, 0:w], in_=src[c * 128:(c + 1) * 128, t0:t0 + w]),
                  r=[self.dk(srcn, c, i_t)], w=[pxk], dma=pxk)
            for (a, b, s) in segs(t0, w):
                S.add("act", I("activation", out=xin[:, c * TW + a:c * TW + b], in_=px[:, a:b],
                               func=AF.Identity, scale=self.SC1P(l, i, c, s), bias=self.mod(l, 3 * i, c, s)),
                      r=[pxk, f"modT{l}", f"sc1p{l}"], w=[("xin", par, c)])

    def pop_deferred(self, n=1):
        for _ in range(n):
            if self.deferred:
                self.deferred.pop(0)()

    def flush_deferred(self):
        while self.deferred:
            self.deferred.pop(0)()

    def resid_stats(self, l, i, oc, yb, ybk, t0, w, pending):
        S = self.S
        xsl = self.xs[:, oc * TW:oc * TW + w]
        for (a, b, s) in segs(t0, w):
            S.add("dve", I("scalar_tensor_tensor",
                out=self.xs[:, oc * TW + a:oc * TW + b], in0=yb[:, a:b], scalar=self.GSC(l, i, oc, s),
                in1=self.xs[:, oc * TW + a:oc * TW + b], op0=ALU.mult, op1=ALU.add),
                r=[ybk, ("xs", oc), f"gsc{l}"], w=[("xs", oc)])
        sq, sqk = self.tmpt()
        S.add("act", I("activation", out=sq[:, 0:w], in_=xsl, func=AF.Square), r=[("xs", oc)], w=[sqk])

        def stats(first, lastf):
            self.mm(self.ps[6][:, 0:w], self.ones[:], xsl, first, lastf, r=[("xs", oc), "ones"], w=[("ps", 6)])
            self.mm(self.ps[7][:, 0:w], self.ones[:], sq[:, 0:w], first, lastf, r=[sqk, "ones"], w=[("ps", 7)])
        pending.append(stats)

    def flush_stats(self, pending, n_done, total, keep=0):
        while len(pending) > keep:
            f = pending.pop(0)
            f(n_done[0] == 0, n_done[0] == total - 1)
            n_done[0] += 1

    def finalize(self, l, i, t0, w, dst, dstn, final):
        S = self.S
        i_t = self.tile_index(t0)
        mean, rstd = self.mean, self.rstd
        msq, msqk = self.tmpt()
        S.add("act", I("activation", out=mean[:, 0:w], in_=self.ps[6][:, 0:w], func=AF.Copy),
              r=[("ps", 6)], w=["mean"])
        S.add("act", I("activation", out=msq[:, 0:w], in_=self.ps[6][:, 0:w], func=AF.Square),
              r=[("ps", 6)], w=[msqk])
        S.add("dve", I("tensor_tensor", out=msq[:, 0:w], in0=self.ps[7][:, 0:w], in1=msq[:, 0:w],
                       op=ALU.subtract), r=[("ps", 7), msqk], w=[msqk])
        S.add("dve", I("tensor_scalar", out=msq[:, 0:w], in0=msq[:, 0:w], scalar1=0.0, scalar2=EPSP,
                       op0=ALU.max, op1=ALU.add), r=[msqk], w=[msqk])
        S.add("act", I("activation", out=msq[:, 0:w], in_=msq[:, 0:w], func=AF.Sqrt), r=[msqk], w=[msqk])
        S.add("dve", I("reciprocal", out=rstd[:, 0:w], in_=msq[:, 0:w]), r=[msqk], w=["rstd"])

        def chunk(c):
            t1, t1k = self.tmpt()
            xsl = self.xs[:, c * TW:c * TW + w]
            S.add("dve", I("tensor_tensor", out=t1[:, 0:w], in0=xsl, in1=mean[:, 0:w], op=ALU.subtract),
                  r=[("xs", c), "mean"], w=[t1k])
            S.add("dve", I("tensor_tensor", out=t1[:, 0:w], in0=t1[:, 0:w], in1=rstd[:, 0:w], op=ALU.mult),
                  r=[t1k, "rstd"], w=[t1k])
            sg, sgk = self.stage()
            g = self.C("lng", (l * 3 + i) * 16 + c)
            bb = self.C("lnb", (l * 3 + i) * 16 + c)
            S.add("act", I("activation", out=sg[:, 0:w], in_=t1[:, 0:w], func=AF.Identity, scale=g, bias=bb),
                  r=[t1k, "cst"], w=[sgk])
            if final:
                dsl = dst[c * 128:(c + 1) * 128, t0 - TCTX:t0 - TCTX + w]
            else:
                dsl = dst[c * 128:(c + 1) * 128, t0:t0 + w]
            S.add("sp", I("dma_start", out=dsl, in_=sg[:, 0:w]), r=[sgk], w=[self.dk(dstn, c, i_t)], dma=sgk)
        for c in range(NCH):
            self.deferred.append(lambda c=c: chunk(c))

    def ffn_tile(self, l, i, src, srcn, dst, dstn, t0, w, final, par, mid):
        S = self.S
        which = 0 if i == 0 else 1
        pa = l * 240 + 72 + which * 44
        pb = l * 32 + which * 16
        xin = self.xin2[par]
        loaded = False
        for j in range(NFF):
            slot, wk = self.wload(self.wA[pa + j], 4096)
            sv = slot[:, 0:4096].rearrange("p (k c) -> p k c", k=16)
            gb, gbk = self.bank()
            ub, ubk = self.bank()
            for kc in range(16):
                self.mm(gb[:, 0:w], sv[:, kc, 0:128], xin[:, kc * TW:kc * TW + w], kc == 0, kc == 15,
                        r=[wk, ("xin", par, kc)], w=[gbk])
            for kc in range(16):
                self.mm(ub[:, 0:w], sv[:, kc, 128:256], xin[:, kc * TW:kc * TW + w], kc == 0, kc == 15,
                        r=[wk, ("xin", par, kc)], w=[ubk])
            sg, sgk = self.tmpt()
            S.add("act", I("activation", out=sg[:, 0:w], in_=gb[:, 0:w], func=AF.Silu), r=[gbk], w=[sgk])
            S.add("dve", I("tensor_tensor", out=self.hid[:, j * TW:j * TW + w], in0=sg[:, 0:w], in1=ub[:, 0:w],
                           op=ALU.mult), r=[sgk, ubk], w=[("hid", j)])
            self.pop_deferred(1)
            if not self.deferred and not loaded:
                self.load_tile(src, srcn, 0, t0, w)
                loaded = True
        mid()
        pending = []
        n_done = [0]
        for oc in range(NCH):
            slot, wk = self.wload(self.wB[pb + oc], 5632)
            sv = slot[:, 0:5632].rearrange("p (k c) -> p k c", k=NFF)
            yb, ybk = self.bank()
            for kc in range(NFF):
                self.mm(yb[:, 0:w], sv[:, kc, :], self.hid[:, kc * TW:kc * TW + w], kc == 0, kc == NFF - 1,
                        r=[wk, ("hid", kc)], w=[ybk])
            self.flush_stats(pending, n_done, NCH, keep=0)
            self.resid_stats(l, i, oc, yb, ybk, t0, w, pending)
        self.flush_stats(pending, n_done, NCH)
        self.finalize(l, i, t0, w, dst, dstn, final)

    def rgx_tile(self, l, t0, w, par, mid):
        S = self.S
        pa = l * 240 + 160
        i_t = self.tile_index(t0)
        xin = self.xin2[par]
        for pn in range(8):
            slot, wk = self.wload(self.wA[pa + pn], 4096)
            sv = slot[:, 0:4096].rearrange("p (k c) -> p k c", k=16)
            for half in range(2):
                oc = pn * 2 + half
                yb, ybk = self.bank()
                for kc in range(16):
                    self.mm(yb[:, 0:w], sv[:, kc, half * 128:(half + 1) * 128],
                            xin[:, kc * TW:kc * TW + w], kc == 0, kc == 15, r=[wk, ("xin", par, kc)], w=[ybk])
                self.pop_deferred(1)
                sg, sgk = self.stage()
                S.add("act", I("activation", out=sg[:, 0:w], in_=yb[:, 0:w], func=AF.Copy), r=[ybk], w=[sgk])
                dsl = self.rgx[oc * 128:(oc + 1) * 128, t0:t0 + w]
                S.add("sp", I("dma_start", out=dsl, in_=sg[:, 0:w]), r=[sgk], w=[self.dk("rgx", oc, i_t)], dma=sgk)
            if pn == 3:
                mid()

    def conv_views(self, ap, off, rowlen, ncols, c0):
        v = ap[:, c0:c0 + ncols]
        if rowlen != ncols:
            v = v.rearrange("p (r g) -> p r g", g=rowlen)
            lo, hi = max(0, -off), rowlen - max(0, off)
            return (lambda x: x[:, c0:c0 + ncols].rearrange("p (r g) -> p r g", g=rowlen)[:, :, lo:hi],
                    lambda x: x[:, c0:c0 + ncols].rearrange("p (r g) -> p r g", g=rowlen)[:, :, lo + off:hi + off])
        lo, hi = max(0, -off), rowlen - max(0, off)
        return (lambda x: x[:, c0 + lo:c0 + hi], lambda x: x[:, c0 + lo + off:c0 + hi + off])

    def gates(self, l, which, k, XC, XCk, Rg, Rgk, Ig, Igk, A, Ak, M, Mk):
        S = self.S
        gi = ((l * 2 + which) * 16 + k) * 2
        S.add("sp", I("dma_start", out=self.gwt[:], in_=self.gw[:, gi:gi + 2, :]), w=["gwt"], dma="gwt")
        for (t0, w) in TILES:
            rb, rbk = self.bank()
            ib, ibk = self.bank()
            self.mm(rb[:, 0:w], self.gwt[:, 0, :], XC[:, t0:t0 + w], True, True, r=["gwt", XCk], w=[rbk])
            self.mm(ib[:, 0:w], self.gwt[:, 1, :], XC[:, t0:t0 + w], True, True, r=["gwt", XCk], w=[ibk])
            br = self.C("gb", ((l * 2 + which) * 2 + 0) * 16 + k)
            bi = self.C("gb", ((l * 2 + which) * 2 + 1) * 16 + k)
            S.add("act", I("activation", out=Rg[:, t0:t0 + w], in_=rb[:, 0:w],
                                                                         func=AF.Sigmoid, bias=br),
                  r=[rbk, "cst"], w=[Rgk])
            S.add("act", I("activation", out=Ig[:, t0:t0 + w], in_=ib[:, 0:w],
                                                                         func=AF.Sigmoid, bias=bi),
                  r=[ibk, "cst"], w=[Igk])
        co = (l * 2 + which) * 16 + k
        S.add("act", I("activation", out=A, in_=Rg, func=AF.Exp, scale=self.c8[:, co:co + 1]),
              r=[Rgk, "c8"], w=[Ak])
        S.add("act", I("activation", out=M, in_=Rg, func=AF.Exp, scale=self.c16[:, co:co + 1]),
              r=[Rgk, "c16"], w=[Mk])
        S.add("dve", I("tensor_scalar", out=M, in0=M, scalar1=1.0, scalar2=None, op0=ALU.min),
              r=[Mk], w=[Mk])
        S.add("act", I("activation", out=M, in_=M, func=AF.Sqrt, scale=-1.0, bias=1.0), r=[Mk], w=[Mk])
        S.add("dve", I("tensor_tensor", out=Ig, in0=Ig, in1=XC, op=ALU.mult), r=[Igk, XCk], w=[Igk])
        S.add("dve", I("tensor_tensor", out=Ig, in0=Ig, in1=M, op=ALU.mult), r=[Igk, Mk], w=[Igk])

    def scan(self, l):
        S = self.S
        B = self.sbuf_scan
        Bk = [("scanbuf", i) for i in range(8)]
        nt = len(TILES)
        for k in range(16):
            U, Uk = B[0], Bk[0]
            XC, XCk = B[1], Bk[1]
            rows = slice(k * 128, (k + 1) * 128)
            S.add("sp", I("dma_start", out=U, in_=self.rgx[rows, :]),
                  r=[self.dk("rgx", k, i) for i in range(nt)], w=[Uk], dma=Uk)
            wb = (l * 16 + k) * 5
            bias = self.C("rgcb", l * 16 + k)
            S.add("dve", I("tensor_scalar", out=XC, in0=U, scalar1=self.C("rgcw", wb + 2),
                                                                   scalar2=bias, op0=ALU.mult, op1=ALU.add),
                  r=[Uk, "cst"], w=[XCk])
            for off in (-2, -1, 1, 2):
                for (c0, ncols, rowlen) in ((0, TCTX, TCTX), (TCTX, TLAT, GRID_W)):
                    ov, iv = self.conv_views(XC, off, rowlen, ncols, c0)
                    S.add("dve", I("scalar_tensor_tensor",
                        out=ov(XC), in0=iv(U), scalar=self.C("rgcw", wb + 2 + off), in1=ov(XC),
                        op0=ALU.mult, op1=ALU.add), r=[Uk, XCk, "cst"], w=[XCk])
            S.add("sp", I("dma_start", out=self.rgx[rows, :], in_=XC),
                  r=[XCk], w=[self.dk("rgx", k, i) for i in range(nt)], dma=("st", XCk))
            Rg, Ig, A, M, H = B[2], B[3], B[4], B[5], B[6]
            self.gates(l, 0, k, XC, XCk, Rg, Bk[2], Ig, Bk[3], A, Bk[4], M, Bk[5])
            S.add("dve", I("tensor_tensor_scan", out=H, data0=A, data1=Ig, initial=0.0, op0=ALU.mult,
                                                       op1=ALU.add), r=[Bk[4], Bk[3]], w=[Bk[6]])
            S.add("dve", I("tensor_copy", out=self.states[:, k:k + 1], in_=H[:, T - 1:T]),
                  r=[Bk[6]], w=["states"])
            S.add("sp", I("dma_start", out=self.hrg[rows, :], in_=H),
                  r=[Bk[6]], w=[self.dk("hrg", k, i) for i in range(nt)], dma=("st", Bk[6]))
        S.add("sp", I("dma_start", out=self.st_in[:, :], in_=self.states[:]), r=["states"], w=["st_in"],
              dma="st_in")
        groups = [[2 * g, 2 * g + 1] for g in range(self.n_cores // 2)]
        S.add("pool", I("collective_compute", "AllGather", ALU.bypass, replica_groups=groups,
                                                     ins=[self.st_in.ap().opt()], outs=[self.st_out.ap().opt()]),
              r=["st_in"], w=["st_out"])
        S.add("sp", I("dma_start", out=self.sgt[:].rearrange("p (r k) -> p r k", r=2),
                                          in_=self.st_out.ap().rearrange("(r p) k -> p r k", p=128)),
              r=["st_out"], w=["sgt"], dma="sgt")
        S.add("dve", I("tensor_scalar", out=self.h0[:], in0=self.sgt[:, 0:16], scalar1=self.C("sel", 0),
                                               scalar2=None, op0=ALU.mult), r=["sgt", "cst"], w=["h0"])
        S.add("dve", I("scalar_tensor_tensor", out=self.h0[:], in0=self.sgt[:, 16:32],
                                                      scalar=self.C("sel", 1), in1=self.h0[:], op0=ALU.mult,
                                                      op1=ALU.add), r=["sgt", "cst", "h0"], w=["h0"])
        for k in range(16):
            XC, XCk = B[1], Bk[1]
            rows = slice(k * 128, (k + 1) * 128)
            S.add("sp", I("dma_start", out=XC, in_=self.rgx[rows, :]),
                  r=[self.dk("rgx", k, i) for i in range(nt)], w=[XCk], dma=XCk)
            HO, HOk = B[7], Bk[7]
            S.add("sp", I("dma_start", out=HO, in_=self.hrg[rows, :]),
                  r=[self.dk("hrg", k, i) for i in range(nt)], w=[HOk], dma=HOk)
            Rg, Ig, A, M, H = B[2], B[3], B[4], B[5], B[6]
            self.gates(l, 1, k, XC, XCk, Rg, Bk[2], Ig, Bk[3], A, Bk[4], M, Bk[5])
            S.add("dve", I("tensor_tensor_scan",
                out=H[:, TCTX:T][:, ::-1], data0=A[:, TCTX:T][:, ::-1], data1=Ig[:, TCTX:T][:, ::-1],
                initial=self.h0[:, k:k + 1], op0=ALU.mult, op1=ALU.add), r=[Bk[4], Bk[3], "h0"], w=[Bk[6]])
            S.add("dve", I("tensor_tensor_scan",
                out=H[:, 0:TCTX][:, ::-1], data0=A[:, 0:TCTX][:, ::-1], data1=Ig[:, 0:TCTX][:, ::-1],
                initial=0.0, op0=ALU.mult, op1=ALU.add), r=[Bk[4], Bk[3]], w=[Bk[6]])
            S.add("dve", I("tensor_tensor", out=HO, in0=HO, in1=H, op=ALU.add), r=[HOk, Bk[6]], w=[HOk])
            S.add("sp", I("dma_start", out=self.hrg[rows, :], in_=HO),
                  r=[HOk], w=[self.dk("hrg", k, i) for i in range(nt)], dma=("st", HOk))

    def mix_tile(self, l, src, srcn, dst, dstn, t0, w, par, mid):
        S = self.S
        p1 = l * 240 + 168
        p2 = l * 240 + 200
        p3 = l * 240 + 232
        xin = self.xin2[par]
        if True:
            i_t = self.tile_index(t0)
            for c in range(NCH):
                slA, wkA = self.wload(self.wA[p1 + 2 * c], 4096)
                slB, wkB = self.wload(self.wA[p1 + 2 * c + 1], 4096)
                svA = slA[:, 0:4096].rearrange("p (k c) -> p k c", k=16)
                svB = slB[:, 0:4096].rearrange("p (k c) -> p k c", k=16)
                banks = []
                for (sv, wk, half) in ((svA, wkA, 0), (svA, wkA, 1), (svB, wkB, 0), (svB, wkB, 1)):
                    bk, bkk = self.bank()
                    for kc in range(16):
                        self.mm(bk[:, 0:w], sv[:, kc, half * 128:(half + 1) * 128],
                                xin[:, kc * TW:kc * TW + w], kc == 0, kc == 15, r=[wk, ("xin", par, kc)], w=[bkk])
                    banks.append((bk, bkk))
                (bg, bgk), (bb, bbk), (bc, bck), (bx, bxk) = banks
                hr = self.hrt[c % 2]
                hrk = ("hrt", c % 2)
                S.add("sp", I("dma_start", out=hr[:, 0:w],
                                                             in_=self.hrg[c * 128:(c + 1) * 128, t0:t0 + w]),
                      r=[self.dk("hrg", c, i_t)], w=[hrk], dma=hrk)
                tA, tAk = self.tmpt()
                S.add("act", I("activation", out=tA[:, 0:w], in_=bg[:, 0:w], func=AF.Square),
                      r=[bgk], w=[tAk])
                S.add("dve", I("tensor_scalar", out=tA[:, 0:w], in0=tA[:, 0:w], scalar1=0.044715,
                                                             scalar2=1.0, op0=ALU.mult, op1=ALU.add),
                      r=[tAk], w=[tAk])
                S.add("dve", I("tensor_tensor", out=tA[:, 0:w], in0=tA[:, 0:w], in1=bg[:, 0:w],
                                                                   op=ALU.mult), r=[tAk, bgk], w=[tAk])
                S.add("act", I("activation", out=tA[:, 0:w], in_=tA[:, 0:w], func=AF.Sigmoid,
                                                          scale=1.5957691216057308), r=[tAk], w=[tAk])
                S.add("dve", I("tensor_tensor", out=tA[:, 0:w], in0=tA[:, 0:w], in1=bg[:, 0:w],
                                                                   op=ALU.mult), r=[tAk, bgk], w=[tAk])
                S.add("dve", I("tensor_tensor",
                    out=self.hid[:, c * TW:c * TW + w], in0=tA[:, 0:w], in1=hr[:, 0:w], op=ALU.mult),
                    r=[tAk, hrk], w=[("hid", c)])
                tB, tBk = self.tmpt()
                tC, tCk = self.tmpt()
                S.add("act", I("activation", out=tB[:, 0:w], in_=bx[:, 0:w], func=AF.Copy),
                      r=[bxk], w=[tBk])
                S.add("dve", I("tensor_tensor", out=tB[:, 0:w], in0=bc[:, 0:w], in1=tB[:, 0:w],
                                                                   op=ALU.mult), r=[tBk, bck], w=[tBk])
                wb = (l * 16 + c) * 3
                S.add("act", I("activation", out=tC[:, 0:w], in_=tB[:, 0:w],
                                                                       func=AF.Identity,
                                                                       scale=self.C("sccw", wb + 1)),
                      r=[tBk, "cst"], w=[tCk])
                for off in (-1, 1):
                    for (a, b, s) in segs(t0, w):
                        rowlen = (b - a) if s == 1 else GRID_W
                        ov, iv = self.conv_views(tC, off, rowlen, b - a, a)
                        S.add("dve", I("scalar_tensor_tensor",
                            out=ov(tC), in0=iv(tB), scalar=self.C("sccw", wb + 1 + off), in1=ov(tC),
                            op0=ALU.mult, op1=ALU.add), r=[tBk, tCk, "cst"], w=[tCk])
                S.add("dve", I("tensor_tensor",
                    out=self.hid[:, (16 + c) * TW:(16 + c) * TW + w], in0=tC[:, 0:w], in1=bb[:, 0:w], op=ALU.mult),
                    r=[tCk, bbk], w=[("hid", 16 + c)])
                self.pop_deferred(1)
            self.flush_deferred()
            self.load_tile(src, srcn, 0, t0, w)
            for c in range(NCH):
                slA, wkA = self.wload(self.wA[p2 + 2 * c], 4096)
                slB, wkB = self.wload(self.wA[p2 + 2 * c + 1], 4096)
                svA = slA[:, 0:4096].rearrange("p (k c) -> p k c", k=16)
                svB = slB[:, 0:4096].rearrange("p (k c) -> p k c", k=16)
                yr, yrk = self.bank()
                for kc in range(16):
                    self.mm(yr[:, 0:w], svA[:, kc, 0:128], self.hid[:, kc * TW:kc * TW + w], kc == 0, kc == 15,
                            r=[wkA, ("hid", kc)], w=[yrk])
                ys, ysk = self.bank()
                for kc in range(16):
                    self.mm(ys[:, 0:w], svA[:, kc, 128:256], self.hid[:, (16 + kc) * TW:(16 + kc) * TW + w],
                            kc == 0, kc == 15, r=[wkA, ("hid", 16 + kc)], w=[ysk])
                gr, grk = self.bank()
                for kc in range(16):
                    self.mm(gr[:, 0:w], svB[:, kc, 0:128], xin[:, kc * TW:kc * TW + w], kc == 0, kc == 15,
                            r=[wkB, ("xin", par, kc)], w=[grk])
                gs, gsk = self.bank()
                for kc in range(16):
                    self.mm(gs[:, 0:w], svB[:, kc, 128:256], xin[:, kc * TW:kc * TW + w], kc == 0, kc == 15,
                            r=[wkB, ("xin", par, kc)], w=[gsk])
                tA, tAk = self.tmpt()
                tB, tBk = self.tmpt()
                b0 = self.C("bmerge", (l * 2 + 0) * 16 + c)
                b1 = self.C("bmerge", (l * 2 + 1) * 16 + c)
                S.add("act", I("activation", out=tA[:, 0:w], in_=gr[:, 0:w],
                                                                       func=AF.Sigmoid, bias=b0),
                      r=[grk, "cst"], w=[tAk])
                S.add("act", I("activation", out=tB[:, 0:w], in_=gs[:, 0:w],
                                                                       func=AF.Sigmoid, bias=b1),
                      r=[gsk, "cst"], w=[tBk])
                S.add("dve", I("tensor_tensor", out=tA[:, 0:w], in0=tA[:, 0:w], in1=yr[:, 0:w],
                                                                   op=ALU.mult), r=[tAk, yrk], w=[tAk])
                S.add("dve", I("tensor_tensor", out=tB[:, 0:w], in0=tB[:, 0:w], in1=ys[:, 0:w],
                                                                   op=ALU.mult), r=[tBk, ysk], w=[tBk])
                S.add("dve", I("tensor_tensor",
                    out=self.hid[:, (32 + c) * TW:(32 + c) * TW + w], in0=tA[:, 0:w], in1=tB[:, 0:w], op=ALU.add),
                    r=[tAk, tBk], w=[("hid", 32 + c)])
            mid()
            pending = []
            n_done = [0]
            for pn in range(8):
                slot, wk = self.wload(self.wA[p3 + pn], 4096)
                sv = slot[:, 0:4096].rearrange("p (k c) -> p k c", k=16)
                for half in range(2):
                    oc = pn * 2 + half
                    yb, ybk = self.bank()
                    for kc in range(16):
                        self.mm(yb[:, 0:w], sv[:, kc, half * 128:(half + 1) * 128],
                                self.hid[:, (32 + kc) * TW:(32 + kc) * TW + w], kc == 0, kc == 15,
                                r=[wk, ("hid", 32 + kc)], w=[ybk])
                    self.flush_stats(pending, n_done, NCH, keep=0)
                    self.resid_stats(l, 1, oc, yb, ybk, t0, w, pending)
            self.flush_stats(pending, n_done, NCH)
            self.finalize(l, 1, t0, w, dst, dstn, False)


def _panelsA(Wm):
    K, N = Wm.shape
    n = N // 256
    return np.ascontiguousarray(Wm.reshape(16, 128, n, 256).transpose(2, 1, 0, 3)).reshape(n, 128, 4096)


def _panelsB(Wm):
    return np.ascontiguousarray(Wm.reshape(44, 128, 16, 128).transpose(2, 1, 0, 3)).reshape(16, 128, 5632)


def _fm(v):
    v = np.asarray(v, np.float32)
    lead = v.shape[:-1]
    return np.moveaxis(v.reshape(*lead, 16, 128), -1, 0)


def prepare(inp, n_cores=8):
    f = lambda k: np.asarray(inp[k], np.float32)
    w_in = f("w_in")
    wA = np.empty((L * 240, 128, 4096), np.float32)
    wB = np.empty((L * 32, 128, 5632), np.float32)
    for l in range(L):
        base = l * 240
        wA[base:base + 72] = _panelsA(f("w_mod")[l])
        for which, nm in enumerate(("ffn1", "ffn2")):
            wi = f(nm + "_w_in")[l]
            g = wi[:, :DFF].reshape(D, NFF, 128)
            u = wi[:, DFF:].reshape(D, NFF, 128)
            wA[base + 72 + which * 44:base + 72 + (which + 1) * 44] = _panelsA(
                np.concatenate([g, u], axis=2).reshape(D, NFF * 256))
            wB[l * 32 + which * 16:l * 32 + (which + 1) * 16] = _panelsB(f(nm + "_w_out")[l])
        W = w_in[l]
        part = lambda i: W[:, i * 2048:(i + 1) * 2048].reshape(D, 16, 128)
        rg_x, rg_gate, sc_b, sc_c, sc_x, g_rg, g_sc = [part(i) for i in range(7)]
        wA[base + 160:base + 168] = _panelsA(W[:, 0:2048])
        p1 = np.stack([np.concatenate([rg_gate, sc_b], axis=2), np.concatenate([sc_c, sc_x], axis=2)], axis=2)
        wA[base + 168:base + 200] = _panelsA(p1.reshape(D, 32 * 256))
        wro = f("w_rg_out")[l].reshape(D, 16, 128)
        wso = f("w_sc_out")[l].reshape(D, 16, 128)
        p2 = np.stack([np.concatenate([wro, wso], axis=2), np.concatenate([g_rg, g_sc], axis=2)], axis=2)
        wA[base + 200:base + 232] = _panelsA(p2.reshape(D, 32 * 256))
        wA[base + 232:base + 240] = _panelsA(f("w_o")[l])
    x, ctx, c, c_ctx = f("x"), f("ctx"), f("c"), f("c_ctx")
    gate_w = f("rg_gate_w")
    in_maps = []
    for core in range(n_cores):
        b, half = core // 2, core % 2
        xc = ctx[b]
        xl = x[b, half * TLAT:(half + 1) * TLAT]
        if half == 1:
            xc = xc[::-1]
            xl = xl[::-1]
        xT = np.ascontiguousarray(np.concatenate([xc, xl], axis=0).T)
        cs = np.zeros((128, NCONST), np.float32)

        def put(name, arr):
            arr = np.ascontiguousarray(arr, dtype=np.float32).reshape(128, -1)
            cs[:, _off[name]:_off[name] + arr.shape[1]] = arr

        put("bmod", np.moveaxis(f("b_mod").reshape(L, 144, 128), -1, 0))
        put("lng", _fm(f("ln_g")))
        put("lnb", _fm(f("ln_b")))
        put("bmerge", _fm(f("b_merge")))
        rw = f("rg_conv_w")
        z = np.zeros_like(rw[:, :1])
        taps = np.concatenate([rw, z], axis=1) if half == 0 else np.concatenate([z, rw[:, ::-1]], axis=1)
        put("rgcw", np.moveaxis(_fm(taps), 2, 3))
        put("rgcb", _fm(f("rg_conv_b")))
        sw = f("sc_conv_w")
        if half == 1:
            sw = sw[:, ::-1]
        put("sccw", np.moveaxis(_fm(sw), 2, 3))
        dirs = [0, 1] if half == 0 else [1, 0]
        put("gb", _fm(f("rg_gate_b")[:, dirs]))
        put("lam", _fm(f("rg_lam")[:, dirs]))
        put("sel", np.tile(np.array([0.0, 1.0] if half == 0 else [1.0, 0.0], np.float32), (128, 1)))
        put("cc", np.stack([_fm(c[b]), _fm(c_ctx)], axis=-1))
        gw = gate_w[:, dirs]
        gw = np.ascontiguousarray(gw.transpose(4, 0, 1, 3, 2, 5)).reshape(128, L * 2 * 16 * 2, 128)
        in_maps.append({"xT": xT, "consts": cs, "gw": gw, "wA": wA, "wB": wB})
    return in_maps


_NC_CACHE = {}


def kernel(**inputs):
    n = 8
    in_maps = prepare(inputs, n)
    if "nc" not in _NC_CACHE:
        _NC_CACHE["nc"] = Builder(n).build()
    res = run_bass_kernel_spmd(_NC_CACHE["nc"], in_maps, core_ids=list(range(n)))
    out = np.empty((4, 4096, D), np.float32)
    for core in range(n):
        b, half = core // 2, core % 2
        o = res.results[core]["outT"].T
        if half == 1:
            o = o[::-1]
        out[b, half * TLAT:(half + 1) * TLAT] = o
    return out
```

```python
import contextlib
import numpy as np
import concourse.bass as bass
import concourse.mybir as mybir
from concourse.bass_utils import run_bass_kernel_spmd

F32 = mybir.dt.float32
BF16 = mybir.dt.bfloat16
AF = mybir.ActivationFunctionType
ALU = mybir.AluOpType

D = 2048
NCH = 16
DFF = 5632
NFF = 44
TCTX = 256
TLAT = 2048
T = TCTX + TLAT
L = 2
GRID_W = 64
TW = 512
TILES = [(0, 512), (512, 512), (1024, 512), (1536, 512), (2048, 256)]
ALPHA = (2 * L) ** 0.25
EPSP = 1e-5 / ALPHA ** 2
NSLOT = 4
SLOT_ELEMS = 5632
EPOCH = 30000

_off = {}
_cur = 0
for _name, _n in [("bmod", L * 144), ("lng", L * 3 * 16), ("lnb", L * 3 * 16), ("bmerge", L * 2 * 16),
                  ("rgcw", L * 16 * 5), ("rgcb", L * 16), ("sccw", L * 16 * 3), ("gb", L * 2 * 2 * 16),
                  ("lam", L * 2 * 16), ("sel", 2), ("cc", 32)]:
    _off[_name] = _cur
    _cur += _n
NCONST = _cur


class Op:
    __slots__ = ("eng", "fn", "deps", "dma", "dma_val", "needed", "ms")

    def __init__(self, eng, fn, deps, dma):
        self.eng, self.fn, self.deps, self.dma = eng, fn, deps, dma
        self.dma_val = 0
        self.needed = False
        self.ms = 0


class Sched:
    ENG = ("pe", "act", "dve", "pool", "sp")

    def __init__(self):
        self.ops = []
        self.lw = {}
        self.rd = {}
        self.last_on = {e: None for e in self.ENG}
        self.fence_deps = {e: [] for e in self.ENG}
        self.dma_cnt = {}

    def add(self, eng, fn, r=(), w=(), dma=None):
        idx = len(self.ops)
        deps = set(self.fence_deps[eng])
        self.fence_deps[eng] = []
        for k in r:
            x = self.lw.get(k)
            if x is not None:
                deps.add(x)
        for k in w:
            x = self.lw.get(k)
            if x is not None:
                deps.add(x)
            for x in self.rd.get(k, {}).values():
                deps.add(x)
        for k in r:
            self.rd.setdefault(k, {})[eng if dma is None else ("dma", idx)] = idx
        for k in w:
            self.lw[k] = idx
            self.rd[k] = {}
        deps.discard(idx)
        op = Op(eng, fn, sorted(deps), dma)
        if dma is not None:
            self.dma_cnt[dma] = self.dma_cnt.get(dma, 0) + 16
            op.dma_val = self.dma_cnt[dma]
        for d in deps:
            self.ops[d].needed = True
        self.ops.append(op)
        if dma is None:
            self.last_on[eng] = idx
        return idx

    def fence(self):
        f = [v for v in self.last_on.values() if v is not None]
        for e in self.ENG:
            self.fence_deps[e] = list(f)

    def emit(self, nc, stack):
        cnt = {e: 0 for e in self.ENG}
        for op in self.ops:
            if op.dma is None and op.needed:
                cnt[op.eng] += 1
                op.ms = cnt[op.eng]
        esem = {e: [stack.enter_context(nc.semaphore(f"s_{e}_{i}")) for i in range(cnt[e] // EPOCH + 1)]
                for e in self.ENG}
        dsem = {}
        for i, ch in enumerate(sorted(self.dma_cnt, key=str)):
            assert self.dma_cnt[ch] < 60000, (ch, self.dma_cnt[ch])
            dsem[ch] = stack.enter_context(nc.semaphore(f"d_{i}"))
        ops = self.ops
        by_eng = {e: [op for op in ops if op.eng == e] for e in self.ENG}

        def run(e, eo):
            waited = {}
            for op in by_eng[e]:
                for d in op.deps:
                    p = ops[d]
                    if p.dma is not None:
                        sid, sem, val = ("d", p.dma), dsem[p.dma], p.dma_val
                    else:
                        if p.eng == "pe" and e == "pe":
                            continue
                        ep = (p.ms - 1) // EPOCH
                        sid, sem, val = (p.eng, ep), esem[p.eng][ep], (p.ms - 1) % EPOCH + 1
                    if waited.get(sid, 0) >= val:
                        continue
                    eo.wait_ge(sem, val)
                    waited[sid] = val
                if op.fn is None:
                    continue
                m_, a_, k_ = op.fn
                ins = getattr(eo, m_)(*a_, **k_)
                if op.dma is not None:
                    ins.then_inc(dsem[op.dma], 16)
                elif op.needed:
                    ins.then_inc(esem[e][(op.ms - 1) // EPOCH], 1)

        block = stack.enter_context(nc.Block())

        @block.tensor
        def _(eo):
            run("pe", eo)

        @block.scalar
        def _(eo):
            run("act", eo)

        @block.vector
        def _(eo):
            run("dve", eo)

        @block.gpsimd
        def _(eo):
            run("pool", eo)

        @block.sync
        def _(eo):
            run("sp", eo)


def I(method, *args, **kw):
    return (method, args, kw)


def segs(t0, w):
    out = []
    if t0 < TCTX:
        out.append((0, min(w, TCTX - t0), 1))
    if t0 + w > TCTX:
        out.append((max(0, TCTX - t0), w, 0))
    return out


class Builder:
    def __init__(self, n_cores, stop_after=None, dbg=None):
        self.n_cores = n_cores
        self.stop_after = stop_after
        self.dbg = dbg

    def build(self):
        nc = bass.Bass("TRN2", target_bir_lowering=False)
        self.nc = nc
        S = Sched()
        self.S = S
        with contextlib.ExitStack() as st:
            self.alloc(nc, st)
            self.program()
            S.emit(nc, st)
        return nc

    def alloc(self, nc, st):
        def dram_in(name, shape):
            return nc.dram_tensor(name, shape, F32, kind="ExternalInput").ap()

        self.xT = dram_in("xT", [D, T])
        self.consts = dram_in("consts", [128, NCONST])
        self.gw = dram_in("gw", [128, L * 2 * 16 * 2, 128])
        self.wA = dram_in("wA", [L * 240, 128, 4096])
        self.wB = dram_in("wB", [L * 32, 128, 5632])
        self.outT = nc.dram_tensor("outT", [D, TLAT], F32, kind="ExternalOutput").ap()
        if self.dbg:
            self.dbg_out = nc.dram_tensor("dbg", [D, T], F32, kind="ExternalOutput").ap()
        self.xa = nc.dram_tensor("xa", [D, T], F32).ap()
        self.xb = nc.dram_tensor("xb", [D, T], F32).ap()
        self.rgx = nc.dram_tensor("rgx", [D, T], F32).ap()
        self.hrg = nc.dram_tensor("hrg", [D, T], F32).ap()
        self.st_in = nc.dram_tensor("st_in", [128, 16], F32)
        self.st_out = nc.dram_tensor("st_out", [256, 16], F32)

        def sb(name, shape, dt=F32):
            return st.enter_context(nc.sbuf_tensor(name, shape, dt))

        self.cst = sb("cst", [128, NCONST])
        self.modT = sb("modT", [128, L * 288])
        self.sc1p = sb("sc1p", [128, L * 96])
        self.gsc = sb("gsc", [128, L * 96])
        self.c8 = sb("c8", [128, L * 32])
        self.c16 = sb("c16", [128, L * 32])
        self.scb = sb("scb", [128, 32], BF16)
        self.ones = sb("ones", [128, 128])
        self.states = sb("states", [128, 16])
        self.sgt = sb("sgt", [128, 32])
        self.h0 = sb("h0", [128, 16])
        self.gwt = sb("gwt", [128, 2, 128])
        self.wslot = [sb(f"wslot{i}", [128, SLOT_ELEMS], BF16) for i in range(NSLOT)]
        self.xs = sb("xs", [128, NCH * TW])
        self.xin2 = [sb("xin0", [128, NCH * TW], BF16), sb("xin1", [128, NCH * TW], BF16)]
        self.px = [sb(f"px{i}", [128, TW]) for i in range(2)]
        self.deferred = []
        self.hid = sb("hid", [128, 48 * TW], BF16)
        self.tmp = [sb(f"tmp{i}", [128, TW]) for i in range(8)]
        self.hrt = [sb(f"hrt{i}", [128, TW]) for i in range(2)]
        self.stg = [sb(f"stg{i}", [128, TW]) for i in range(2)]
        self.mean = sb("mean", [128, TW])
        self.rstd = sb("rstd", [128, TW])
        self.ps = [st.enter_context(nc.psum_tensor(f"ps{i}", [128, 512], F32)) for i in range(8)]
        hid32 = self.hid.bitcast(F32)
        self.sbuf_scan = [hid32[:, i * T:(i + 1) * T] for i in range(5)] + \
                         [self.xs[:, i * T:(i + 1) * T] for i in range(3)] + \
                         [self.xin2[0].bitcast(F32)[:, 0:T], self.xin2[1].bitcast(F32)[:, 0:T]]
        self.ws_i = 0
        self.pp_i = 0
        self.tp_i = 0
        self.stg_i = 0

    def C(self, name, idx, n=1):
        o = _off[name] + idx
        return self.cst[:, o:o + n]

    def wload(self, dram_panel, nelem):
        i = self.ws_i % NSLOT
        self.ws_i += 1
        slot = self.wslot[i]
        self.S.add("pool", I("dma_start", out=slot[:, 0:nelem], in_=dram_panel, max_dma_last_dim=8192),
                   w=[("w", i)], dma=("w", i))
        return slot, ("w", i)

    def bank(self):
        i = self.pp_i % 6
        self.pp_i += 1
        return self.ps[i], ("ps", i)

    def tmpt(self):
        i = self.tp_i % 8
        self.tp_i += 1
        return self.tmp[i], ("tmp", i)

    def stage(self):
        i = self.stg_i % 2
        self.stg_i += 1
        return self.stg[i], ("stg", i)

    def mm(self, out, lhsT, rhs, start, stop, r, w):
        self.S.add("pe", I("matmul", out, lhsT, rhs, start=start, stop=stop), r=r, w=w)

    def mod(self, l, j, c, s):
        o = (l * 144 + j * 16 + c) * 2 + s
        return self.modT[:, o:o + 1]

    def dk(self, name, c, i):
        return ("dram", name, c, i)

    def program(self):
        S = self.S
        nc = self.nc
        S.add("sp", I("dma_start", out=self.cst[:], in_=self.consts), w=["cst"], dma="cst")
        S.add("dve", I("memset", self.ones[:], 1.0 / D), w=["ones"])
        self.setup_consts()
        for pn in range(40):
            self.mods_panel(0, pn)
        self.mods_evac(0, 0, 5)
        self.bg = [lambda pn=pn: self.mods_panel(0, pn) for pn in range(40, 72)]
        self.bg.append(lambda: self.mods_evac(0, 5, 9))
        for l in range(1, L):
            self.bg += [lambda l=l, pn=pn: self.mods_panel(l, pn) for pn in range(72)]
            self.bg.append(lambda l=l: self.mods_evac(l, 0, 9))
        if self.stop_after == "mods":
            return self.finish_dbg_small()
        cur = (self.xT, "xT")
        A = (self.xa, "xa")
        Bb = (self.xb, "xb")
        seq = []

        def tile_items(kind, l, i, src, dst, tiles, final=False):
            for (t0, w) in tiles:
                if kind == "ffn":
                    body = (lambda par, mid, l=l, i=i, src=src, dst=dst, t0=t0, w=w, final=final:
                            self.ffn_tile(l, i, src[0], src[1], dst[0], dst[1], t0, w, final, par, mid))
                elif kind == "rgx":
                    body = (lambda par, mid, l=l, t0=t0, w=w: self.rgx_tile(l, t0, w, par, mid))
                else:
                    body = (lambda par, mid, l=l, src=src, dst=dst, t0=t0, w=w:
                            self.mix_tile(l, src[0], src[1], dst[0], dst[1], t0, w, par, mid))
                prep = (lambda par, l=l, i=i, src=src, t0=t0, w=w: self.prep(l, i, src[0], src[1], t0, w, par))
                seq.append(("tile", prep, body))

        for l in range(L):
            last = l == L - 1
            tile_items("ffn", l, 0, cur, A, TILES)
            seq.append(("mark", f"ffn1_{l}", A))
            tile_items("rgx", l, 1, A, None, TILES)
            seq.append(("mark", f"rgx_{l}", (self.rgx, "rgx")))
            seq.append(("barrier", lambda l=l: self.scan(l)))
            seq.append(("mark", f"scan_{l}", (self.hrg, "hrg")))
            tiles = TILES if not last else [(256, 256)] + TILES[1:]
            tile_items("mix", l, 1, A, Bb, tiles)
            seq.append(("mark", f"mix_{l}", Bb))
            if last:
                tile_items("ffn", l, 2, Bb, (self.outT, "out"), tiles, final=True)
            else:
                tile_items("ffn", l, 2, Bb, A, tiles)
                seq.append(("mark", f"ffn2_{l}", A))
                cur = A
                A, Bb = Bb, A
        prepped = -1
        for idx, it in enumerate(seq):
            if it[0] == "mark":
                if self.stop_after == it[1]:
                    self.flush_deferred()
                    return self.finish_dbg(*it[2])
                continue
            if it[0] == "barrier":
                self.flush_deferred()
                S.fence()
                it[1]()
                self.run_bg(len(self.bg))
                S.fence()
                continue
            par = idx % 2
            if prepped != idx:
                it[1](par)
            nxt = None
            for k in range(idx + 1, len(seq)):
                if seq[k][0] == "mark":
                    continue
                if seq[k][0] == "tile":
                    nxt = k
                break
            if nxt is not None and self.stop_after is not None:
                for k in range(idx + 1, nxt):
                    if seq[k][0] == "mark" and seq[k][1] == self.stop_after:
                        nxt = None
                        break

            def mid(nxt=nxt):
                if nxt is not None:
                    seq[nxt][1](nxt % 2)
            it[2](par, mid)
            if nxt is not None:
                prepped = nxt
        self.flush_deferred()
        S.add("sp", None, r=[self.dk("out", c, i) for c in range(NCH) for i in range(len(TILES))])

    def finish_dbg(self, src_ap, name):
        S = self.S
        S.fence()
        for c in range(NCH):
            buf = self.sbuf_scan[c % 2]
            key = ("scanbuf", c % 2)
            S.add("sp", I("dma_start", out=buf, in_=src_ap[c * 128:(c + 1) * 128, :]),
                  r=[self.dk(name, c, i) for i in range(len(TILES))], w=[key], dma=("dbgl", c % 2))
            S.add("sp", I("dma_start", out=self.dbg_out[c * 128:(c + 1) * 128, :], in_=buf),
                  r=[key], w=[("dbgout", c)], dma=("dbgs", c % 2))
        S.add("sp", None, r=[("dbgout", c) for c in range(NCH)])

    def finish_dbg_small(self):
        S = self.S
        S.add("sp", I("dma_start", out=self.dbg_out[0:128, 0:L * 288], in_=self.modT[:]),
              r=["modT0", "modT1"], w=["dbgo"], dma="dbgs")
        S.add("sp", I("dma_start", out=self.dbg_out[128:256, 0:L * 32], in_=self.c8[:]),
              r=["c8"], w=["dbgo2"], dma="dbgs2")
        S.add("sp", None, r=["dbgo", "dbgo2"])

    def setup_consts(self):
        S = self.S
        lam = self.C("lam", 0, L * 32)
        t0 = self.tmp[0]
        S.add("act", I("activation", out=t0[:, 0:L * 32], in_=lam, func=AF.Exp, scale=-1.0),
              r=["cst"], w=[("tmp", 0)])
        S.add("act", I("activation", out=t0[:, 0:L * 32], in_=t0[:, 0:L * 32], func=AF.Ln, bias=1.0),
              r=[("tmp", 0)], w=[("tmp", 0)])
        S.add("dve", I("tensor_scalar", out=self.c8[:], in0=t0[:, 0:L * 32], scalar1=-8.0, scalar2=None,
                                               op0=ALU.mult), r=[("tmp", 0)], w=["c8"])
        S.add("dve", I("tensor_scalar", out=self.c16[:], in0=t0[:, 0:L * 32], scalar1=-16.0, scalar2=None,
                                               op0=ALU.mult), r=[("tmp", 0)], w=["c16"])
        cc = self.C("cc", 0, 32)
        S.add("act", I("activation", out=self.scb[:], in_=cc, func=AF.Silu), r=["cst"], w=["scb"])

    def mods_panel(self, l, pn):
        psb = self.ps[6 + l]
        slot, wk = self.wload(self.wA[l * 240 + pn], 4096)
        sv = slot[:, 0:4096].rearrange("p (k c) -> p k c", k=16)
        for half in range(2):
            m = pn * 2 + half
            for kc in range(16):
                self.mm(psb[:, 2 * m:2 * m + 2], sv[:, kc, half * 128:(half + 1) * 128],
                        self.scb[:, 2 * kc:2 * kc + 2], kc == 0, kc == 15,
                        r=[wk, "scb"], w=[("ps", 6 + l)])

    def mods_evac(self, l, j_lo, j_hi):
        S = self.S
        psb = self.ps[6 + l]
        m0, m1 = j_lo * 16, j_hi * 16
        mt = self.modT[:, l * 288:(l + 1) * 288]
        bm = self.C("bmod", l * 144 + m0, m1 - m0)
        for s in range(2):
            S.add("dve", I("tensor_tensor", out=mt[:, 2 * m0 + s:2 * m1:2], in0=psb[:, 2 * m0 + s:2 * m1:2], in1=bm,
                           op=ALU.add), r=[("ps", 6 + l), "cst"], w=[f"modT{l}"])
        coef = [0.5 / ALPHA, 1.0 / ALPHA, 0.5 / ALPHA]
        for i in range(3):
            if j_lo <= 3 * i + 1 < j_hi:
                src1 = self.modT[:, (l * 144 + (3 * i + 1) * 16) * 2:(l * 144 + (3 * i + 2) * 16) * 2]
                dst1 = self.sc1p[:, l * 96 + i * 32:l * 96 + (i + 1) * 32]
                S.add("dve", I("tensor_scalar", out=dst1, in0=src1, scalar1=1.0, scalar2=None, op0=ALU.add),
                      r=[f"modT{l}"], w=[f"sc1p{l}"])
            if j_lo <= 3 * i + 2 < j_hi:
                src2 = self.modT[:, (l * 144 + (3 * i + 2) * 16) * 2:(l * 144 + (3 * i + 3) * 16) * 2]
                dst2 = self.gsc[:, l * 96 + i * 32:l * 96 + (i + 1) * 32]
                S.add("dve", I("tensor_scalar", out=dst2, in0=src2, scalar1=coef[i], scalar2=None, op0=ALU.mult),
                      r=[f"modT{l}"], w=[f"gsc{l}"])

    def run_bg(self, n):
        for _ in range(n):
            if self.bg:
                self.bg.pop(0)()

    def SC1P(self, l, i, c, s):
        o = l * 96 + i * 32 + c * 2 + s
        return self.sc1p[:, o:o + 1]

    def GSC(self, l, i, c, s):
        o = l * 96 + i * 32 + c * 2 + s
        return self.gsc[:, o:o + 1]

    def load_tile(self, src, srcn, ti, t0, w):
        S = self.S
        i_t = self.tile_index(t0)
        srcv = src.rearrange("(c p) t -> p c t", p=128)[:, :, t0:t0 + w]
        dstv = self.xs.rearrange("p (c t) -> p c t", c=NCH)[:, :, 0:w]
        S.add("sp", I("dma_start", out=dstv, in_=srcv),
              r=[self.dk(srcn, c, i_t) for c in range(NCH)], w=[("xs", c) for c in range(NCH)], dma="xs")

    def tile_index(self, t0):
        return t0 // TW

    def prep(self, l, i, src, srcn, t0, w, par):
        S = self.S
        i_t = self.tile_index(t0)
        xin = self.xin2[par]
        for c in range(NCH):
            px = self.px[c % 2]
            pxk = ("px", c % 2)
            S.add("sp", I("dma_start", out=px[:, 0:w], in_=src[c * 128:(c + 1) * 128, t0:t0 + w]),
                  r=[self.dk(srcn, c, i_t)], w=[pxk], dma=pxk)
            for (a, b, s) in segs(t0, w):
                S.add("act", I("activation", out=xin[:, c * TW + a:c * TW + b], in_=px[:, a:b],
                               func=AF.Identity, scale=self.SC1P(l, i, c, s), bias=self.mod(l, 3 * i, c, s)),
                      r=[pxk, f"modT{l}", f"sc1p{l}"], w=[("xin", par, c)])

    def pop_deferred(self, n=1):
        for _ in range(n):
            if self.deferred:
                self.deferred.pop(0)()

    def flush_deferred(self):
        while self.deferred:
            self.deferred.pop(0)()

    def resid_stats(self, l, i, oc, yb, ybk, t0, w, pending):
        S = self.S
        xsl = self.xs[:, oc * TW:oc * TW + w]
        for (a, b, s) in segs(t0, w):
            S.add("dve", I("scalar_tensor_tensor",
                out=self.xs[:, oc * TW + a:oc * TW + b], in0=yb[:, a:b], scalar=self.GSC(l, i, oc, s),
                in1=self.xs[:, oc * TW + a:oc * TW + b], op0=ALU.mult, op1=ALU.add),
                r=[ybk, ("xs", oc), f"gsc{l}"], w=[("xs", oc)])
        sq, sqk = self.tmpt()
        S.add("act", I("activation", out=sq[:, 0:w], in_=xsl, func=AF.Square), r=[("xs", oc)], w=[sqk])

        def stats(first, lastf):
            self.mm(self.ps[6][:, 0:w], self.ones[:], xsl, first, lastf, r=[("xs", oc), "ones"], w=[("ps", 6)])
            self.mm(self.ps[7][:, 0:w], self.ones[:], sq[:, 0:w], first, lastf, r=[sqk, "ones"], w=[("ps", 7)])
        pending.append(stats)

    def flush_stats(self, pending, n_done, total, keep=0):
        while len(pending) > keep:
            f = pending.pop(0)
            f(n_done[0] == 0, n_done[0] == total - 1)
            n_done[0] += 1

    def finalize(self, l, i, t0, w, dst, dstn, final):
        S = self.S
        i_t = self.tile_index(t0)
        mean, rstd = self.mean, self.rstd
        msq, msqk = self.tmpt()
        S.add("act", I("activation", out=mean[:, 0:w], in_=self.ps[6][:, 0:w], func=AF.Copy),
              r=[("ps", 6)], w=["mean"])
        S.add("act", I("activation", out=msq[:, 0:w], in_=self.ps[6][:, 0:w], func=AF.Square),
              r=[("ps", 6)], w=[msqk])
        S.add("dve", I("tensor_tensor", out=msq[:, 0:w], in0=self.ps[7][:, 0:w], in1=msq[:, 0:w],
                       op=ALU.subtract), r=[("ps", 7), msqk], w=[msqk])
        S.add("dve", I("tensor_scalar", out=msq[:, 0:w], in0=msq[:, 0:w], scalar1=0.0, scalar2=EPSP,
                       op0=ALU.max, op1=ALU.add), r=[msqk], w=[msqk])
        S.add("act", I("activation", out=msq[:, 0:w], in_=msq[:, 0:w], func=AF.Sqrt), r=[msqk], w=[msqk])
        S.add("dve", I("reciprocal", out=rstd[:, 0:w], in_=msq[:, 0:w]), r=[msqk], w=["rstd"])

        def chunk(c):
            t1, t1k = self.tmpt()
            xsl = self.xs[:, c * TW:c * TW + w]
            S.add("dve", I("tensor_tensor", out=t1[:, 0:w], in0=xsl, in1=mean[:, 0:w], op=ALU.subtract),
                  r=[("xs", c), "mean"], w=[t1k])
            S.add("dve", I("tensor_tensor", out=t1[:, 0:w], in0=t1[:, 0:w], in1=rstd[:, 0:w], op=ALU.mult),
                  r=[t1k, "rstd"], w=[t1k])
            sg, sgk = self.stage()
            g = self.C("lng", (l * 3 + i) * 16 + c)
            bb = self.C("lnb", (l * 3 + i) * 16 + c)
            S.add("act", I("activation", out=sg[:, 0:w], in_=t1[:, 0:w], func=AF.Identity, scale=g, bias=bb),
                  r=[t1k, "cst"], w=[sgk])
            if final:
                dsl = dst[c * 128:(c + 1) * 128, t0 - TCTX:t0 - TCTX + w]
            else:
                dsl = dst[c * 128:(c + 1) * 128, t0:t0 + w]
            S.add("sp", I("dma_start", out=dsl, in_=sg[:, 0:w]), r=[sgk], w=[self.dk(dstn, c, i_t)], dma=sgk)
        for c in range(NCH):
            self.deferred.append(lambda c=c: chunk(c))

    def ffn_tile(self, l, i, src, srcn, dst, dstn, t0, w, final, par, mid):
        S = self.S
        which = 0 if i == 0 else 1
        pa = l * 240 + 72 + which * 44
        pb = l * 32 + which * 16
        xin = self.xin2[par]
        loaded = False
        for j in range(NFF):
            slot, wk = self.wload(self.wA[pa + j], 4096)
            sv = slot[:, 0:4096].rearrange("p (k c) -> p k c", k=16)
            gb, gbk = self.bank()
            ub, ubk = self.bank()
            for kc in range(16):
                self.mm(gb[:, 0:w], sv[:, kc, 0:128], xin[:, kc * TW:kc * TW + w], kc == 0, kc == 15,
                        r=[wk, ("xin", par, kc)], w=[gbk])
            for kc in range(16):
                self.mm(ub[:, 0:w], sv[:, kc, 128:256], xin[:, kc * TW:kc * TW + w], kc == 0, kc == 15,
                        r=[wk, ("xin", par, kc)], w=[ubk])
            sg, sgk = self.tmpt()
            S.add("act", I("activation", out=sg[:, 0:w], in_=gb[:, 0:w], func=AF.Silu), r=[gbk], w=[sgk])
            S.add("dve", I("tensor_tensor", out=self.hid[:, j * TW:j * TW + w], in0=sg[:, 0:w], in1=ub[:, 0:w],
                           op=ALU.mult), r=[sgk, ubk], w=[("hid", j)])
            self.pop_deferred(1)
            if not self.deferred and not loaded:
                self.load_tile(src, srcn, 0, t0, w)
                loaded = True
        mid()
        pending = []
        n_done = [0]
        for oc in range(NCH):
            slot, wk = self.wload(self.wB[pb + oc], 5632)
            sv = slot[:, 0:5632].rearrange("p (k c) -> p k c", k=NFF)
            yb, ybk = self.bank()
            for kc in range(NFF):
                self.mm(yb[:, 0:w], sv[:, kc, :], self.hid[:, kc * TW:kc * TW + w], kc == 0, kc == NFF - 1,
                        r=[wk, ("hid", kc)], w=[ybk])
            self.flush_stats(pending, n_done, NCH, keep=0)
            self.resid_stats(l, i, oc, yb, ybk, t0, w, pending)
        self.flush_stats(pending, n_done, NCH)
        self.finalize(l, i, t0, w, dst, dstn, final)

    def rgx_tile(self, l, t0, w, par, mid):
        S = self.S
        pa = l * 240 + 160
        i_t = self.tile_index(t0)
        xin = self.xin2[par]
        for pn in range(8):
            slot, wk = self.wload(self.wA[pa + pn], 4096)
            sv = slot[:, 0:4096].rearrange("p (k c) -> p k c", k=16)
            for half in range(2):
                oc = pn * 2 + half
                yb, ybk = self.bank()
                for kc in range(16):
                    self.mm(yb[:, 0:w], sv[:, kc, half * 128:(half + 1) * 128],
                            xin[:, kc * TW:kc * TW + w], kc == 0, kc == 15, r=[wk, ("xin", par, kc)], w=[ybk])
                self.pop_deferred(1)
                sg, sgk = self.stage()
                S.add("act", I("activation", out=sg[:, 0:w], in_=yb[:, 0:w], func=AF.Copy), r=[ybk], w=[sgk])
                dsl = self.rgx[oc * 128:(oc + 1) * 128, t0:t0 + w]
                S.add("sp", I("dma_start", out=dsl, in_=sg[:, 0:w]), r=[sgk], w=[self.dk("rgx", oc, i_t)], dma=sgk)
            if pn == 3:
                mid()

    def conv_views(self, ap, off, rowlen, ncols, c0):
        v = ap[:, c0:c0 + ncols]
        if rowlen != ncols:
            v = v.rearrange("p (r g) -> p r g", g=rowlen)
            lo, hi = max(0, -off), rowlen - max(0, off)
            return (lambda x: x[:, c0:c0 + ncols].rearrange("p (r g) -> p r g", g=rowlen)[:, :, lo:hi],
                    lambda x: x[:, c0:c0 + ncols].rearrange("p (r g) -> p r g", g=rowlen)[:, :, lo + off:hi + off])
        lo, hi = max(0, -off), rowlen - max(0, off)
        return (lambda x: x[:, c0 + lo:c0 + hi], lambda x: x[:, c0 + lo + off:c0 + hi + off])

    def gates(self, l, which, k, XC, XCk, Rg, Rgk, Ig, Igk, A, Ak, M, Mk):
        S = self.S
        gi = ((l * 2 + which) * 16 + k) * 2
        S.add("sp", I("dma_start", out=self.gwt[:], in_=self.gw[:, gi:gi + 2, :]), w=["gwt"], dma="gwt")
        for (t0, w) in TILES:
            rb, rbk = self.bank()
            ib, ibk = self.bank()
            self.mm(rb[:, 0:w], self.gwt[:, 0, :], XC[:, t0:t0 + w], True, True, r=["gwt", XCk], w=[rbk])
            self.mm(ib[:, 0:w], self.gwt[:, 1, :], XC[:, t0:t0 + w], True, True, r=["gwt", XCk], w=[ibk])
            br = self.C("gb", ((l * 2 + which) * 2 + 0) * 16 + k)
            bi = self.C("gb", ((l * 2 + which) * 2 + 1) * 16 + k)
            S.add("act", I("activation", out=Rg[:, t0:t0 + w], in_=rb[:, 0:w], func=AF.Sigmoid, bias=br),
                  r=[rbk, "cst"], w=[Rgk])
            S.add("act", I("activation", out=Ig[:, t0:t0 + w], in_=ib[:, 0:w], func=AF.Sigmoid, bias=bi),
                  r=[ibk, "cst"], w=[Igk])
        co = (l * 2 + which) * 16 + k
        S.add("act", I("activation", out=A, in_=Rg, func=AF.Exp, scale=self.c8[:, co:co + 1]),
              r=[Rgk, "c8"], w=[Ak])
        S.add("act", I("activation", out=M, in_=Rg, func=AF.Exp, scale=self.c16[:, co:co + 1]),
              r=[Rgk, "c16"], w=[Mk])
        S.add("act", I("activation", out=M, in_=M, func=AF.Relu, scale=-1.0, bias=1.0), r=[Mk], w=[Mk])
        S.add("act", I("activation", out=M, in_=M, func=AF.Sqrt), r=[Mk], w=[Mk])

    def scan(self, l):
        S = self.S
        B = self.sbuf_scan
        nt = len(TILES)

        def bufs(k):
            p = k % 2
            return [(B[5 * p + i], ("scanbuf", 5 * p + i)) for i in range(5)]

        def rows_of(k):
            return slice(k * 128, (k + 1) * 128)

        def vmul(XC, XCk, Ig, Igk, M, Mk):
            S.add("dve", I("tensor_tensor", out=Ig, in0=Ig, in1=XC, op=ALU.mult), r=[Igk, XCk], w=[Igk])
            S.add("dve", I("tensor_tensor", out=Ig, in0=Ig, in1=M, op=ALU.mult), r=[Igk, Mk], w=[Igk])

        def A1(k):
            (U, Uk), (XC, XCk), (Ig, Igk), (A, Ak), (M, Mk) = bufs(k)
            rows = rows_of(k)
            S.add("sp", I("dma_start", out=U, in_=self.rgx[rows, :]),
                  r=[self.dk("rgx", k, i) for i in range(nt)], w=[Uk], dma=Uk)
            wb = (l * 16 + k) * 5
            bias = self.C("rgcb", l * 16 + k)
            S.add("act", I("activation", out=XC, in_=U, func=AF.Identity, scale=self.C("rgcw", wb + 2), bias=bias),
                  r=[Uk, "cst"], w=[XCk])
            for off in (-2, -1, 1, 2):
                for (c0, ncols, rowlen) in ((0, TCTX, TCTX), (TCTX, TLAT, GRID_W)):
                    ov, iv = self.conv_views(XC, off, rowlen, ncols, c0)
                    S.add("dve", I("scalar_tensor_tensor", out=ov(XC), in0=iv(U), scalar=self.C("rgcw", wb + 2 + off),
                                   in1=ov(XC), op0=ALU.mult, op1=ALU.add), r=[Uk, XCk, "cst"], w=[XCk])
            S.add("sp", I("dma_start", out=self.rgx[rows, :], in_=XC),
                  r=[XCk], w=[self.dk("rgx", k, i) for i in range(nt)], dma=("st", XCk))
            self.gates(l, 0, k, XC, XCk, U, Uk, Ig, Igk, A, Ak, M, Mk)

        def B1(k):
            (U, Uk), (XC, XCk), (Ig, Igk), (A, Ak), (M, Mk) = bufs(k)
            rows = rows_of(k)
            vmul(XC, XCk, Ig, Igk, M, Mk)
            H, Hk = M, Mk
            S.add("dve", I("tensor_tensor_scan", out=H, data0=A, data1=Ig, initial=0.0, op0=ALU.mult, op1=ALU.add),
                  r=[Ak, Igk], w=[Hk])
            S.add("dve", I("tensor_copy", out=self.states[:, k:k + 1], in_=H[:, T - 1:T]), r=[Hk], w=["states"])
            S.add("sp", I("dma_start", out=self.hrg[rows, :], in_=H),
                  r=[Hk], w=[self.dk("hrg", k, i) for i in range(nt)], dma=("st", Hk))

        def A2(k):
            (HO, HOk), (XC, XCk), (Ig, Igk), (A, Ak), (M, Mk) = bufs(k)
            rows = rows_of(k)
            S.add("sp", I("dma_start", out=XC, in_=self.rgx[rows, :]),
                  r=[self.dk("rgx", k, i) for i in range(nt)], w=[XCk], dma=XCk)
            self.gates(l, 1, k, XC, XCk, HO, HOk, Ig, Igk, A, Ak, M, Mk)
            S.add("sp", I("dma_start", out=HO, in_=self.hrg[rows, :]),
                  r=[self.dk("hrg", k, i) for i in range(nt)], w=[HOk], dma=HOk)

        def B2(k):
            (HO, HOk), (XC, XCk), (Ig, Igk), (A, Ak), (M, Mk) = bufs(k)
            rows = rows_of(k)
            vmul(XC, XCk, Ig, Igk, M, Mk)
            H, Hk = M, Mk
            S.add("dve", I("tensor_tensor_scan", out=H[:, TCTX:T][:, ::-1], data0=A[:, TCTX:T][:, ::-1],
                           data1=Ig[:, TCTX:T][:, ::-1], initial=self.h0[:, k:k + 1], op0=ALU.mult, op1=ALU.add),
                  r=[Ak, Igk, "h0"], w=[Hk])
            S.add("dve", I("tensor_tensor_scan", out=H[:, 0:TCTX][:, ::-1], data0=A[:, 0:TCTX][:, ::-1],
                           data1=Ig[:, 0:TCTX][:, ::-1], initial=0.0, op0=ALU.mult, op1=ALU.add),
                  r=[Ak, Igk], w=[Hk])
            S.add("dve", I("tensor_tensor", out=HO, in0=HO, in1=H, op=ALU.add), r=[HOk, Hk], w=[HOk])
            S.add("sp", I("dma_start", out=self.hrg[rows, :], in_=HO),
                  r=[HOk], w=[self.dk("hrg", k, i) for i in range(nt)], dma=("st", HOk))

        nbg = (len(self.bg) + 31) // 32
        A1(0)
        for k in range(16):
            if k + 1 < 16:
                A1(k + 1)
            B1(k)
            self.run_bg(nbg)
        A2(0)
        S.add("sp", I("dma_start", out=self.st_in[:, :], in_=self.states[:]), r=["states"], w=["st_in"],
              dma="st_in")
        groups = [[2 * g, 2 * g + 1] for g in range(self.n_cores // 2)]
        S.add("pool", I("collective_compute", "AllGather", ALU.bypass, replica_groups=groups,
                        ins=[self.st_in.ap().opt()], outs=[self.st_out.ap().opt()]),
              r=["st_in"], w=["st_out"])
        S.add("sp", I("dma_start", out=self.sgt[:].rearrange("p (r k) -> p r k", r=2),
                      in_=self.st_out.ap().rearrange("(r p) k -> p r k", p=128)),
              r=["st_out"], w=["sgt"], dma="sgt")
        S.add("dve", I("tensor_scalar", out=self.h0[:], in0=self.sgt[:, 0:16], scalar1=self.C("sel", 0),
                       scalar2=None, op0=ALU.mult), r=["sgt", "cst"], w=["h0"])
        S.add("dve", I("scalar_tensor_tensor", out=self.h0[:], in0=self.sgt[:, 16:32], scalar=self.C("sel", 1),
                       in1=self.h0[:], op0=ALU.mult, op1=ALU.add), r=["sgt", "cst", "h0"], w=["h0"])
        for k in range(16):
            if k + 1 < 16:
                A2(k + 1)
            B2(k)
            self.run_bg(nbg)

    def mix_tile(self, l, src, srcn, dst, dstn, t0, w, par, mid):
        S = self.S
        p1 = l * 240 + 168
        p2 = l * 240 + 200
        p3 = l * 240 + 232
        xin = self.xin2[par]
        if True:
            i_t = self.tile_index(t0)
            for c in range(NCH):
                slA, wkA = self.wload(self.wA[p1 + 2 * c], 4096)
                slB, wkB = self.wload(self.wA[p1 + 2 * c + 1], 4096)
                svA = slA[:, 0:4096].rearrange("p (k c) -> p k c", k=16)
                svB = slB[:, 0:4096].rearrange("p (k c) -> p k c", k=16)
                banks = []
                for (sv, wk, half) in ((svA, wkA, 0), (svA, wkA, 1), (svB, wkB, 0), (svB, wkB, 1)):
                    bk, bkk = self.bank()
                    for kc in range(16):
                        self.mm(bk[:, 0:w], sv[:, kc, half * 128:(half + 1) * 128],
                                xin[:, kc * TW:kc * TW + w], kc == 0, kc == 15, r=[wk, ("xin", par, kc)], w=[bkk])
                    banks.append((bk, bkk))
                (bg, bgk), (bb, bbk), (bc, bck), (bx, bxk) = banks
                hr = self.hrt[c % 2]
                hrk = ("hrt", c % 2)
                S.add("sp", I("dma_start", out=hr[:, 0:w],
                                                             in_=self.hrg[c * 128:(c + 1) * 128, t0:t0 + w]),
                      r=[self.dk("hrg", c, i_t)], w=[hrk], dma=hrk)
                tA, tAk = self.tmpt()
                S.add("act", I("activation", out=tA[:, 0:w], in_=bg[:, 0:w], func=AF.Square),
                      r=[bgk], w=[tAk])
                S.add("dve", I("tensor_scalar", out=tA[:, 0:w], in0=tA[:, 0:w], scalar1=0.044715,
                                                             scalar2=1.0, op0=ALU.mult, op1=ALU.add),
                      r=[tAk], w=[tAk])
                S.add("dve", I("tensor_tensor", out=tA[:, 0:w], in0=tA[:, 0:w], in1=bg[:, 0:w],
                                                                   op=ALU.mult), r=[tAk, bgk], w=[tAk])
                S.add("act", I("activation", out=tA[:, 0:w], in_=tA[:, 0:w], func=AF.Sigmoid,
                                                          scale=1.5957691216057308), r=[tAk], w=[tAk])
                S.add("dve", I("tensor_tensor", out=tA[:, 0:w], in0=tA[:, 0:w], in1=bg[:, 0:w],
                                                                   op=ALU.mult), r=[tAk, bgk], w=[tAk])
                S.add("dve", I("tensor_tensor",
                    out=self.hid[:, c * TW:c * TW + w], in0=tA[:, 0:w], in1=hr[:, 0:w], op=ALU.mult),
                    r=[tAk, hrk], w=[("hid", c)])
                tB, tBk = self.tmpt()
                tC, tCk = self.tmpt()
                S.add("act", I("activation", out=tB[:, 0:w], in_=bx[:, 0:w], func=AF.Copy),
                      r=[bxk], w=[tBk])
                S.add("dve", I("tensor_tensor", out=tB[:, 0:w], in0=bc[:, 0:w], in1=tB[:, 0:w],
                                                                   op=ALU.mult), r=[tBk, bck], w=[tBk])
                wb = (l * 16 + c) * 3
                S.add("act", I("activation", out=tC[:, 0:w], in_=tB[:, 0:w],
                                                                       func=AF.Identity,
                                                                       scale=self.C("sccw", wb + 1)),
                      r=[tBk, "cst"], w=[tCk])
                for off in (-1, 1):
                    for (a, b, s) in segs(t0, w):
                        rowlen = (b - a) if s == 1 else GRID_W
                        ov, iv = self.conv_views(tC, off, rowlen, b - a, a)
                        S.add("dve", I("scalar_tensor_tensor",
                            out=ov(tC), in0=iv(tB), scalar=self.C("sccw", wb + 1 + off), in1=ov(tC),
                            op0=ALU.mult, op1=ALU.add), r=[tBk, tCk, "cst"], w=[tCk])
                S.add("dve", I("tensor_tensor",
                    out=self.hid[:, (16 + c) * TW:(16 + c) * TW + w], in0=tC[:, 0:w], in1=bb[:, 0:w], op=ALU.mult),
                    r=[tCk, bbk], w=[("hid", 16 + c)])
                self.pop_deferred(1)
            self.flush_deferred()
            self.load_tile(src, srcn, 0, t0, w)
            for c in range(NCH):
                slA, wkA = self.wload(self.wA[p2 + 2 * c], 4096)
                slB, wkB = self.wload(self.wA[p2 + 2 * c + 1], 4096)
                svA = slA[:, 0:4096].rearrange("p (k c) -> p k c", k=16)
                svB = slB[:, 0:4096].rearrange("p (k c) -> p k c", k=16)
                yr, yrk = self.bank()
                for kc in range(16):
                    self.mm(yr[:, 0:w], svA[:, kc, 0:128], self.hid[:, kc * TW:kc * TW + w], kc == 0, kc == 15,
                            r=[wkA, ("hid", kc)], w=[yrk])
                ys, ysk = self.bank()
                for kc in range(16):
                    self.mm(ys[:, 0:w], svA[:, kc, 128:256], self.hid[:, (16 + kc) * TW:(16 + kc) * TW + w],
                            kc == 0, kc == 15, r=[wkA, ("hid", 16 + kc)], w=[ysk])
                gr, grk = self.bank()
                for kc in range(16):
                    self.mm(gr[:, 0:w], svB[:, kc, 0:128], xin[:, kc * TW:kc * TW + w], kc == 0, kc == 15,
                            r=[wkB, ("xin", par, kc)], w=[grk])
                gs, gsk = self.bank()
                for kc in range(16):
                    self.mm(gs[:, 0:w], svB[:, kc, 128:256], xin[:, kc * TW:kc * TW + w], kc == 0, kc == 15,
                            r=[wkB, ("xin", par, kc)], w=[gsk])
                tA, tAk = self.tmpt()
                tB, tBk = self.tmpt()
                b0 = self.C("bmerge", (l * 2 + 0) * 16 + c)
                b1 = self.C("bmerge", (l * 2 + 1) * 16 + c)
                S.add("act", I("activation", out=tA[:, 0:w], in_=gr[:, 0:w],
                                                                       func=AF.Sigmoid, bias=b0),
                      r=[grk, "cst"], w=[tAk])
                S.add("act", I("activation", out=tB[:, 0:w], in_=gs[:, 0:w],
                                                                       func=AF.Sigmoid, bias=b1),
                      r=[gsk, "cst"], w=[tBk])
                S.add("dve", I("tensor_tensor", out=tA[:, 0:w], in0=tA[:, 0:w], in1=yr[:, 0:w],
                                                                   op=ALU.mult), r=[tAk, yrk], w=[tAk])
                S.add("dve", I("tensor_tensor", out=tB[:, 0:w], in0=tB[:, 0:w], in1=ys[:, 0:w],
                                                                   op=ALU.mult), r=[tBk, ysk], w=[tBk])
                S.add("dve", I("tensor_tensor",
                    out=self.hid[:, (32 + c) * TW:(32 + c) * TW + w], in0=tA[:, 0:w], in1=tB[:, 0:w], op=ALU.add),
                    r=[tAk, tBk], w=[("hid", 32 + c)])
            mid()
            pending = []
            n_done = [0]
            for pn in range(8):
                slot, wk = self.wload(self.wA[p3 + pn], 4096)
                sv = slot[:, 0:4096].rearrange("p (k c) -> p k c", k=16)
                for half in range(2):
                    oc = pn * 2 + half
                    yb, ybk = self.bank()
                    for kc in range(16):
                        self.mm(yb[:, 0:w], sv[:, kc, half * 128:(half + 1) * 128],
                                self.hid[:, (32 + kc) * TW:(32 + kc) * TW + w], kc == 0, kc == 15,
                                r=[wk, ("hid", 32 + kc)], w=[ybk])
                    self.flush_stats(pending, n_done, NCH, keep=0)
                    self.resid_stats(l, 1, oc, yb, ybk, t0, w, pending)
            self.flush_stats(pending, n_done, NCH)
            self.finalize(l, 1, t0, w, dst, dstn, False)


def _panelsA(Wm):
    K, N = Wm.shape
    n = N // 256
    return np.ascontiguousarray(Wm.reshape(16, 128, n, 256).transpose(2, 1, 0, 3)).reshape(n, 128, 4096)


def _panelsB(Wm):
    return np.ascontiguousarray(Wm.reshape(44, 128, 16, 128).transpose(2, 1, 0, 3)).reshape(16, 128, 5632)


def _fm(v):
    v = np.asarray(v, np.float32)
    lead = v.shape[:-1]
    return np.moveaxis(v.reshape(*lead, 16, 128), -1, 0)


def prepare(inp, n_cores=8):
    f = lambda k: np.asarray(inp[k], np.float32)
    w_in = f("w_in")
    wA = np.empty((L * 240, 128, 4096), np.float32)
    wB = np.empty((L * 32, 128, 5632), np.float32)
    for l in range(L):
        base = l * 240
        wA[base:base + 72] = _panelsA(f("w_mod")[l])
        for which, nm in enumerate(("ffn1", "ffn2")):
            wi = f(nm + "_w_in")[l]
            g = wi[:, :DFF].reshape(D, NFF, 128)
            u = wi[:, DFF:].reshape(D, NFF, 128)
            wA[base + 72 + which * 44:base + 72 + (which + 1) * 44] = _panelsA(
                np.concatenate([g, u], axis=2).reshape(D, NFF * 256))
            wB[l * 32 + which * 16:l * 32 + (which + 1) * 16] = _panelsB(f(nm + "_w_out")[l])
        W = w_in[l]
        part = lambda i: W[:, i * 2048:(i + 1) * 2048].reshape(D, 16, 128)
        rg_x, rg_gate, sc_b, sc_c, sc_x, g_rg, g_sc = [part(i) for i in range(7)]
        wA[base + 160:base + 168] = _panelsA(W[:, 0:2048])
        p1 = np.stack([np.concatenate([rg_gate, sc_b], axis=2), np.concatenate([sc_c, sc_x], axis=2)], axis=2)
        wA[base + 168:base + 200] = _panelsA(p1.reshape(D, 32 * 256))
        wro = f("w_rg_out")[l].reshape(D, 16, 128)
        wso = f("w_sc_out")[l].reshape(D, 16, 128)
        p2 = np.stack([np.concatenate([wro, wso], axis=2), np.concatenate([g_rg, g_sc], axis=2)], axis=2)
        wA[base + 200:base + 232] = _panelsA(p2.reshape(D, 32 * 256))
        wA[base + 232:base + 240] = _panelsA(f("w_o")[l])
    x, ctx, c, c_ctx = f("x"), f("ctx"), f("c"), f("c_ctx")
    gate_w = f("rg_gate_w")
    in_maps = []
    for core in range(n_cores):
        b, half = core // 2, core % 2
        xc = ctx[b]
        xl = x[b, half * TLAT:(half + 1) * TLAT]
        if half == 1:
            xc = xc[::-1]
            xl = xl[::-1]
        xT = np.ascontiguousarray(np.concatenate([xc, xl], axis=0).T)
        cs = np.zeros((128, NCONST), np.float32)

        def put(name, arr):
            arr = np.ascontiguousarray(arr, dtype=np.float32).reshape(128, -1)
            cs[:, _off[name]:_off[name] + arr.shape[1]] = arr

        put("bmod", np.moveaxis(f("b_mod").reshape(L, 144, 128), -1, 0))
        put("lng", _fm(f("ln_g")))
        put("lnb", _fm(f("ln_b")))
        put("bmerge", _fm(f("b_merge")))
        rw = f("rg_conv_w")
        z = np.zeros_like(rw[:, :1])
        taps = np.concatenate([rw, z], axis=1) if half == 0 else np.concatenate([z, rw[:, ::-1]], axis=1)
        put("rgcw", np.moveaxis(_fm(taps), 2, 3))
        put("rgcb", _fm(f("rg_conv_b")))
        sw = f("sc_conv_w")
        if half == 1:
            sw = sw[:, ::-1]
        put("sccw", np.moveaxis(_fm(sw), 2, 3))
        dirs = [0, 1] if half == 0 else [1, 0]
        put("gb", _fm(f("rg_gate_b")[:, dirs]))
        put("lam", _fm(f("rg_lam")[:, dirs]))
        put("sel", np.tile(np.array([0.0, 1.0] if half == 0 else [1.0, 0.0], np.float32), (128, 1)))
        put("cc", np.stack([_fm(c[b]), _fm(c_ctx)], axis=-1))
        gw = gate_w[:, dirs]
        gw = np.ascontiguousarray(gw.transpose(4, 0, 1, 3, 2, 5)).reshape(128, L * 2 * 16 * 2, 128)
        in_maps.append({"xT": xT, "consts": cs, "gw": gw, "wA": wA, "wB": wB})
    return in_maps


_NC_CACHE = {}


def kernel(**inputs):
    n = 8
    in_maps = prepare(inputs, n)
    if "nc" not in _NC_CACHE:
        _NC_CACHE["nc"] = Builder(n).build()
    res = run_bass_kernel_spmd(_NC_CACHE["nc"], in_maps, core_ids=list(range(n)))
    out = np.empty((4, 4096, D), np.float32)
    for core in range(n):
        b, half = core // 2, core % 2
        o = res.results[core]["outT"].T
        if half == 1:
            o = o[::-1]
        out[b, half * TLAT:(half + 1) * TLAT] = o
    return out
```

```python
import contextlib
import numpy as np
import concourse.bass as bass
import concourse.mybir as mybir
from concourse.bass_utils import run_bass_kernel_spmd

F32 = mybir.dt.float32
BF16 = mybir.dt.bfloat16
AF = mybir.ActivationFunctionType
ALU = mybir.AluOpType

D = 2048
NCH = 16
DFF = 5632
NFF = 44
TCTX = 256
TLAT = 2048
T = TCTX + TLAT
L = 2
GRID_W = 64
TW = 512
TILES = [(0, 512), (512, 512), (1024, 512), (1536, 512), (2048, 256)]
ALPHA = (2 * L) ** 0.25
EPSP = 1e-5 / ALPHA ** 2
NSLOT = 4
SLOT_ELEMS = 5632
EPOCH = 30000
PIPE = True

_off = {}
_cur = 0
for _name, _n in [("bmod", L * 144), ("lng", L * 3 * 16), ("lnb", L * 3 * 16), ("bmerge", L * 2 * 16),
                  ("rgcw", L * 16 * 5), ("rgcb", L * 16), ("sccw", L * 16 * 3), ("gb", L * 2 * 2 * 16),
                  ("lam", L * 2 * 16), ("sel", 2), ("cc", 32)]:
    _off[_name] = _cur
    _cur += _n
NCONST = _cur


class Op:
    __slots__ = ("eng", "fn", "deps", "dma", "dma_val", "needed", "ms")

    def __init__(self, eng, fn, deps, dma):
        self.eng, self.fn, self.deps, self.dma = eng, fn, deps, dma
        self.dma_val = 0
        self.needed = False
        self.ms = 0


class Sched:
    ENG = ("pe", "act", "dve", "pool", "sp")

    def __init__(self):
        self.ops = []
        self.lw = {}
        self.rd = {}
        self.last_on = {e: None for e in self.ENG}
        self.fence_deps = {e: [] for e in self.ENG}
        self.dma_cnt = {}

    def add(self, eng, fn, r=(), w=(), dma=None):
        idx = len(self.ops)
        deps = set(self.fence_deps[eng])
        self.fence_deps[eng] = []
        for k in r:
            x = self.lw.get(k)
            if x is not None:
                deps.add(x)
        for k in w:
            x = self.lw.get(k)
            if x is not None:
                deps.add(x)
            for x in self.rd.get(k, {}).values():
                deps.add(x)
        for k in r:
            self.rd.setdefault(k, {})[eng if dma is None else ("dma", idx)] = idx
        for k in w:
            self.lw[k] = idx
            self.rd[k] = {}
        deps.discard(idx)
        op = Op(eng, fn, sorted(deps), dma)
        if dma is not None:
            self.dma_cnt[dma] = self.dma_cnt.get(dma, 0) + 16
            op.dma_val = self.dma_cnt[dma]
        for d in deps:
            self.ops[d].needed = True
        self.ops.append(op)
        if dma is None:
            self.last_on[eng] = idx
        return idx

    def fence(self):
        f = [v for v in self.last_on.values() if v is not None]
        for e in self.ENG:
            self.fence_deps[e] = list(f)

    def emit(self, nc, stack):
        cnt = {e: 0 for e in self.ENG}
        for op in self.ops:
            if op.dma is None and op.needed:
                cnt[op.eng] += 1
                op.ms = cnt[op.eng]
        esem = {e: [stack.enter_context(nc.semaphore(f"s_{e}_{i}")) for i in range(cnt[e] // EPOCH + 1)]
                for e in self.ENG}
        dsem = {}
        for i, ch in enumerate(sorted(self.dma_cnt, key=str)):
            assert self.dma_cnt[ch] < 60000, (ch, self.dma_cnt[ch])
            dsem[ch] = stack.enter_context(nc.semaphore(f"d_{i}"))
        ops = self.ops
        by_eng = {e: [op for op in ops if op.eng == e] for e in self.ENG}

        def run(e, eo):
            waited = {}
            for op in by_eng[e]:
                for d in op.deps:
                    p = ops[d]
                    if p.dma is not None:
                        sid, sem, val = ("d", p.dma), dsem[p.dma], p.dma_val
                    else:
                        if p.eng == "pe" and e == "pe":
                            continue
                        ep = (p.ms - 1) // EPOCH
                        sid, sem, val = (p.eng, ep), esem[p.eng][ep], (p.ms - 1) % EPOCH + 1
                    if waited.get(sid, 0) >= val:
                        continue
                    eo.wait_ge(sem, val)
                    waited[sid] = val
                if op.fn is None:
                    continue
                m_, a_, k_ = op.fn
                ins = getattr(eo, m_)(*a_, **k_)
                if op.dma is not None:
                    ins.then_inc(dsem[op.dma], 16)
                elif op.needed:
                    ins.then_inc(esem[e][(op.ms - 1) // EPOCH], 1)

        block = stack.enter_context(nc.Block())

        @block.tensor
        def _(eo):
            run("pe", eo)

        @block.scalar
        def _(eo):
            run("act", eo)

        @block.vector
        def _(eo):
            run("dve", eo)

        @block.gpsimd
        def _(eo):
            run("pool", eo)

        @block.sync
        def _(eo):
            run("sp", eo)


def I(method, *args, **kw):
    return (method, args, kw)


def segs(t0, w):
    out = []
    if t0 < TCTX:
        out.append((0, min(w, TCTX - t0), 1))
    if t0 + w > TCTX:
        out.append((max(0, TCTX - t0), w, 0))
    return out


class Builder:
    def __init__(self, n_cores, stop_after=None, dbg=None):
        self.n_cores = n_cores
        self.stop_after = stop_after
        self.dbg = dbg

    def build(self):
        nc = bass.Bass("TRN2", target_bir_lowering=False)
        self.nc = nc
        S = Sched()
        self.S = S
        with contextlib.ExitStack() as st:
            self.alloc(nc, st)
            self.program()
            S.emit(nc, st)
        return nc

    def alloc(self, nc, st):
        def dram_in(name, shape):
            return nc.dram_tensor(name, shape, F32, kind="ExternalInput").ap()

        self.xT = dram_in("xT", [D, T])
        self.consts = dram_in("consts", [128, NCONST])
        self.gw = dram_in("gw", [128, L * 2 * 16 * 2, 128])
        self.wA = dram_in("wA", [L * 240, 128, 4096])
        self.wB = dram_in("wB", [L * 32, 128, 5632])
        self.outT = nc.dram_tensor("outT", [D, TLAT], F32, kind="ExternalOutput").ap()
        if self.dbg:
            self.dbg_out = nc.dram_tensor("dbg", [D, T], F32, kind="ExternalOutput").ap()
        self.xa = nc.dram_tensor("xa", [D, T], F32).ap()
        self.xb = nc.dram_tensor("xb", [D, T], F32).ap()
        self.rgx = nc.dram_tensor("rgx", [D, T], F32).ap()
        self.hrg = nc.dram_tensor("hrg", [D, T], F32).ap()
        self.st_in = nc.dram_tensor("st_in", [128, 16], F32)
        self.st_out = nc.dram_tensor("st_out", [256, 16], F32)

        def sb(name, shape, dt=F32):
            return st.enter_context(nc.sbuf_tensor(name, shape, dt))

        self.cst = sb("cst", [128, NCONST])
        self.modT = sb("modT", [128, L * 288])
        self.sc1p = sb("sc1p", [128, L * 96])
        self.gsc = sb("gsc", [128, L * 96])
        self.c8 = sb("c8", [128, L * 32])
        self.c16 = sb("c16", [128, L * 32])
        self.scb = sb("scb", [128, 32], BF16)
        self.onesb = sb("onesb", [128, 128], BF16)
        self.states = sb("states", [128, 16])
        self.sgt = sb("sgt", [128, 32])
        self.h0 = sb("h0", [128, 16])
        self.gwt = sb("gwt", [128, 2, 128])
        self.wslot = [sb(f"wslot{i}", [128, SLOT_ELEMS], BF16) for i in range(NSLOT)]
        self.xs = sb("xs", [128, NCH * TW])
        self.xin2 = [sb("xin0", [128, NCH * TW], BF16), sb("xin1", [128, NCH * TW], BF16)]
        self.px = [sb(f"px{i}", [128, TW]) for i in range(2)]
        self.deferred = []
        self.hid = sb("hid", [128, 48 * TW], BF16)
        self.tmp = [sb(f"tmp{i}", [128, TW]) for i in range(8)]
        self.hrt = [sb(f"hrt{i}", [128, TW]) for i in range(2)]
        self.stg = [sb(f"stg{i}", [128, TW]) for i in range(2)]
        self.mean = sb("mean", [128, TW])
        self.rstd = sb("rstd", [128, TW])
        self.ps = [st.enter_context(nc.psum_tensor(f"ps{i}", [128, 512], F32)) for i in range(8)]
        hid32 = self.hid.bitcast(F32)
        self.sbuf_scan = [hid32[:, i * T:(i + 1) * T] for i in range(5)] + \
                         [self.xs[:, i * T:(i + 1) * T] for i in range(3)] + \
                         [self.xin2[0].bitcast(F32)[:, 0:T], self.xin2[1].bitcast(F32)[:, 0:T]]
        self.ws_i = 0
        self.pp_i = 0
        self.tp_i = 0
        self.stg_i = 0

    def C(self, name, idx, n=1):
        o = _off[name] + idx
        return self.cst[:, o:o + n]

    def wload(self, dram_panel, nelem):
        i = self.ws_i % NSLOT
        self.ws_i += 1
        slot = self.wslot[i]
        self.S.add("pool", I("dma_start", out=slot[:, 0:nelem], in_=dram_panel, max_dma_last_dim=8192),
                   w=[("w", i)], dma=("w", i))
        return slot, ("w", i)

    def bank(self):
        i = self.pp_i % 6
        self.pp_i += 1
        return self.ps[i], ("ps", i)

    def tmpt(self):
        i = self.tp_i % 8
        self.tp_i += 1
        return self.tmp[i], ("tmp", i)

    def stage(self):
        i = self.stg_i % 2
        self.stg_i += 1
        return self.stg[i], ("stg", i)

    def mm(self, out, lhsT, rhs, start, stop, r, w):
        self.S.add("pe", I("matmul", out, lhsT, rhs, start=start, stop=stop), r=r, w=w)

    def mod(self, l, j, c, s):
        o = (l * 144 + j * 16 + c) * 2 + s
        return self.modT[:, o:o + 1]

    def dk(self, name, c, i):
        return ("dram", name, c, i)

    def program(self):
        S = self.S
        nc = self.nc
        S.add("sp", I("dma_start", out=self.cst[:], in_=self.consts), w=["cst"], dma="cst")
        S.add("dve", I("memset", self.onesb[:], 1.0 / D), w=["onesb"])
        self.setup_consts()
        for pn in range(40):
            self.mods_panel(0, pn)
        self.mods_evac(0, 0, 5)
        self.bg = [lambda pn=pn: self.mods_panel(0, pn) for pn in range(40, 72)]
        self.bg.append(lambda: self.mods_evac(0, 5, 9))
        for l in range(1, L):
            self.bg += [lambda l=l, pn=pn: self.mods_panel(l, pn) for pn in range(72)]
            self.bg.append(lambda l=l: self.mods_evac(l, 0, 9))
        if self.stop_after == "mods":
            return self.finish_dbg_small()
        cur = (self.xT, "xT")
        A = (self.xa, "xa")
        Bb = (self.xb, "xb")
        seq = []

        def tile_items(kind, l, i, src, dst, tiles, final=False):
            for (t0, w) in tiles:
                if kind == "ffn":
                    body = (lambda par, mid, l=l, i=i, src=src, dst=dst, t0=t0, w=w, final=final:
                            self.ffn_tile(l, i, src[0], src[1], dst[0], dst[1], t0, w, final, par, mid))
                elif kind == "rgx":
                    body = (lambda par, mid, l=l, t0=t0, w=w: self.rgx_tile(l, t0, w, par, mid))
                else:
                    body = (lambda par, mid, l=l, src=src, dst=dst, t0=t0, w=w:
                            self.mix_tile(l, src[0], src[1], dst[0], dst[1], t0, w, par, mid))
                prep = (lambda par, l=l, i=i, src=src, t0=t0, w=w: self.prep(l, i, src[0], src[1], t0, w, par))
                seq.append(("tile", prep, body))

        for l in range(L):
            last = l == L - 1
            tile_items("ffn", l, 0, cur, A, TILES)
            seq.append(("mark", f"ffn1_{l}", A))
            tile_items("rgx", l, 1, A, None, TILES)
            seq.append(("mark", f"rgx_{l}", (self.rgx, "rgx")))
            seq.append(("barrier", lambda l=l: self.scan(l)))
            seq.append(("mark", f"scan_{l}", (self.hrg, "hrg")))
            tiles = TILES if not last else [(256, 512), (768, 512), (1280, 512), (1792, 512)]
            tile_items("mix", l, 1, A, Bb, tiles)
            seq.append(("mark", f"mix_{l}", Bb))
            if last:
                tile_items("ffn", l, 2, Bb, (self.outT, "out"), tiles, final=True)
            else:
                tile_items("ffn", l, 2, Bb, A, tiles)
                seq.append(("mark", f"ffn2_{l}", A))
                cur = A
                A, Bb = Bb, A
        prepped = -1
        for idx, it in enumerate(seq):
            if it[0] == "mark":
                if self.stop_after == it[1]:
                    self.flush_deferred()
                    return self.finish_dbg(*it[2])
                continue
            if it[0] == "barrier":
                self.flush_deferred()
                S.fence()
                it[1]()
                self.run_bg(len(self.bg))
                S.fence()
                continue
            par = idx % 2
            if prepped != idx:
                it[1](par)
            nxt = None
            for k in range(idx + 1, len(seq)):
                if seq[k][0] == "mark":
                    continue
                if seq[k][0] == "tile":
                    nxt = k
                break
            if nxt is not None and self.stop_after is not None:
                for k in range(idx + 1, nxt):
                    if seq[k][0] == "mark" and seq[k][1] == self.stop_after:
                        nxt = None
                        break

            if not PIPE:
                nxt = None

            def mid(nxt=nxt):
                if nxt is not None:
                    seq[nxt][1](nxt % 2)
            it[2](par, mid)
            if nxt is not None:
                prepped = nxt
            if not PIPE:
                self.flush_deferred()
        self.flush_deferred()
        S.add("sp", None, r=[self.dk("out", c, i) for c in range(NCH) for i in range(T // 256)])

    def finish_dbg(self, src_ap, name):
        S = self.S
        S.fence()
        for c in range(NCH):
            buf = self.sbuf_scan[c % 2]
            key = ("scanbuf", c % 2)
            S.add("sp", I("dma_start", out=buf, in_=src_ap[c * 128:(c + 1) * 128, :]),
                  r=[self.dk(name, c, i) for i in range(T // 256)], w=[key], dma=("dbgl", c % 2))
            S.add("sp", I("dma_start", out=self.dbg_out[c * 128:(c + 1) * 128, :], in_=buf),
                  r=[key], w=[("dbgout", c)], dma=("dbgs", c % 2))
        S.add("sp", None, r=[("dbgout", c) for c in range(NCH)])

    def finish_dbg_small(self):
        S = self.S
        S.add("sp", I("dma_start", out=self.dbg_out[0:128, 0:L * 288], in_=self.modT[:]),
              r=["modT0", "modT1"], w=["dbgo"], dma="dbgs")
        S.add("sp", I("dma_start", out=self.dbg_out[128:256, 0:L * 32], in_=self.c8[:]),
              r=["c8"], w=["dbgo2"], dma="dbgs2")
        S.add("sp", None, r=["dbgo", "dbgo2"])

    def setup_consts(self):
        S = self.S
        lam = self.C("lam", 0, L * 32)
        t0 = self.tmp[0]
        S.add("act", I("activation", out=t0[:, 0:L * 32], in_=lam, func=AF.Exp, scale=-1.0),
              r=["cst"], w=[("tmp", 0)])
        S.add("act", I("activation", out=t0[:, 0:L * 32], in_=t0[:, 0:L * 32], func=AF.Ln, bias=1.0),
              r=[("tmp", 0)], w=[("tmp", 0)])
        S.add("dve", I("tensor_scalar", out=self.c8[:], in0=t0[:, 0:L * 32], scalar1=-8.0, scalar2=None,
                                               op0=ALU.mult), r=[("tmp", 0)], w=["c8"])
        S.add("dve", I("tensor_scalar", out=self.c16[:], in0=t0[:, 0:L * 32], scalar1=-16.0, scalar2=None,
                                               op0=ALU.mult), r=[("tmp", 0)], w=["c16"])
        cc = self.C("cc", 0, 32)
        S.add("act", I("activation", out=self.scb[:], in_=cc, func=AF.Silu), r=["cst"], w=["scb"])

    def mods_panel(self, l, pn):
        psb = self.ps[6 + l]
        slot, wk = self.wload(self.wA[l * 240 + pn], 4096)
        sv = slot[:, 0:4096].rearrange("p (k c) -> p k c", k=16)
        for half in range(2):
            m = pn * 2 + half
            for kc in range(16):
                self.mm(psb[:, 2 * m:2 * m + 2], sv[:, kc, half * 128:(half + 1) * 128],
                        self.scb[:, 2 * kc:2 * kc + 2], kc == 0, kc == 15,
                        r=[wk, "scb"], w=[("ps", 6 + l)])

    def mods_evac(self, l, j_lo, j_hi):
        S = self.S
        psb = self.ps[6 + l]
        m0, m1 = j_lo * 16, j_hi * 16
        mt = self.modT[:, l * 288:(l + 1) * 288]
        bm = self.C("bmod", l * 144 + m0, m1 - m0)
        for s in range(2):
            S.add("dve", I("tensor_tensor", out=mt[:, 2 * m0 + s:2 * m1:2], in0=psb[:, 2 * m0 + s:2 * m1:2], in1=bm,
                           op=ALU.add), r=[("ps", 6 + l), "cst"], w=[f"modT{l}"])
        coef = [0.5 / ALPHA, 1.0 / ALPHA, 0.5 / ALPHA]
        for i in range(3):
            if j_lo <= 3 * i + 1 < j_hi:
                src1 = self.modT[:, (l * 144 + (3 * i + 1) * 16) * 2:(l * 144 + (3 * i + 2) * 16) * 2]
                dst1 = self.sc1p[:, l * 96 + i * 32:l * 96 + (i + 1) * 32]
                S.add("dve", I("tensor_scalar", out=dst1, in0=src1, scalar1=1.0, scalar2=None, op0=ALU.add),
                      r=[f"modT{l}"], w=[f"sc1p{l}"])
            if j_lo <= 3 * i + 2 < j_hi:
                src2 = self.modT[:, (l * 144 + (3 * i + 2) * 16) * 2:(l * 144 + (3 * i + 3) * 16) * 2]
                dst2 = self.gsc[:, l * 96 + i * 32:l * 96 + (i + 1) * 32]
                S.add("dve", I("tensor_scalar", out=dst2, in0=src2, scalar1=coef[i], scalar2=None, op0=ALU.mult),
                      r=[f"modT{l}"], w=[f"gsc{l}"])

    def run_bg(self, n):
        for _ in range(n):
            if self.bg:
                self.bg.pop(0)()

    def SC1P(self, l, i, c, s):
        o = l * 96 + i * 32 + c * 2 + s
        return self.sc1p[:, o:o + 1]

    def GSC(self, l, i, c, s):
        o = l * 96 + i * 32 + c * 2 + s
        return self.gsc[:, o:o + 1]

    def load_tile(self, src, srcn, ti, t0, w):
        S = self.S
        tk = self.tkeys(t0, w)
        srcv = src.rearrange("(c p) t -> p c t", p=128)[:, :, t0:t0 + w]
        dstv = self.xs.rearrange("p (c t) -> p c t", c=NCH)[:, :, 0:w]
        S.add("sp", I("dma_start", out=dstv, in_=srcv),
              r=[self.dk(srcn, c, i) for c in range(NCH) for i in tk], w=[("xs", c) for c in range(NCH)],
              dma="xs")

    def tkeys(self, t0, w):
        return list(range(t0 // 256, (t0 + w) // 256))

    def prep(self, l, i, src, srcn, t0, w, par):
        S = self.S
        tk = self.tkeys(t0, w)
        xin = self.xin2[par]
        for c in range(NCH):
            px = self.px[c % 2]
            pxk = ("px", c % 2)
            S.add("sp", I("dma_start", out=px[:, 0:w], in_=src[c * 128:(c + 1) * 128, t0:t0 + w]),
                  r=[self.dk(srcn, c, i) for i in tk], w=[pxk], dma=pxk)
            for (a, b, s) in segs(t0, w):
                S.add("act", I("activation", out=xin[:, c * TW + a:c * TW + b], in_=px[:, a:b],
                               func=AF.Identity, scale=self.SC1P(l, i, c, s), bias=self.mod(l, 3 * i, c, s)),
                      r=[pxk, f"modT{l}", f"sc1p{l}"], w=[("xin", par, c)])

    def pop_deferred(self, n=1):
        for _ in range(n):
            if self.deferred:
                self.deferred.pop(0)()

    def flush_deferred(self):
        while self.deferred:
            self.deferred.pop(0)()

    def resid_stats(self, l, i, oc, yb, ybk, t0, w, pending):
        S = self.S
        xsl = self.xs[:, oc * TW:oc * TW + w]
        for (a, b, s) in segs(t0, w):
            S.add("dve", I("scalar_tensor_tensor", out=self.xs[:, oc * TW + a:oc * TW + b], in0=yb[:, a:b],
                           scalar=self.GSC(l, i, oc, s), in1=self.xs[:, oc * TW + a:oc * TW + b],
                           op0=ALU.mult, op1=ALU.add), r=[ybk, ("xs", oc), f"gsc{l}"], w=[("xs", oc)])
        sq, sqk = self.tmpt()
        S.add("act", I("activation", out=sq[:, 0:w], in_=xsl, func=AF.Square), r=[("xs", oc)], w=[sqk])
        hl, hlk = self.tmpt()
        hlv = hl.bitcast(BF16)
        hi, lo = hlv[:, 0:w], hlv[:, TW:TW + w]
        S.add("act", I("activation", out=hi, in_=xsl, func=AF.Copy), r=[("xs", oc)], w=[hlk])
        S.add("dve", I("tensor_tensor", out=lo, in0=xsl, in1=hi, op=ALU.subtract), r=[("xs", oc), hlk], w=[hlk])
        sh, shk = self.tmpt()
        shv = sh.bitcast(BF16)
        sqh, sql = shv[:, 0:w], shv[:, TW:TW + w]
        S.add("act", I("activation", out=sqh, in_=sq[:, 0:w], func=AF.Copy), r=[sqk], w=[shk])
        S.add("dve", I("tensor_tensor", out=sql, in0=sq[:, 0:w], in1=sqh, op=ALU.subtract), r=[sqk, shk], w=[shk])

        def stats(first, lastf):
            self.mm(self.ps[6][:, 0:w], self.onesb[:], hi, first, False, r=[hlk, "onesb"], w=[("ps", 6)])
            self.mm(self.ps[6][:, 0:w], self.onesb[:], lo, False, lastf, r=[hlk, "onesb"], w=[("ps", 6)])
            self.mm(self.ps[7][:, 0:w], self.onesb[:], sqh, first, False, r=[shk, "onesb"], w=[("ps", 7)])
            self.mm(self.ps[7][:, 0:w], self.onesb[:], sql, False, lastf, r=[shk, "onesb"], w=[("ps", 7)])
        pending.append(stats)

    def flush_stats(self, pending, n_done, total, keep=0):
        while len(pending) > keep:
            f = pending.pop(0)
            f(n_done[0] == 0, n_done[0] == total - 1)
            n_done[0] += 1

    def finalize(self, l, i, t0, w, dst, dstn, final):
        S = self.S
        tk = self.tkeys(t0, w)
        mean, rstd = self.mean, self.rstd
        msq, msqk = self.tmpt()
        S.add("act", I("activation", out=mean[:, 0:w], in_=self.ps[6][:, 0:w], func=AF.Copy),
              r=[("ps", 6)], w=["mean"])
        S.add("act", I("activation", out=msq[:, 0:w], in_=self.ps[6][:, 0:w], func=AF.Square),
              r=[("ps", 6)], w=[msqk])
        S.add("dve", I("tensor_tensor", out=msq[:, 0:w], in0=self.ps[7][:, 0:w], in1=msq[:, 0:w],
                       op=ALU.subtract), r=[("ps", 7), msqk], w=[msqk])
        S.add("dve", I("tensor_scalar", out=msq[:, 0:w], in0=msq[:, 0:w], scalar1=0.0, scalar2=EPSP,
                       op0=ALU.max, op1=ALU.add), r=[msqk], w=[msqk])
        S.add("act", I("activation", out=msq[:, 0:w], in_=msq[:, 0:w], func=AF.Sqrt), r=[msqk], w=[msqk])
        S.add("dve", I("reciprocal", out=rstd[:, 0:w], in_=msq[:, 0:w]), r=[msqk], w=["rstd"])

        def chunk(c):
            t1, t1k = self.tmpt()
            xsl = self.xs[:, c * TW:c * TW + w]
            S.add("dve", I("tensor_tensor", out=t1[:, 0:w], in0=xsl, in1=mean[:, 0:w], op=ALU.subtract),
                  r=[("xs", c), "mean"], w=[t1k])
            S.add("dve", I("tensor_tensor", out=t1[:, 0:w], in0=t1[:, 0:w], in1=rstd[:, 0:w], op=ALU.mult),
                  r=[t1k, "rstd"], w=[t1k])
            sg, sgk = self.stage()
            g = self.C("lng", (l * 3 + i) * 16 + c)
            bb = self.C("lnb", (l * 3 + i) * 16 + c)
            S.add("act", I("activation", out=sg[:, 0:w], in_=t1[:, 0:w], func=AF.Identity, scale=g, bias=bb),
                  r=[t1k, "cst"], w=[sgk])
            if final:
                dsl = dst[c * 128:(c + 1) * 128, t0 - TCTX:t0 - TCTX + w]
            else:
                dsl = dst[c * 128:(c + 1) * 128, t0:t0 + w]
            S.add("sp", I("dma_start", out=dsl, in_=sg[:, 0:w]), r=[sgk], w=[self.dk(dstn, c, i) for i in tk], dma=sgk)
        for c in range(NCH):
            self.deferred.append(lambda c=c: chunk(c))

    def ffn_tile(self, l, i, src, srcn, dst, dstn, t0, w, final, par, mid):
        S = self.S
        which = 0 if i == 0 else 1
        pa = l * 240 + 72 + which * 44
        pb = l * 32 + which * 16
        xin = self.xin2[par]
        loaded = False
        for j in range(NFF):
            slot, wk = self.wload(self.wA[pa + j], 4096)
            sv = slot[:, 0:4096].rearrange("p (k c) -> p k c", k=16)
            gb, gbk = self.bank()
            ub, ubk = self.bank()
            for kc in range(16):
                self.mm(gb[:, 0:w], sv[:, kc, 0:128], xin[:, kc * TW:kc * TW + w], kc == 0, kc == 15,
                        r=[wk, ("xin", par, kc)], w=[gbk])
            for kc in range(16):
                self.mm(ub[:, 0:w], sv[:, kc, 128:256], xin[:, kc * TW:kc * TW + w], kc == 0, kc == 15,
                        r=[wk, ("xin", par, kc)], w=[ubk])
            sg, sgk = self.tmpt()
            S.add("act", I("activation", out=sg[:, 0:w], in_=gb[:, 0:w], func=AF.Silu), r=[gbk], w=[sgk])
            S.add("dve", I("tensor_tensor", out=self.hid[:, j * TW:j * TW + w], in0=sg[:, 0:w], in1=ub[:, 0:w],
                           op=ALU.mult), r=[sgk, ubk], w=[("hid", j)])
            self.pop_deferred(1)
            if not self.deferred and not loaded:
                self.load_tile(src, srcn, 0, t0, w)
                loaded = True
        mid()
        pending = []
        n_done = [0]
        for oc in range(NCH):
            slot, wk = self.wload(self.wB[pb + oc], 5632)
            sv = slot[:, 0:5632].rearrange("p (k c) -> p k c", k=NFF)
            yb, ybk = self.bank()
            for kc in range(NFF):
                self.mm(yb[:, 0:w], sv[:, kc, :], self.hid[:, kc * TW:kc * TW + w], kc == 0, kc == NFF - 1,
                        r=[wk, ("hid", kc)], w=[ybk])
            self.flush_stats(pending, n_done, NCH, keep=0)
            self.resid_stats(l, i, oc, yb, ybk, t0, w, pending)
        self.flush_stats(pending, n_done, NCH)
        self.finalize(l, i, t0, w, dst, dstn, final)

    def rgx_tile(self, l, t0, w, par, mid):
        S = self.S
        pa = l * 240 + 160
        tk = self.tkeys(t0, w)
        xin = self.xin2[par]
        for pn in range(8):
            slot, wk = self.wload(self.wA[pa + pn], 4096)
            sv = slot[:, 0:4096].rearrange("p (k c) -> p k c", k=16)
            for half in range(2):
                oc = pn * 2 + half
                yb, ybk = self.bank()
                for kc in range(16):
                    self.mm(yb[:, 0:w], sv[:, kc, half * 128:(half + 1) * 128],
                            xin[:, kc * TW:kc * TW + w], kc == 0, kc == 15, r=[wk, ("xin", par, kc)], w=[ybk])
                self.pop_deferred(1)
                sg, sgk = self.stage()
                S.add("act", I("activation", out=sg[:, 0:w], in_=yb[:, 0:w], func=AF.Copy), r=[ybk], w=[sgk])
                dsl = self.rgx[oc * 128:(oc + 1) * 128, t0:t0 + w]
                S.add("sp", I("dma_start", out=dsl, in_=sg[:, 0:w]), r=[sgk], w=[self.dk("rgx", oc, i) for i in tk], dma=sgk)
            if pn == 3:
                mid()

    def conv_views(self, ap, off, rowlen, ncols, c0):
        v = ap[:, c0:c0 + ncols]
        if rowlen != ncols:
            v = v.rearrange("p (r g) -> p r g", g=rowlen)
            lo, hi = max(0, -off), rowlen - max(0, off)
            return (lambda x: x[:, c0:c0 + ncols].rearrange("p (r g) -> p r g", g=rowlen)[:, :, lo:hi],
                    lambda x: x[:, c0:c0 + ncols].rearrange("p (r g) -> p r g", g=rowlen)[:, :, lo + off:hi + off])
        lo, hi = max(0, -off), rowlen - max(0, off)
        return (lambda x: x[:, c0 + lo:c0 + hi], lambda x: x[:, c0 + lo + off:c0 + hi + off])

    def gates(self, l, which, k, XC, XCk, Rg, Rgk, Ig, Igk, A, Ak, M, Mk):
        S = self.S
        gi = ((l * 2 + which) * 16 + k) * 2
        S.add("sp", I("dma_start", out=self.gwt[:], in_=self.gw[:, gi:gi + 2, :]), w=["gwt"], dma="gwt")
        for (t0, w) in TILES:
            rb, rbk = self.bank()
            ib, ibk = self.bank()
            self.mm(rb[:, 0:w], self.gwt[:, 0, :], XC[:, t0:t0 + w], True, True, r=["gwt", XCk], w=[rbk])
            self.mm(ib[:, 0:w], self.gwt[:, 1, :], XC[:, t0:t0 + w], True, True, r=["gwt", XCk], w=[ibk])
            br = self.C("gb", ((l * 2 + which) * 2 + 0) * 16 + k)
            bi = self.C("gb", ((l * 2 + which) * 2 + 1) * 16 + k)
            S.add("act", I("activation", out=Rg[:, t0:t0 + w], in_=rb[:, 0:w], func=AF.Sigmoid, bias=br),
                  r=[rbk, "cst"], w=[Rgk])
            S.add("act", I("activation", out=Ig[:, t0:t0 + w], in_=ib[:, 0:w], func=AF.Sigmoid, bias=bi),
                  r=[ibk, "cst"], w=[Igk])
        co = (l * 2 + which) * 16 + k
        S.add("act", I("activation", out=A, in_=Rg, func=AF.Exp, scale=self.c8[:, co:co + 1]),
              r=[Rgk, "c8"], w=[Ak])
        S.add("act", I("activation", out=M, in_=Rg, func=AF.Exp, scale=self.c16[:, co:co + 1]),
              r=[Rgk, "c16"], w=[Mk])
        S.add("act", I("activation", out=M, in_=M, func=AF.Relu, scale=-1.0, bias=1.0), r=[Mk], w=[Mk])
        S.add("act", I("activation", out=M, in_=M, func=AF.Sqrt), r=[Mk], w=[Mk])

    def scan(self, l):
        S = self.S
        B = self.sbuf_scan
        nt = T // 256

        def bufs(k):
            p = k % 2
            return [(B[5 * p + i], ("scanbuf", 5 * p + i)) for i in range(5)]

        def rows_of(k):
            return slice(k * 128, (k + 1) * 128)

        def vmul(XC, XCk, Ig, Igk, M, Mk):
            S.add("dve", I("tensor_tensor", out=Ig, in0=Ig, in1=XC, op=ALU.mult), r=[Igk, XCk], w=[Igk])
            S.add("dve", I("tensor_tensor", out=Ig, in0=Ig, in1=M, op=ALU.mult), r=[Igk, Mk], w=[Igk])

        def A1(k):
            (U, Uk), (XC, XCk), (Ig, Igk), (A, Ak), (M, Mk) = bufs(k)
            rows = rows_of(k)
            S.add("sp", I("dma_start", out=U, in_=self.rgx[rows, :]),
                  r=[self.dk("rgx", k, i) for i in range(nt)], w=[Uk], dma=Uk)
            wb = (l * 16 + k) * 5
            bias = self.C("rgcb", l * 16 + k)
            S.add("dve", I("tensor_scalar", out=XC, in0=U, scalar1=self.C("rgcw", wb + 2), scalar2=bias,
                           op0=ALU.mult, op1=ALU.add), r=[Uk, "cst"], w=[XCk])
            for off in (-2, -1, 1, 2):
                for (c0, ncols, rowlen) in ((0, TCTX, TCTX), (TCTX, TLAT, GRID_W)):
                    ov, iv = self.conv_views(XC, off, rowlen, ncols, c0)
                    S.add("dve", I("scalar_tensor_tensor", out=ov(XC), in0=iv(U), scalar=self.C("rgcw", wb + 2 + off),
                                   in1=ov(XC), op0=ALU.mult, op1=ALU.add), r=[Uk, XCk, "cst"], w=[XCk])
            S.add("sp", I("dma_start", out=self.rgx[rows, :], in_=XC),
                  r=[XCk], w=[self.dk("rgx", k, i) for i in range(nt)], dma=("st", XCk))
            self.gates(l, 0, k, XC, XCk, U, Uk, Ig, Igk, A, Ak, M, Mk)

        def B1(k):
            (U, Uk), (XC, XCk), (Ig, Igk), (A, Ak), (M, Mk) = bufs(k)
            rows = rows_of(k)
            vmul(XC, XCk, Ig, Igk, M, Mk)
            H, Hk = M, Mk
            S.add("dve", I("tensor_tensor_scan", out=H, data0=A, data1=Ig, initial=0.0, op0=ALU.mult, op1=ALU.add),
                  r=[Ak, Igk], w=[Hk])
            S.add("dve", I("tensor_copy", out=self.states[:, k:k + 1], in_=H[:, T - 1:T]), r=[Hk], w=["states"])
            S.add("sp", I("dma_start", out=self.hrg[rows, :], in_=H),
                  r=[Hk], w=[self.dk("hrg", k, i) for i in range(nt)], dma=("st", Hk))

        def A2(k):
            (HO, HOk), (XC, XCk), (Ig, Igk), (A, Ak), (M, Mk) = bufs(k)
            rows = rows_of(k)
            S.add("sp", I("dma_start", out=XC, in_=self.rgx[rows, :]),
                  r=[self.dk("rgx", k, i) for i in range(nt)], w=[XCk], dma=XCk)
            self.gates(l, 1, k, XC, XCk, HO, HOk, Ig, Igk, A, Ak, M, Mk)
            S.add("sp", I("dma_start", out=HO, in_=self.hrg[rows, :]),
                  r=[self.dk("hrg", k, i) for i in range(nt)], w=[HOk], dma=HOk)

        def B2(k):
            (HO, HOk), (XC, XCk), (Ig, Igk), (A, Ak), (M, Mk) = bufs(k)
            rows = rows_of(k)
            vmul(XC, XCk, Ig, Igk, M, Mk)
            H, Hk = M, Mk
            S.add("dve", I("tensor_tensor_scan", out=H[:, TCTX:T][:, ::-1], data0=A[:, TCTX:T][:, ::-1],
                           data1=Ig[:, TCTX:T][:, ::-1], initial=self.h0[:, k:k + 1], op0=ALU.mult, op1=ALU.add),
                  r=[Ak, Igk, "h0"], w=[Hk])
            S.add("dve", I("tensor_tensor_scan", out=H[:, 0:TCTX][:, ::-1], data0=A[:, 0:TCTX][:, ::-1],
                           data1=Ig[:, 0:TCTX][:, ::-1], initial=0.0, op0=ALU.mult, op1=ALU.add),
                  r=[Ak, Igk], w=[Hk])
            S.add("dve", I("tensor_tensor", out=HO, in0=HO, in1=H, op=ALU.add), r=[HOk, Hk], w=[HOk])
            S.add("sp", I("dma_start", out=self.hrg[rows, :], in_=HO),
                  r=[HOk], w=[self.dk("hrg", k, i) for i in range(nt)], dma=("st", HOk))

        nbg = (len(self.bg) + 31) // 32
        A1(0)
        for k in range(16):
            if k + 1 < 16:
                A1(k + 1)
            B1(k)
            self.run_bg(nbg)
        A2(0)
        S.add("sp", I("dma_start", out=self.st_in[:, :], in_=self.states[:]), r=["states"], w=["st_in"],
              dma="st_in")
        groups = [[2 * g, 2 * g + 1] for g in range(self.n_cores // 2)]
        S.add("pool", I("collective_compute", "AllGather", ALU.bypass, replica_groups=groups,
                        ins=[self.st_in.ap().opt()], outs=[self.st_out.ap().opt()]),
              r=["st_in"], w=["st_out"])
        S.add("sp", I("dma_start", out=self.sgt[:].rearrange("p (r k) -> p r k", r=2),
                      in_=self.st_out.ap().rearrange("(r p) k -> p r k", p=128)),
              r=["st_out"], w=["sgt"], dma="sgt")
        S.add("dve", I("tensor_scalar", out=self.h0[:], in0=self.sgt[:, 0:16], scalar1=self.C("sel", 0),
                       scalar2=None, op0=ALU.mult), r=["sgt", "cst"], w=["h0"])
        S.add("dve", I("scalar_tensor_tensor", out=self.h0[:], in0=self.sgt[:, 16:32], scalar=self.C("sel", 1),
                       in1=self.h0[:], op0=ALU.mult, op1=ALU.add), r=["sgt", "cst", "h0"], w=["h0"])
        for k in range(16):
            if k + 1 < 16:
                A2(k + 1)
            B2(k)
            self.run_bg(nbg)

    def mix_tile(self, l, src, srcn, dst, dstn, t0, w, par, mid):
        S = self.S
        p1 = l * 240 + 168
        p2 = l * 240 + 200
        p3 = l * 240 + 232
        xin = self.xin2[par]
        if True:
            tk = self.tkeys(t0, w)
            for c in range(NCH):
                slA, wkA = self.wload(self.wA[p1 + 2 * c], 4096)
                slB, wkB = self.wload(self.wA[p1 + 2 * c + 1], 4096)
                svA = slA[:, 0:4096].rearrange("p (k c) -> p k c", k=16)
                svB = slB[:, 0:4096].rearrange("p (k c) -> p k c", k=16)
                banks = []
                for (sv, wk, half) in ((svA, wkA, 0), (svA, wkA, 1), (svB, wkB, 0), (svB, wkB, 1)):
                    bk, bkk = self.bank()
                    for kc in range(16):
                        self.mm(bk[:, 0:w], sv[:, kc, half * 128:(half + 1) * 128],
                                xin[:, kc * TW:kc * TW + w], kc == 0, kc == 15, r=[wk, ("xin", par, kc)], w=[bkk])
                    banks.append((bk, bkk))
                (bg, bgk), (bb, bbk), (bc, bck), (bx, bxk) = banks
                hr = self.hrt[c % 2]
                hrk = ("hrt", c % 2)
                S.add("sp", I("dma_start", out=hr[:, 0:w],
                                                             in_=self.hrg[c * 128:(c + 1) * 128, t0:t0 + w]),
                      r=[self.dk("hrg", c, i) for i in tk], w=[hrk], dma=hrk)
                tA, tAk = self.tmpt()
                S.add("act", I("activation", out=tA[:, 0:w], in_=bg[:, 0:w], func=AF.Square),
                      r=[bgk], w=[tAk])
                S.add("dve", I("tensor_scalar", out=tA[:, 0:w], in0=tA[:, 0:w], scalar1=0.044715,
                                                             scalar2=1.0, op0=ALU.mult, op1=ALU.add),
                      r=[tAk], w=[tAk])
                S.add("dve", I("tensor_tensor", out=tA[:, 0:w], in0=tA[:, 0:w], in1=bg[:, 0:w],
                                                                   op=ALU.mult), r=[tAk, bgk], w=[tAk])
                S.add("act", I("activation", out=tA[:, 0:w], in_=tA[:, 0:w], func=AF.Sigmoid,
                                                          scale=1.5957691216057308), r=[tAk], w=[tAk])
                S.add("dve", I("tensor_tensor", out=tA[:, 0:w], in0=tA[:, 0:w], in1=bg[:, 0:w],
                                                                   op=ALU.mult), r=[tAk, bgk], w=[tAk])
                S.add("dve", I("tensor_tensor",
                    out=self.hid[:, c * TW:c * TW + w], in0=tA[:, 0:w], in1=hr[:, 0:w], op=ALU.mult),
                    r=[tAk, hrk], w=[("hid", c)])
                tB, tBk = self.tmpt()
                tC, tCk = self.tmpt()
                S.add("act", I("activation", out=tB[:, 0:w], in_=bx[:, 0:w], func=AF.Copy),
                      r=[bxk], w=[tBk])
                S.add("dve", I("tensor_tensor", out=tB[:, 0:w], in0=bc[:, 0:w], in1=tB[:, 0:w],
                                                                   op=ALU.mult), r=[tBk, bck], w=[tBk])
                wb = (l * 16 + c) * 3
                S.add("act", I("activation", out=tC[:, 0:w], in_=tB[:, 0:w],
                                                                       func=AF.Identity,
                                                                       scale=self.C("sccw", wb + 1)),
                      r=[tBk, "cst"], w=[tCk])
                for off in (-1, 1):
                    for (a, b, s) in segs(t0, w):
                        rowlen = (b - a) if s == 1 else GRID_W
                        ov, iv = self.conv_views(tC, off, rowlen, b - a, a)
                        S.add("dve", I("scalar_tensor_tensor",
                            out=ov(tC), in0=iv(tB), scalar=self.C("sccw", wb + 1 + off), in1=ov(tC),
                            op0=ALU.mult, op1=ALU.add), r=[tBk, tCk, "cst"], w=[tCk])
                S.add("dve", I("tensor_tensor",
                    out=self.hid[:, (16 + c) * TW:(16 + c) * TW + w], in0=tC[:, 0:w], in1=bb[:, 0:w], op=ALU.mult),
                    r=[tCk, bbk], w=[("hid", 16 + c)])
                self.pop_deferred(1)
            self.flush_deferred()
            self.load_tile(src, srcn, 0, t0, w)
            for c in range(NCH):
                slA, wkA = self.wload(self.wA[p2 + 2 * c], 4096)
                slB, wkB = self.wload(self.wA[p2 + 2 * c + 1], 4096)
                svA = slA[:, 0:4096].rearrange("p (k c) -> p k c", k=16)
                svB = slB[:, 0:4096].rearrange("p (k c) -> p k c", k=16)
                yr, yrk = self.bank()
                for kc in range(16):
                    self.mm(yr[:, 0:w], svA[:, kc, 0:128], self.hid[:, kc * TW:kc * TW + w], kc == 0, kc == 15,
                            r=[wkA, ("hid", kc)], w=[yrk])
                ys, ysk = self.bank()
                for kc in range(16):
                    self.mm(ys[:, 0:w], svA[:, kc, 128:256], self.hid[:, (16 + kc) * TW:(16 + kc) * TW + w],
                            kc == 0, kc == 15, r=[wkA, ("hid", 16 + kc)], w=[ysk])
                gr, grk = self.bank()
                for kc in range(16):
                    self.mm(gr[:, 0:w], svB[:, kc, 0:128], xin[:, kc * TW:kc * TW + w], kc == 0, kc == 15,
                            r=[wkB, ("xin", par, kc)], w=[grk])
                gs, gsk = self.bank()
                for kc in range(16):
                    self.mm(gs[:, 0:w], svB[:, kc, 128:256], xin[:, kc * TW:kc * TW + w], kc == 0, kc == 15,
                            r=[wkB, ("xin", par, kc)], w=[gsk])
                tA, tAk = self.tmpt()
                tB, tBk = self.tmpt()
                b0 = self.C("bmerge", (l * 2 + 0) * 16 + c)
                b1 = self.C("bmerge", (l * 2 + 1) * 16 + c)
                S.add("act", I("activation", out=tA[:, 0:w], in_=gr[:, 0:w],
                                                                       func=AF.Sigmoid, bias=b0),
                      r=[grk, "cst"], w=[tAk])
                S.add("act", I("activation", out=tB[:, 0:w], in_=gs[:, 0:w],
                                                                       func=AF.Sigmoid, bias=b1),
                      r=[gsk, "cst"], w=[tBk])
                S.add("dve", I("tensor_tensor", out=tA[:, 0:w], in0=tA[:, 0:w], in1=yr[:, 0:w],
                                                                   op=ALU.mult), r=[tAk, yrk], w=[tAk])
                S.add("dve", I("tensor_tensor", out=tB[:, 0:w], in0=tB[:, 0:w], in1=ys[:, 0:w],
                                                                   op=ALU.mult), r=[tBk, ysk], w=[tBk])
                S.add("dve", I("tensor_tensor",
                    out=self.hid[:, (32 + c) * TW:(32 + c) * TW + w], in0=tA[:, 0:w], in1=tB[:, 0:w], op=ALU.add),
                    r=[tAk, tBk], w=[("hid", 32 + c)])
            mid()
            pending = []
            n_done = [0]
            for pn in range(8):
                slot, wk = self.wload(self.wA[p3 + pn], 4096)
                sv = slot[:, 0:4096].rearrange("p (k c) -> p k c", k=16)
                for half in range(2):
                    oc = pn * 2 + half
                    yb, ybk = self.bank()
                    for kc in range(16):
                        self.mm(yb[:, 0:w], sv[:, kc, half * 128:(half + 1) * 128],
                                self.hid[:, (32 + kc) * TW:(32 + kc) * TW + w], kc == 0, kc == 15,
                                r=[wk, ("hid", 32 + kc)], w=[ybk])
                    self.flush_stats(pending, n_done, NCH, keep=0)
                    self.resid_stats(l, 1, oc, yb, ybk, t0, w, pending)
            self.flush_stats(pending, n_done, NCH)
            self.finalize(l, 1, t0, w, dst, dstn, False)


def _panelsA(Wm):
    K, N = Wm.shape
    n = N // 256
    return np.ascontiguousarray(Wm.reshape(16, 128, n, 256).transpose(2, 1, 0, 3)).reshape(n, 128, 4096)


def _panelsB(Wm):
    return np.ascontiguousarray(Wm.reshape(44, 128, 16, 128).transpose(2, 1, 0, 3)).reshape(16, 128, 5632)


def _fm(v):
    v = np.asarray(v, np.float32)
    lead = v.shape[:-1]
    return np.moveaxis(v.reshape(*lead, 16, 128), -1, 0)


def prepare(inp, n_cores=8):
    f = lambda k: np.asarray(inp[k], np.float32)
    w_in = f("w_in")
    wA = np.empty((L * 240, 128, 4096), np.float32)
    wB = np.empty((L * 32, 128, 5632), np.float32)
    for l in range(L):
        base = l * 240
        wA[base:base + 72] = _panelsA(f("w_mod")[l])
        for which, nm in enumerate(("ffn1", "ffn2")):
            wi = f(nm + "_w_in")[l]
            g = wi[:, :DFF].reshape(D, NFF, 128)
            u = wi[:, DFF:].reshape(D, NFF, 128)
            wA[base + 72 + which * 44:base + 72 + (which + 1) * 44] = _panelsA(
                np.concatenate([g, u], axis=2).reshape(D, NFF * 256))
            wB[l * 32 + which * 16:l * 32 + (which + 1) * 16] = _panelsB(f(nm + "_w_out")[l])
        W = w_in[l]
        part = lambda i: W[:, i * 2048:(i + 1) * 2048].reshape(D, 16, 128)
        rg_x, rg_gate, sc_b, sc_c, sc_x, g_rg, g_sc = [part(i) for i in range(7)]
        wA[base + 160:base + 168] = _panelsA(W[:, 0:2048])
        p1 = np.stack([np.concatenate([rg_gate, sc_b], axis=2), np.concatenate([sc_c, sc_x], axis=2)], axis=2)
        wA[base + 168:base + 200] = _panelsA(p1.reshape(D, 32 * 256))
        wro = f("w_rg_out")[l].reshape(D, 16, 128)
        wso = f("w_sc_out")[l].reshape(D, 16, 128)
        p2 = np.stack([np.concatenate([wro, wso], axis=2), np.concatenate([g_rg, g_sc], axis=2)], axis=2)
        wA[base + 200:base + 232] = _panelsA(p2.reshape(D, 32 * 256))
        wA[base + 232:base + 240] = _panelsA(f("w_o")[l])
    x, ctx, c, c_ctx = f("x"), f("ctx"), f("c"), f("c_ctx")
    gate_w = f("rg_gate_w")
    in_maps = []
    for core in range(n_cores):
        b, half = core // 2, core % 2
        xc = ctx[b]
        xl = x[b, half * TLAT:(half + 1) * TLAT]
        if half == 1:
            xc = xc[::-1]
            xl = xl[::-1]
        xT = np.ascontiguousarray(np.concatenate([xc, xl], axis=0).T)
        cs = np.zeros((128, NCONST), np.float32)

        def put(name, arr):
            arr = np.ascontiguousarray(arr, dtype=np.float32).reshape(128, -1)
            cs[:, _off[name]:_off[name] + arr.shape[1]] = arr

        put("bmod", np.moveaxis(f("b_mod").reshape(L, 144, 128), -1, 0))
        put("lng", _fm(f("ln_g")))
        put("lnb", _fm(f("ln_b")))
        put("bmerge", _fm(f("b_merge")))
        rw = f("rg_conv_w")
        z = np.zeros_like(rw[:, :1])
        taps = np.concatenate([rw, z], axis=1) if half == 0 else np.concatenate([z, rw[:, ::-1]], axis=1)
        put("rgcw", np.moveaxis(_fm(taps), 2, 3))
        put("rgcb", _fm(f("rg_conv_b")))
        sw = f("sc_conv_w")
        if half == 1:
            sw = sw[:, ::-1]
        put("sccw", np.moveaxis(_fm(sw), 2, 3))
        dirs = [0, 1] if half == 0 else [1, 0]
        put("gb", _fm(f("rg_gate_b")[:, dirs]))
        put("lam", _fm(f("rg_lam")[:, dirs]))
        put("sel", np.tile(np.array([0.0, 1.0] if half == 0 else [1.0, 0.0], np.float32), (128, 1)))
        put("cc", np.stack([_fm(c[b]), _fm(c_ctx)], axis=-1))
        gw = gate_w[:, dirs]
        gw = np.ascontiguousarray(gw.transpose(4, 0, 1, 3, 2, 5)).reshape(128, L * 2 * 16 * 2, 128)
        in_maps.append({"xT": xT, "consts": cs, "gw": gw, "wA": wA, "wB": wB})
    return in_maps


_NC_CACHE = {}


def kernel(**inputs):
    n = 8
    in_maps = prepare(inputs, n)
    if "nc" not in _NC_CACHE:
        _NC_CACHE["nc"] = Builder(n).build()
    res = run_bass_kernel_spmd(_NC_CACHE["nc"], in_maps, core_ids=list(range(n)))
    out = np.empty((4, 4096, D), np.float32)
    for core in range(n):
        b, half = core // 2, core % 2
        o = res.results[core]["outT"].T
        if half == 1:
            o = o[::-1]
        out[b, half * TLAT:(half + 1) * TLAT] = o
    return out
```
